# Optimizing a Trainium2 kernel written in Bass

```python
import math
import jax
import jax.numpy as jnp
from jax import lax
import numpy as np

D_MODEL = 2048
BATCH = 4
SEQ = 2048
DEPTH = 1
DEC_BATCH = 128
DEC_SEQ = 1
PAST_LEN = 16384
PAGE_SIZE = 128

D_MIX = 2 * D_MODEL
D_A = D_MIX // 2
HA_DK = 128
HA_DV = 128
H_A = D_A // HA_DK
D_B = D_MIX - D_A
B_HEADDIM = 64
H_B = D_B // B_HEADDIM
B_NGROUPS = 8
HEADS_PER_GROUP = H_B // B_NGROUPS
B_DSTATE = 128
CONV_W = 4
CONV_DIM = D_B + 2 * B_NGROUPS * B_DSTATE
D_IN_PROJ = 4 * D_A + D_B + CONV_DIM + H_B
D_FF = 4 * D_MODEL
N_MOD = 6
CHUNK_A = 16
CHUNK_B = 64
EPS = 1e-6

kernel_name = 'hymba_hgrn2_mamba2_sandwich_adaln_step'


def _rms(x):
    xf = x.astype(jnp.float32)
    return (xf * lax.rsqrt(jnp.mean(xf * xf, axis=-1, keepdims=True) + EPS)).astype(x.dtype)


def _pad_time(a, n_pad):
    return jnp.pad(a, [(0, 0), (0, n_pad)] + [(0, 0)] * (a.ndim - 2))


def _to_chunks(a, chunk):
    b, t = a.shape[:2]
    return jnp.moveaxis(a.reshape((b, t // chunk, chunk) + a.shape[2:]), 1, 0)


def _from_chunks(a, t):
    a = jnp.moveaxis(a, 0, 1)
    return a.reshape((a.shape[0], a.shape[1] * a.shape[2]) + a.shape[3:])[:, :t]


def hgrn2_recurrence(q, log_f, k, v, s0):
    t = q.shape[1]
    chunk = min(CHUNK_A, t)
    n_pad = (-t) % chunk
    q, log_f, k, v = [_to_chunks(_pad_time(a.astype(jnp.float32), n_pad), chunk) for a in (q, log_f, k, v)]
    causal = jnp.tril(jnp.ones((chunk, chunk), dtype=bool))

    def step(s, inp):
        qc, lfc, kc, vc = inp
        b = jnp.cumsum(lfc, axis=1)
        qg = qc * jnp.exp(b)
        kg = kc * jnp.exp(-b)
        scores = jnp.where(causal, jnp.einsum('bthk,bshk->bhts', qg, kg), 0.0)
        o = jnp.einsum('bhts,bshv->bthv', scores, vc) + jnp.einsum('bthk,bhkv->bthv', qg, s)
        b_end = b[:, -1]
        k_end = kc * jnp.exp(b_end[:, None] - b)
        s_new = jnp.exp(b_end)[..., None] * s + jnp.einsum('bshk,bshv->bhkv', k_end, vc)
        return s_new, o

    s_fin, o = lax.scan(step, s0.astype(jnp.float32), (q, log_f, k, v))
    return _from_chunks(o, t), s_fin


def ssd_recurrence(x, dt, a, b_in, c_in, h0):
    t = x.shape[1]
    chunk = min(CHUNK_B, t)
    n_pad = (-t) % chunk
    x, dt, b_in, c_in = [_to_chunks(_pad_time(u.astype(jnp.float32), n_pad), chunk) for u in (x, dt, b_in, c_in)]
    causal = jnp.tril(jnp.ones((chunk, chunk), dtype=bool))

    def step(h, inp):
        xc, dtc, bc, cc = inp
        bh = jnp.repeat(bc, HEADS_PER_GROUP, axis=2)
        ch = jnp.repeat(cc, HEADS_PER_GROUP, axis=2)
        cum = jnp.cumsum(dtc * a, axis=1)
        seg = cum[:, :, None, :] - cum[:, None, :, :]
        decay = jnp.exp(jnp.where(causal[None, :, :, None], seg, -jnp.inf))
        scores = jnp.einsum('bthn,bshn->btsh', ch, bh) * decay * dtc[:, None, :, :]
        y = (jnp.einsum('btsh,bshp->bthp', scores, xc)
             + jnp.einsum('bthn,bhpn->bthp', ch, h) * jnp.exp(cum)[..., None])
        w_end = jnp.exp(cum[:, -1:] - cum) * dtc
        h_new = (jnp.exp(cum[:, -1])[:, :, None, None] * h
                 + jnp.einsum('bshn,bshp->bhpn', bh * w_end[..., None], xc))
        return h_new, y

    h_fin, y = lax.scan(step, h0.astype(jnp.float32), (x, dt, b_in, c_in))
    return _from_chunks(y, t), h_fin


def causal_conv(u, buf, w, b):
    t = u.shape[1]
    full = jnp.concatenate([buf.astype(u.dtype), u], axis=1)
    out = b
    for i in range(CONV_W):
        out = out + full[:, i:i + t] * w[i]
    return jax.nn.silu(out), full[:, full.shape[1] - (CONV_W - 1):]


def _layer(x, c, s_hgrn, s_ssm, s_conv, w_ada, b_ada, g_pre_mix, g_post_mix, g_pre_mlp, g_post_mlp,
           w_in, lb, hgrn_norm, conv_w, conv_b, dt_bias, a_log, d_skip, ssd_norm, w_out, w_up, w_down):
    bsz, t, _ = x.shape
    mod = jnp.einsum('bd,de->be', jax.nn.silu(c), w_ada) + b_ada
    sh1, sc1, gt1, sh2, sc2, gt2 = jnp.split(mod[:, None, :], N_MOD, axis=-1)

    h = _rms(x) * g_pre_mix * (1 + sc1) + sh1
    proj = jnp.einsum('btd,de->bte', h, w_in)
    bounds = np.cumsum([D_A, D_A, D_A, D_A, D_B, CONV_DIM]).tolist()
    q, f_logit, i_in, g_out, z, xbc, dt_raw = jnp.split(proj, bounds, axis=-1)

    f = lb + (1.0 - lb) * jax.nn.sigmoid(f_logit.astype(jnp.float32))
    qa = jax.nn.silu(q).reshape(bsz, t, H_A, HA_DK)
    log_f = jnp.log(f).reshape(bsz, t, H_A, HA_DK)
    ka = (1.0 - f).reshape(bsz, t, H_A, HA_DK)
    va = i_in.reshape(bsz, t, H_A, HA_DV)
    o_a, s_hgrn_new = hgrn2_recurrence(qa, log_f, ka, va, s_hgrn)
    o_a = _rms(o_a.astype(x.dtype)).reshape(bsz, t, D_A) * hgrn_norm * jax.nn.silu(g_out)

    xbc, s_conv_new = causal_conv(xbc, s_conv, conv_w, conv_b)
    xs, b_in, c_in = jnp.split(xbc, [D_B, D_B + B_NGROUPS * B_DSTATE], axis=-1)
    xs = xs.reshape(bsz, t, H_B, B_HEADDIM)
    dt = jax.nn.softplus(dt_raw.astype(jnp.float32) + dt_bias)
    a = -jnp.exp(a_log.astype(jnp.float32))
    y, s_ssm_new = ssd_recurrence(xs, dt, a, b_in.reshape(bsz, t, B_NGROUPS, B_DSTATE),
                                  c_in.reshape(bsz, t, B_NGROUPS, B_DSTATE), s_ssm)
    y = (y.astype(x.dtype) + d_skip[:, None] * xs).reshape(bsz, t, D_B)
    yz = (y * jax.nn.silu(z)).reshape(bsz, t, B_NGROUPS, D_B // B_NGROUPS)
    o_b = _rms(yz).reshape(bsz, t, D_B) * ssd_norm

    mix = jnp.einsum('bte,ed->btd', jnp.concatenate([o_a, o_b], axis=-1), w_out)
    x = x + gt1 * (_rms(mix) * g_post_mix)

    h = _rms(x) * g_pre_mlp * (1 + sc2) + sh2
    m = jnp.einsum('btf,fd->btd', jnp.square(jax.nn.relu(jnp.einsum('btd,df->btf', h, w_up))), w_down)
    x = x + gt2 * (_rms(m) * g_post_mlp)
    return x, s_hgrn_new.astype(x.dtype), s_ssm_new.astype(x.dtype), s_conv_new


def setup_inputs(seed: int = 0) -> dict:
    key = jax.random.key(seed)
    ks = jax.random.split(key, 28)
    f32 = jnp.float32

    def nrm(k, shape, scale):
        return jax.random.normal(k, shape, f32) * scale

    dt0 = jnp.exp(jax.random.uniform(ks[18], (DEPTH, H_B), f32, math.log(1e-3), math.log(1e-1)))
    return {
        'x_prompt': nrm(ks[0], (BATCH, SEQ, D_MODEL), 1.0),
        'x_sample': nrm(ks[1], (DEC_BATCH, DEC_SEQ, D_MODEL), 1.0),
        'c_prompt': nrm(ks[2], (BATCH, D_MODEL), 1.0),
        'c_sample': nrm(ks[3], (DEC_BATCH, D_MODEL), 1.0),
        'state_hgrn': nrm(ks[4], (DEPTH, DEC_BATCH, H_A, HA_DK, HA_DV), 0.3),
        'state_ssm': nrm(ks[5], (DEPTH, DEC_BATCH, H_B, B_HEADDIM, B_DSTATE), 0.1),
        'state_conv': nrm(ks[6], (DEPTH, DEC_BATCH, CONV_W - 1, CONV_DIM), 1.0),
        'w_ada': nrm(ks[7], (DEPTH, D_MODEL, N_MOD * D_MODEL), 0.5 * D_MODEL ** -0.5),
        'b_ada': nrm(ks[8], (DEPTH, N_MOD * D_MODEL), 0.02),
        'norm_pre_mix': 1.0 + nrm(ks[9], (DEPTH, D_MODEL), 0.02),
        'norm_post_mix': 1.0 + nrm(ks[10], (DEPTH, D_MODEL), 0.02),
        'norm_pre_mlp': 1.0 + nrm(ks[11], (DEPTH, D_MODEL), 0.02),
        'norm_post_mlp': 1.0 + nrm(ks[12], (DEPTH, D_MODEL), 0.02),
        'w_in': nrm(ks[13], (DEPTH, D_MODEL, D_IN_PROJ), D_MODEL ** -0.5),
        'hgrn_lb_logits': nrm(ks[14], (DEPTH + 1, D_A), 0.1),
        'hgrn_norm': 1.0 + nrm(ks[15], (DEPTH, D_A), 0.02),
        'conv_w': nrm(ks[16], (DEPTH, CONV_W, CONV_DIM), CONV_W ** -0.5),
        'conv_b': nrm(ks[17], (DEPTH, CONV_DIM), 0.02),
        'dt_bias': dt0 + jnp.log(-jnp.expm1(-dt0)),
        'a_log': jnp.log(jax.random.uniform(ks[19], (DEPTH, H_B), f32, 1.0, 16.0)),
        'd_skip': 1.0 + nrm(ks[20], (DEPTH, H_B), 0.1),
        'ssd_norm': 1.0 + nrm(ks[21], (DEPTH, D_B), 0.02),
        'w_out': nrm(ks[22], (DEPTH, D_MIX, D_MODEL), D_MIX ** -0.5),
        'w_up': nrm(ks[23], (DEPTH, D_MODEL, D_FF), D_MODEL ** -0.5),
        'w_down': nrm(ks[24], (DEPTH, D_FF, D_MODEL), D_FF ** -0.5),
    }


def reference(x_prompt, x_sample, c_prompt, c_sample, state_hgrn, state_ssm, state_conv,
              w_ada, b_ada, norm_pre_mix, norm_post_mix, norm_pre_mlp, norm_post_mlp,
              w_in, hgrn_lb_logits, hgrn_norm, conv_w, conv_b, dt_bias, a_log, d_skip, ssd_norm,
              w_out, w_up, w_down):
    lb_all = jnp.cumsum(jax.nn.softmax(hgrn_lb_logits.astype(jnp.float32), axis=0), axis=0)
    dtp = x_prompt.dtype
    zero_hgrn = jnp.zeros((BATCH, H_A, HA_DK, HA_DV), dtp)
    zero_ssm = jnp.zeros((BATCH, H_B, B_HEADDIM, B_DSTATE), dtp)
    zero_conv = jnp.zeros((BATCH, CONV_W - 1, CONV_DIM), dtp)
    xp, xs = x_prompt, x_sample
    hp_l, sp_l, cp_l, hs_l, ss_l, cs_l = [], [], [], [], [], []
    for l in range(DEPTH):
        lp = (w_ada[l], b_ada[l], norm_pre_mix[l], norm_post_mix[l], norm_pre_mlp[l], norm_post_mlp[l],
              w_in[l], lb_all[l], hgrn_norm[l], conv_w[l], conv_b[l], dt_bias[l], a_log[l], d_skip[l],
              ssd_norm[l], w_out[l], w_up[l], w_down[l])
        xp, hp, sp, cp = _layer(xp, c_prompt, zero_hgrn, zero_ssm, zero_conv, *lp)
        xs, hs, ss, cs = _layer(xs, c_sample, state_hgrn[l], state_ssm[l], state_conv[l], *lp)
        hp_l.append(hp); sp_l.append(sp); cp_l.append(cp)
        hs_l.append(hs); ss_l.append(ss); cs_l.append(cs)
    return (xp, xs, jnp.stack(hp_l), jnp.stack(sp_l), jnp.stack(cp_l),
            jnp.stack(hs_l), jnp.stack(ss_l), jnp.stack(cs_l))
```

```python
import os
from contextlib import ExitStack
import numpy as np
import concourse.bass as bass
import concourse.mybir as mybir
from concourse.bass_utils import run_bass_kernel_spmd

F32 = mybir.dt.float32
BF16 = mybir.dt.bfloat16
AF = mybir.ActivationFunctionType
ALU = mybir.AluOpType
AX = mybir.AxisListType

D = 2048
NKC = 16
SEQ_HALF = 1024
NS = 16
ST = 256
TT = ST // 128
WB = 256
NW = 4
EPS = 1e-6
NST = SEQ_HALF // ST
NEG = -1.0e5


class Buf:
    def __init__(self, name, t, nparts=1, share=None, excl=False):
        self.name = name
        self.t = t
        self.nparts = nparts
        self.excl = excl
        if share is not None:
            self.nparts = share.nparts
            self.excl = share.excl
            self.last_w = share.last_w
            self.readers = share.readers
        else:
            self.last_w = [None] * nparts
            self.readers = [[] for _ in range(nparts)]

    def __getitem__(self, k):
        return self.t[k]


def _parts(acc):
    if isinstance(acc, Buf):
        return acc, range(acc.nparts)
    b, p = acc
    if p is None:
        return b, range(b.nparts)
    if isinstance(p, int):
        return b, (p,)
    return b, tuple(p)


class Sched:
    ENGS = ("pe", "act", "dve", "pool", "sp")

    def __init__(self, nc):
        self.nc = nc
        self.ops = {e: [] for e in self.ENGS}
        self.waited = {e: {} for e in self.ENGS}
        self.dma_count = {}
        self.pending = {e: [] for e in self.ENGS}

    def barrier(self):
        tgt = []
        for e in self.ENGS:
            for i in range(len(self.ops[e]) - 1, -1, -1):
                if self.ops[e][i][2] is None:
                    tgt.append(("eng", e, i))
                    break
        for c, n in self.dma_count.items():
            tgt.append(("dma", c, n))
        for e in self.ENGS:
            self.pending[e] = list(tgt)

    def op(self, eng, fn, reads=(), writes=(), dma=None):
        idx = len(self.ops[eng])
        deps = self.pending[eng]
        self.pending[eng] = []
        for acc in reads:
            b, ps = _parts(acc)
            for p in ps:
                if b.last_w[p] is not None:
                    deps.append(b.last_w[p])
                if b.excl:
                    deps.extend(r for r in b.readers[p] if not (r[0] == "eng" and r[1] == eng))
        for acc in writes:
            b, ps = _parts(acc)
            for p in ps:
                if b.last_w[p] is not None:
                    deps.append(b.last_w[p])
                deps.extend(b.readers[p])
        if dma is not None:
            n = self.dma_count.get(dma, 0)
            if n > 0:
                deps.append(("dma", dma, n))
            self.dma_count[dma] = n + 1
            me = ("dma", dma, n + 1)
        else:
            me = ("eng", eng, idx)
        waits = {}
        wd = self.waited[eng]
        for kind, src, val in deps:
            if kind == "eng" and src == eng and eng == "pe":
                continue
            key = (kind, src)
            if wd.get(key, -1) >= val:
                continue
            if waits.get(key, -1) < val:
                waits[key] = val
        for key, val in waits.items():
            wd[key] = val
        self.ops[eng].append([fn, waits, dma, False])
        for acc in reads:
            b, ps = _parts(acc)
            for p in ps:
                b.readers[p].append(me)
        for acc in writes:
            b, ps = _parts(acc)
            for p in ps:
                b.last_w[p] = me
                b.readers[p] = []
        return me

    def emit(self, stack):
        nc = self.nc
        for e in self.ENGS:
            for rec in self.ops[e]:
                for (kind, src), val in rec[1].items():
                    if kind == "eng":
                        self.ops[src][val][3] = True
        rank = {}
        for e in self.ENGS:
            r = 0
            rk = []
            for rec in self.ops[e]:
                if rec[2] is None and rec[3]:
                    r += 1
                rk.append(r)
            rank[e] = rk
        esem = {e: stack.enter_context(nc.semaphore("s_" + e)) for e in self.ENGS}
        dsem = {c: stack.enter_context(nc.semaphore("d_%s" % (c,))) for c in self.dma_count}
        final_waits = dict(self.dma_count)
        block = stack.enter_context(nc.Block())
        sched = self

        def body(ename):
            def run(eng):
                for fn, waits, dma, sig in sched.ops[ename]:
                    for (kind, src), val in waits.items():
                        if kind == "eng":
                            eng.wait_ge(esem[src], rank[src][val])
                        else:
                            eng.wait_ge(dsem[src], 16 * val)
                    ins = fn(eng)
                    if dma is not None:
                        ins.then_inc(dsem[dma], 16)
                    elif sig:
                        ins.then_inc(esem[ename], 1)
                if ename == "sp":
                    for c, n in final_waits.items():
                        eng.wait_ge(dsem[c], 16 * n)
            return run

        block.tensor(body("pe"))
        block.scalar(body("act"))
        block.vector(body("dve"))
        block.gpsimd(body("pool"))
        block.sync(body("sp"))


CST_NAMES = ["ident", "triP", "U", "ones", "upper", "cmask", "negmask"]


def host_consts():
    s = np.arange(128)[:, None]
    t = np.arange(128)[None, :]
    c = {}
    c["ident"] = (s == t)
    c["triP"] = (s <= t).astype(np.float32) - (s <= 63).astype(np.float32) * np.ones_like(t)
    c["U"] = (s <= t)
    c["ones"] = np.ones((128, 128))
    c["upper"] = (s > 63) * np.ones_like(t)
    c["cmask"] = (t >= s)
    c["negmask"] = np.where(t >= s, 0.0, NEG)
    arr = np.concatenate([np.asarray(c[n], np.float32) for n in CST_NAMES], axis=1)
    hsel = np.stack([(np.arange(128) <= 63).astype(np.float32), np.ones(128, np.float32)], axis=1)
    return np.ascontiguousarray(np.concatenate([arr, hsel], axis=1), dtype=np.float32)


NCST = len(CST_NAMES) * 128 + 2

COLS = {}
_o = 0
for _n, _w in [("g1", 16), ("g2", 16), ("gp1", 16), ("gp2", 16), ("bada", 96), ("hn", 16), ("sn", 16),
               ("cb", 32), ("cw", 128), ("lb0", 16), ("lb1", 16), ("dskE", 16)]:
    COLS[_n] = (_o, _w)
    _o += _w
NCOL = _o


def fm(v, nch):
    return np.ascontiguousarray(np.asarray(v, np.float32).reshape(nch, 128).T)


def build(nc, taps=None, nst=NST, npre=NST, do_samples=True, stop=None, wreqs=None):
    taps = taps or []
    dt_in = lambda name, shape: nc.dram_tensor(name, shape, F32, kind="ExternalInput").ap()
    dt_out = lambda name, shape: nc.dram_tensor(name, shape, F32, kind="ExternalOutput").ap()
    xp_d = dt_in("xp", [SEQ_HALF, D])
    xpre_d = dt_in("xpre", [SEQ_HALF, D])
    xs_d = dt_in("xs", [NS, D])
    c17_d = dt_in("c17", [NS + 1, D])
    flag_d = dt_in("flag", [128, 1])
    shg_d = dt_in("shg", [NS, 16, 128, 128])
    ssm_d = dt_in("ssm", [NS, 32, 64, 128])
    scv_d = dt_in("scv", [NS * 3, 4096])
    cst_d = dt_in("cst", [128, NCST])
    colv_d = dt_in("colv", [128, NCOL])
    rowv_d = dt_in("rowv", [1, 2 * D])
    hcol_d = dt_in("hcol", [32, 2])
    rows3_d = dt_in("rows3", [1, 96])
    w_ada_d = dt_in("w_ada", [D, 6 * D])
    w_in_d = dt_in("w_in", [D, 14368])
    w_out_d = dt_in("w_out", [2 * D, D])
    w_up_d = dt_in("w_up", [D, 4 * D])
    w_down_d = dt_in("w_down", [4 * D, D])
    yp_d = dt_out("yp", [SEQ_HALF, D])
    ys_d = dt_out("ys", [NS, D])
    hgp_d = dt_out("hgp", [16, 128, 128])
    ssp_d = dt_out("ssp", [32, 64, 128])
    cvp_d = dt_out("cvp", [3, 4096])
    hgs_d = dt_out("hgs", [NS, 16, 128, 128])
    sss_d = dt_out("sss", [NS, 32, 64, 128])
    cvs_d = dt_out("cvs", [NS, 3, 4096])
    scr_d = nc.dram_tensor("scr", [16, 2, 2 * NS], F32).ap()
    tap_d = {t[0]: nc.dram_tensor("tap_" + t[0], t[1], BF16 if (len(t) > 2 and t[2] == "bf16") else F32,
                                  kind="ExternalOutput").ap() for t in taps}

    S = Sched(nc)
    st = ExitStack()
    with st:
        def sb(name, shape, dt=F32, nparts=1):
            return Buf(name, st.enter_context(nc.sbuf_tensor("sb_" + name, shape, dt)), nparts)

        def MM(out, lhsT, rhs, R, W, start=True, stop=True):
            S.op("pe", lambda e: e.matmul(out, lhsT, rhs, start=start, stop=stop), R, W)

        def TR(out, in_, idn, R, W):
            S.op("pe", lambda e: e.transpose(out, in_, idn), R, W)

        def ACT(out, in_, func, R, W, bias=None, scale=None, accum=None):
            kw = {}
            if bias is not None:
                kw["bias"] = bias
            if scale is not None:
                kw["scale"] = scale
            if accum is not None:
                kw["accum_out"] = accum
            S.op("act", lambda e: e.activation(out, in_, func, **kw), R, W)

        def TS(eng, out, in0, s1, s2, op0, op1, R, W):
            if s2 is None:
                S.op(eng, lambda e: e.tensor_scalar(out, in0, s1, None, op0), R, W)
            else:
                S.op(eng, lambda e: e.tensor_scalar(out, in0, s1, s2, op0, op1), R, W)

        def TTo(eng, out, in0, in1, op, R, W):
            S.op(eng, lambda e: e.tensor_tensor(out, in0, in1, op), R, W)

        def STT(eng, out, in0, sc, in1, op0, op1, R, W, accum=None):
            if accum is None:
                S.op(eng, lambda e: e.scalar_tensor_tensor(out, in0, sc, in1, op0, op1), R, W)
            else:
                S.op(eng, lambda e: e.scalar_tensor_tensor(out, in0, sc, in1, op0, op1, accum_out=accum), R, W)

        def CP(eng, out, in_, R, W):
            if eng == "act":
                S.op("act", lambda e: e.copy(out, in_), R, W)
            else:
                S.op(eng, lambda e: e.tensor_copy(out, in_), R, W)

        def MSET(eng, ap, val, W):
            S.op(eng, lambda e: e.memset(ap, val), (), W)

        def RECIP(out, in_, R, W):
            S.op("dve", lambda e: e.reciprocal(out, in_), R, W)

        def DMA(eng, chan, out, in_, R, W):
            S.op(eng, lambda e: e.dma_start(out=out, in_=in_), R, W, dma=chan)

        def TAP(name, ap, R):
            if name in tap_d:
                DMA("sp", "tap_" + name, tap_d[name], ap, R, ())

        def rsqrt(out, in_, R, W, scale):
            ACT(out, in_, AF.Sqrt, R, W, bias=epsc[0:out.shape[0], 0:1], scale=scale)
            RECIP(out, out, W, W)

        class Ring:
            def __init__(self, bufs):
                self.bufs = bufs
                self.i = 0

            def next(self):
                b = self.bufs[self.i % len(self.bufs)]
                self.i += 1
                return b

        psb = [st.enter_context(nc.psum_tensor("psb%d" % i, [128, 512], F32)) for i in range(8)]
        bank = [Buf("bank%d" % i, psb[i], excl=True) for i in range(8)]
        MMR = Ring([Buf("pmm%d" % i, psb[i], share=bank[i]) for i in range(3)])
        HR = Ring([Buf("ph%d" % i, psb[3 + i % 2][:, (i // 2) * 256:(i // 2) * 256 + 256], share=bank[3 + i % 2])
                   for i in range(3)])
        PO = Buf("ppo", psb[4][:, 256:512], share=bank[4])
        QR = Ring([Buf("pq%d" % i, psb[5 + i % 2][:, (i // 2) * 128:(i // 2 + 1) * 128], share=bank[5 + i % 2])
                   for i in range(4)])
        STAT = Buf("pstat", psb[7], share=bank[7])

        cst = sb("cst", [128, NCST])
        colv = sb("colv", [128, NCOL])
        flag = sb("flag", [128, 1])
        epsc = sb("epsc", [128, 1])
        identb = sb("identb", [128, 128], BF16)
        DMA("sp", "cst", cst[:], cst_d, (), [cst])
        DMA("sp", "colv", colv[:], colv_d, (), [colv])
        DMA("sp", "flag", flag[:], flag_d, (), [flag])
        MSET("dve", epsc[:], EPS, [epsc])

        def C(name):
            i = CST_NAMES.index(name)
            return cst[:, i * 128:(i + 1) * 128]
        hsel = cst[:, len(CST_NAMES) * 128:len(CST_NAMES) * 128 + 2]
        CP("dve", identb[:], C("ident"), [cst], [identb])

        def col(name, j0=0, n=None):
            o, w = COLS[name]
            n = w - j0 if n is None else n
            return colv[:, o + j0:o + j0 + n]

        oml_b = sb("oml_b", [128, D])
        omlc = sb("omlc", [128, 16])
        lbc = sb("lbc", [128, 16])
        xtile = [sb("xtile%d" % i, [128, D]) for i in range(1)] * 2
        xnb = [sb("xnb%d" % i, [128, D], BF16) for i in range(1)] * 2
        DMA("sp", "oml_b", oml_b[:], rowv_d[:, D:2 * D].partition_broadcast(128), (), [oml_b])
        DMA("sp", "xtile0", xtile[0][:], rowv_d[:, 0:D].partition_broadcast(128), (), [xtile[0]])
        TTo("dve", oml_b[:], oml_b[:], xtile[0][:], ALU.subtract, [oml_b, xtile[0]], [oml_b])
        ACT(oml_b[:], oml_b[:], AF.Sigmoid, [oml_b], [oml_b])
        TTo("dve", omlc[:], col("lb1"), col("lb0"), ALU.subtract, [colv], [omlc])
        ACT(omlc[:], omlc[:], AF.Sigmoid, [omlc], [omlc])
        TTo("dve", lbc[:], col("lb0"), col("lb1"), ALU.subtract, [colv], [lbc])
        ACT(lbc[:], lbc[:], AF.Sigmoid, [lbc], [lbc])
        hcol = sb("hcol", [32, 2])
        DMA("sp", "hcol", hcol[:], hcol_d, (), [hcol])

        wslots = [sb("w%d" % i, [128, NKC, WB], BF16) for i in range(NW)]

        def w_requests():
            req = []
            for cbk in range(6 * D // WB):
                req.append((w_ada_d, 0, [(cbk * WB, WB)]))

            def inproj(state_only):
                r = []
                for hb in range(D // WB):
                    for gi, gname in enumerate("qfig"):
                        if state_only and gname in "qg":
                            continue
                        r.append((w_in_d, 0, [(gi * D + hb * WB, WB)]))
                for g in range(8):
                    if not state_only:
                        r.append((w_in_d, 0, [(4 * D + g * 256, 256)]))
                    r.append((w_in_d, 0, [(5 * D + g * 256, 256)]))
                    r.append((w_in_d, 0, [(5 * D + D + g * 128, 128), (5 * D + D + 1024 + g * 128, 128)]))
                return r
            for _ in range(npre):
                req += inproj(True)
            for _ in range(nst):
                req += inproj(False)
                for cbk in range(D // WB):
                    for kh in range(2):
                        req.append((w_out_d, kh * D, [(cbk * WB, WB)]))
                for cbk in range(4 * D // WB):
                    req.append((w_up_d, 0, [(cbk * WB, WB)]))
                for cbk in range(D // WB):
                    for kq in range(4):
                        req.append((w_down_d, kq * D, [(cbk * WB, WB)]))
            return req

        class WStream:
            def __init__(self, reqs):
                self.issued = 0
                self.taken = 0
                self.record = [] if reqs is None else None
                self.released = set()
                self.auto = []
                self.held = {}
                self.conv_ptr = 0
                self.nconv = 0
                self.free = list(range(NW))
                self.slot_of = {}
                if reqs is None:
                    return
                wmap = {a.tensor.name: a for a in (w_ada_d, w_in_d, w_out_d, w_up_d, w_down_d)}
                reqs = [(wmap[n], r0, list(segs)) for (n, r0, segs) in reqs]
                self.reqs = reqs
                self.keys = [(id(r[0]), r[1], tuple(r[2])) for r in reqs]
                uses = {}
                for k in self.keys:
                    uses[k] = uses.get(k, 0) + 1
                self.cidx = {}
                for k in self.keys:
                    if uses[k] > 1 and k not in self.cidx:
                        self.cidx[k] = len(self.cidx)
                self.cached = set()
                ncache = max(1, len(self.cidx))
                self.cache_d = nc.dram_tensor("wcache", [ncache, 128, NKC * WB], BF16).ap()
                self.cbuf = Buf("wcache", self.cache_d, ncache)

            def _issue(self, i):
                wd, r0, segs = self.reqs[i]
                k = self.keys[i]
                slot = wslots[self.slot_of[i]]
                chan = "w%d" % self.slot_of[i]
                if k in self.cached:
                    ci = self.cidx[k]
                    DMA("pool", chan, slot[:].rearrange("p a b -> p (a b)"), self.cache_d[ci], [(self.cbuf, ci)], [slot])
                    return
                off = 0
                for c0, ncol in segs:
                    src = wd[r0:r0 + D, c0:c0 + ncol].rearrange("(kc p) c -> p kc c", p=128)
                    DMA("pool", chan, slot[:, :, off:off + ncol], src, (), [slot])
                    off += ncol
                if k in self.cidx:
                    ci = self.cidx[k]
                    DMA("sp", "wb%d" % self.slot_of[i], self.cache_d[ci], slot[:].rearrange("p a b -> p (a b)"), [slot], [(self.cbuf, ci)])
                    self.cached.add(k)

            def _pump(self, upto):
                while self.issued < min(len(self.reqs), upto) and self.free:
                    j = self.issued
                    self.slot_of[j] = self.free.pop(0)
                    self._issue(j)
                    self.issued += 1

            def take(self, wd, r0=0, segs=None, hold=False):
                i = self.taken
                if self.record is not None:
                    self.record.append((wd.tensor.name, r0, tuple(segs)))
                    self.taken += 1
                    return wslots[i % NW]
                assert i < len(self.reqs), "weight stream exhausted"
                assert self.reqs[i][0] is wd and self.reqs[i][1] == r0 and tuple(self.reqs[i][2]) == tuple(segs), \
                    ("weight stream order mismatch", i, self.reqs[i][1:], r0, segs)
                for j in self.auto:
                    self.free.append(self.slot_of[j])
                self.auto = []
                self._pump(i + NW)
                assert self.issued > i, ("no free weight slot", i)
                if hold:
                    self.held[id(wslots[self.slot_of[i]])] = i
                else:
                    self.auto.append(i)
                self.taken += 1
                return wslots[self.slot_of[i]]

            def preconvert(self):
                if self.record is not None:
                    return
                j = max(self.conv_ptr, self.issued)
                while j < len(self.reqs):
                    k = self.keys[j]
                    if k in self.cidx and k not in self.cached:
                        break
                    j += 1
                self.conv_ptr = j
                if j >= len(self.reqs):
                    return
                wd, r0, segs = self.reqs[j]
                ci = self.cidx[k]
                dst = self.cache_d[ci].rearrange("p (a b) -> p a b", a=NKC)
                off = 0
                for c0, ncol in segs:
                    src = wd[r0:r0 + D, c0:c0 + ncol].rearrange("(kc p) c -> p kc c", p=128)
                    DMA("pool", "cv%d" % (self.nconv % 4), dst[:, :, off:off + ncol], src, (), [(self.cbuf, ci)])
                    off += ncol
                self.nconv += 1
                self.cached.add(k)

            def release(self, slot):
                if self.record is not None:
                    return
                self.free.append(self.slot_of[self.held.pop(id(slot))])
                self._pump(self.taken + NW - 1)

        W = WStream(wreqs)
        wdt = sb("wdt", [128, NKC, 32], BF16)
        DMA("pool", "wdt", wdt[:], w_in_d[:, 14336:14368].rearrange("(kc p) c -> p kc c", p=128), (), [wdt])

        NC_ = ST + NS
        XM = st.enter_context(nc.sbuf_tensor("sb_XM", [128, 2 * NKC * NC_], F32))
        xT = Buf("xT", XM[:, 0:NKC * NC_].rearrange("p (a b) -> p a b", a=NKC), NKC)
        hT = sb("hT", [128, NKC, NC_], BF16)
        oT = sb("oT", [128, 32, NC_], BF16, nparts=32)
        modT = Buf("modT", oT[:].rearrange("p a b -> p (a b)")[:, 0:96 * (NS + 1) * 2].bitcast(F32)
                   .rearrange("p (a b) -> p a b", a=96), 96)
        mixT = Buf("mixT", XM[:, NKC * NC_:2 * NKC * NC_].rearrange("p (a b) -> p a b", a=NKC), NKC)
        Sst = sb("Sst", [128, 16, 128], F32, nparts=16)
        hst = sb("hst", [128, D], F32, nparts=8)
        hstb = sb("hstb", [128, D], BF16, nparts=8)
        halo = sb("halo", [128, 32, 3], F32, nparts=32)
        RA_BYTES = 64 * NC_ * 2 + 4096
        RA = st.enter_context(nc.sbuf_tensor("sb_RA", [128, RA_BYTES // 2], BF16))
        SH1b = sb("SH1b", [128, NKC, NS + 1]); SH2b = sb("SH2b", [128, NKC, NS + 1])
        MSET("dve", Sst[:], 0.0, [Sst])
        MSET("dve", hst[:], 0.0, [hst])
        MSET("dve", hstb[:], 0.0, [hstb])
        MSET("dve", halo[:], 0.0, [halo])

        ctok = xtile[0]
        ctb = xnb[0]
        scT = sb("scT", [128, NKC, NS + 1], BF16)
        DMA("sp", "xtile0", ctok[0:NS + 1, :], c17_d, (), [ctok])
        ACT(ctb[0:NS + 1, :], ctok[0:NS + 1, :], AF.Silu, [ctok], [ctb])
        for kc in range(NKC):
            ph = HR.next()
            pv = ph[:].bitcast(BF16)
            TR(pv[:, 0:NS + 1], ctb[0:NS + 1, kc * 128:(kc + 1) * 128], identb[0:NS + 1, 0:NS + 1], [ctb, identb], [ph])
            CP("dve", scT[:, kc, :], pv[:, 0:NS + 1], [ph], [scT])
        def ada_block(cbk):
            wsl = W.take(w_ada_d, 0, [(cbk * WB, WB)])
            for m in range(WB // 128):
                ch = cbk * (WB // 128) + m
                pq = QR.next()
                for kc in range(NKC):
                    MM(pq[:, 0:NS + 1], wsl[:, kc, m * 128:(m + 1) * 128], scT[:, kc, :], [wsl, scT], [pq],
                       start=(kc == 0), stop=(kc == NKC - 1))
                ACT(modT[:, ch, :], pq[:, 0:NS + 1], AF.Identity, [pq, colv], [(modT, ch)], bias=col("bada", ch, 1))
        NADA = 6 * D // WB
        NADA1 = 2 * D // WB
        for cbk in range(NADA1):
            ada_block(cbk)
        G1 = sb("G1", [128, NKC, NS + 1]); G2 = sb("G2", [128, NKC, NS + 1])
        GT1 = sb("GT1", [128, NKC, NS + 1]); GT2 = sb("GT2", [128, NKC, NS + 1])

        def bc3(ap2):
            return ap2.unsqueeze(2).to_broadcast([128, NKC, NS + 1])

        def derive_G(Gx, sci, gname):
            TS("dve", Gx[:], modT[:, sci * 16:(sci + 1) * 16, :], 1.0, None, ALU.add, None, [modT], [Gx])
            TTo("dve", Gx[:], Gx[:], bc3(col(gname)), ALU.mult, [Gx, colv], [Gx])
        derive_G(G1, 1, "g1")
        CP("act", SH1b[:], modT[:, 0:16, :], [modT], [SH1b])
        SH1 = SH1b
        SH2 = SH2b

        def gada():
            for cbk in range(NADA1, NADA):
                ada_block(cbk)
                yield

        def ada_finish():
            derive_G(G2, 4, "g2")
            for (Gx, gti, gname) in ((GT1, 2, "gp1"), (GT2, 5, "gp2")):
                TTo("dve", Gx[:], modT[:, gti * 16:(gti + 1) * 16, :], bc3(col(gname)), ALU.mult, [modT, colv], [Gx])
            CP("act", SH2b[:], modT[:, 48:64, :], [modT], [SH2b])
            TAP("modT", modT[:], [modT])
            S.barrier()

        junk = sb("junk", [128, 512], BF16)
        junkf = sb("junkf", [128, 512], F32)
        dumf = sb("dumf", [128, 128], F32)
        st4 = sb("st4", [128, 8])
        xcnt = [0]
        dtt = sb("dtt", [128, TT, 32], nparts=TT); dtA = sb("dtA", [128, TT, 32], nparts=TT)
        cum = sb("cum", [128, TT, 32], nparts=TT); expcum = sb("expcum", [128, TT, 32], nparts=TT)
        Eend = sb("Eend", [128, TT, 32], nparts=TT); wend = sb("wend", [128, TT, 32], nparts=TT)
        rowb = sb("rowb", [128, 96])


        HU = WB // 128

        ra_off = [0]

        def ubuf(name, shape, dt=F32, np_=1):
            n = 1
            for d_ in shape[1:]:
                n *= d_
            nb = n * (4 if dt == F32 else 2)
            nb = (nb + 31) // 32 * 32
            o = ra_off[0]
            ra_off[0] += nb
            assert ra_off[0] <= RA_BYTES, ("RA overflow", name, ra_off[0], RA_BYTES)
            v = RA[:, o // 2:(o + nb) // 2]
            if dt == F32:
                v = v.bitcast(F32)
            v = v[:, 0:n]
            if len(shape) == 3:
                v = v.rearrange("p (a b) -> p a b", a=shape[1])
            elif len(shape) == 4:
                v = v.rearrange("p (a b c) -> p a b c", a=shape[1], b=shape[2])
            b_ = Buf("ra_" + name, v, np_)
            return [b_, b_]
        sq_b = ubuf("sq", [128, TT, WB], F32, TT)
        k_b = ubuf("kk", [128, TT, WB], F32, TT)
        lf_b = ubuf("lf", [128, TT, WB], F32, TT)
        ex_b = ubuf("ex", [128, 3, WB], F32, 3)
        qg_b = ubuf("qg", [128, TT, WB], BF16, TT)
        kg_b = ubuf("kg", [128, TT, WB], BF16, TT)
        ke_b = ubuf("ke", [128, TT, WB], BF16, TT)
        v_b = ubuf("vv", [128, TT, WB], BF16, TT)
        sg_b = ubuf("sg", [128, TT, WB], F32, TT)
        ec_b = ubuf("ec", [128, TT, HU, 2], F32, TT)
        qgT_b = ubuf("qgT", [128, HU, ST], BF16, TT)
        kgT_b = ubuf("kgT", [128, HU, ST], BF16, TT)
        AT_b = [ubuf("AT", [128, 128], BF16)[0], ubuf("AT2", [128, 128], BF16)[0]]
        Sm_b = [ubuf("Sm", [128, 128], BF16)[0], ubuf("Sm2", [128, 128], BF16)[0]]
        on_b = ubuf("on", [128, WB], F32)
        onb_b = ubuf("onb", [128, WB], BF16)
        rs_b = ubuf("rs", [128, 4], F32)
        rs2_b = ubuf("rs2", [128, 4], F32)
        assert TT == 2 and WB == 256 and ST == 256
        sz_b = ubuf("sz", [128, TT, 256], F32, TT)
        raw_b = ubuf("raw", [128, 2, 3 + ST], F32, 2)
        acc_b = ubuf("acc", [128, 2, ST], F32, 2)
        xa_b = ubuf("xa", [128, 4, ST], BF16, 4)
        xst_b = ubuf("xst", [128, 256], BF16)
        Bt_b = ubuf("Bt", [128, 128], BF16)
        xw_b = ubuf("xw", [128, 256], BF16)
        xdt_b = ubuf("xdt", [128, 256], BF16)
        xsd_b = ubuf("xsd", [128, 256], BF16)
        CBT_b = ubuf("CBT", [128, 128], F32)
        seg_b = ubuf("seg", [128, 512], F32)
        dec_b = ubuf("dec", [128, 512], F32)
        scT_b = ubuf("scTT", [128, 512], BF16)
        yi_b = ubuf("yi", [128, 256], F32)
        yg_b = ubuf("yg", [128, 256], F32)
        ob_b = ubuf("ob", [128, 256], BF16)
        qsT_s = sb("qsT_s", [128, 16, NS]); fT_s = sb("fT_s", [128, 16, NS]); kkT_s = sb("kkT_s", [128, 16, NS])
        vT_s = sb("vT_s", [128, 16, NS]); sgT_s = sb("sgT_s", [128, 16, NS]); szT_s = sb("szT_s", [128, 16, NS])
        xbc_s = sb("xbc_s", [128, 32, NS]); raw_s = sb("raw_s", [128, 32, NS])
        cnt = {"u": 0}
        aT = Buf("aT", RA[:, 0:64 * NC_].rearrange("p (a b) -> p a b", a=64), 64)

        def supertile(x_rows, state_only, with_s, is_last, preconv=False):
            ncol = NC_ if with_s else ST
            tiles = [(x_rows[tt * 128:(tt + 1) * 128, :], 128, tt * 128) for tt in range(TT)]
            if with_s:
                tiles.append((xs_d, NS, ST))
            for (src, nr, c0) in tiles:
                xi = xcnt[0] % 2
                xcnt[0] += 1
                xt, xn = xtile[xi], xnb[xi]
                DMA("sp", xt.name, xt[0:nr, :], src, (), [xt])
                for q4 in range(4):
                    ACT(junk[0:nr, :], xt[0:nr, q4 * 512:(q4 + 1) * 512], AF.Square, [xt], [st4],
                        accum=st4[0:nr, q4:q4 + 1])
                S.op("dve", lambda e, nr=nr: e.tensor_reduce(st4[0:nr, 4:5], st4[0:nr, 0:4], AX.X, ALU.add), [st4], [st4])
                rsqrt(st4[0:nr, 5:6], st4[0:nr, 4:5], [st4], [st4], 1.0 / D)
                TS("dve", xn[0:nr, :], xt[0:nr, :], st4[0:nr, 5:6], None, ALU.mult, None, [xt, st4], [xn])
                for k4 in range(4):
                    if not state_only:
                        pm = MMR.next()
                        for j in range(4):
                            kc = k4 * 4 + j
                            TR(pm[:, j * 128:j * 128 + nr], xt[0:nr, kc * 128:(kc + 1) * 128], C("ident")[0:nr, 0:nr],
                               [xt, cst], [pm])
                        CP("act", xT[:, k4 * 4:k4 * 4 + 4, c0:c0 + nr],
                           pm[:].rearrange("p (j c) -> p j c", j=4)[:, :, 0:nr], [pm], [(xT, range(k4 * 4, k4 * 4 + 4))])
                    ph = HR.next()
                    pv = ph[:].bitcast(BF16)
                    for j in range(4):
                        kc = k4 * 4 + j
                        TR(pv[:, j * 128:j * 128 + nr], xn[0:nr, kc * 128:(kc + 1) * 128], identb[0:nr, 0:nr],
                           [xn, identb], [ph])
                    for j in range(4):
                        kc = k4 * 4 + j
                        if nr == 128:
                            TS("dve", hT[:, kc, c0:c0 + nr], pv[:, j * 128:j * 128 + nr], G1[:, kc, 0:1], SH1[:, kc, 0:1],
                               ALU.mult, ALU.add, [ph, G1, SH1b], [hT])
                        else:
                            TTo("dve", hT[:, kc, c0:c0 + nr], pv[:, j * 128:j * 128 + nr], G1[:, kc, 1:NS + 1], ALU.mult,
                                [ph, G1], [hT])
                            TTo("dve", hT[:, kc, c0:c0 + nr], hT[:, kc, c0:c0 + nr], SH1[:, kc, 1:NS + 1], ALU.add,
                                [hT, SH1b], [hT])
            TAP("hT", hT[:, :, 0:ST], [hT])
            TAP("xT", xT[:, :, 0:ST], [xT])
            if stop == "A":
                return

            def optB(wsl, wcols, nco):
                outs = []
                for tt in range(TT):
                    pm = MMR.next()
                    for kc in range(NKC):
                        MM(pm[:, 0:nco], hT[:, kc, tt * 128:(tt + 1) * 128], wsl[:, kc, wcols:wcols + nco], [hT, wsl], [pm],
                           start=(kc == 0), stop=(kc == NKC - 1))
                    outs.append(pm)
                return outs

            def optA(wsl, m, c0, c1):
                pm = MMR.next()
                for kc in range(NKC):
                    MM(pm[:, 0:c1 - c0], wsl[:, kc, m * 128:(m + 1) * 128], hT[:, kc, c0:c1], [hT, wsl], [pm],
                       start=(kc == 0), stop=(kc == NKC - 1))
                return pm

            for tt in range(TT):
                pq = QR.next()
                for kc in range(NKC):
                    MM(pq[:, 0:32], hT[:, kc, tt * 128:(tt + 1) * 128], wdt[:, kc, :], [hT, wdt], [pq],
                       start=(kc == 0), stop=(kc == NKC - 1))
                TTo("dve", dtt[:, tt, :], pq[:, 0:32], rowb[:, 0:32], ALU.add, [pq, rowb], [(dtt, tt)])
                ACT(dtt[:, tt, :], dtt[:, tt, :], AF.Exp, [(dtt, tt)], [(dtt, tt)])
                ACT(dtt[:, tt, :], dtt[:, tt, :], AF.Ln, [(dtt, tt)], [(dtt, tt)], bias=1.0)
                TTo("dve", dtA[:, tt, :], dtt[:, tt, :], rowb[:, 32:64], ALU.mult, [(dtt, tt), rowb], [(dtA, tt)])
                p1 = QR.next(); p2 = QR.next()
                MM(p1[:, 0:32], C("U"), dtA[:, tt, :], [cst, (dtA, tt)], [p1])
                MM(p2[:, 0:32], C("ones"), dtA[:, tt, :], [cst, (dtA, tt)], [p2])
                CP("dve", cum[:, tt, :], p1[:, 0:32], [p1], [(cum, tt)])
                ACT(expcum[:, tt, :], p1[:, 0:32], AF.Exp, [p1], [(expcum, tt)])
                ACT(Eend[:, tt, :], p2[:, 0:32], AF.Exp, [p2], [(Eend, tt)])
                TTo("dve", wend[:, tt, :], p2[:, 0:32], cum[:, tt, :], ALU.subtract, [p2, (cum, tt)], [(wend, tt)])
                ACT(wend[:, tt, :], wend[:, tt, :], AF.Exp, [(wend, tt)], [(wend, tt)])
                TTo("dve", wend[:, tt, :], wend[:, tt, :], dtt[:, tt, :], ALU.mult, [(wend, tt), (dtt, tt)], [(wend, tt)])
            if with_s:
                pq = QR.next()
                for kc in range(NKC):
                    MM(pq[0:32, 0:NS], wdt[:, kc, :], hT[:, kc, ST:NC_], [hT, wdt], [pq], start=(kc == 0), stop=(kc == NKC - 1))
                ACT(dts[:, 0, :], pq[0:32, 0:NS], AF.Exp, [pq, hcol], [dts], bias=hcol[:, 0:1])
                ACT(dts[:, 0, :], dts[:, 0, :], AF.Ln, [dts], [dts], bias=1.0)
                TS("dve", dts[:, 1, :], dts[:, 0, :], hcol[:, 1:2], None, ALU.mult, None, [dts, hcol], [dts])
                ACT(dts[:, 1, :], dts[:, 1, :], AF.Exp, [dts], [dts])

            def gen_hgrn(hb):
                sq, kk, lf, ex, qg, kg, ke, vv, sg, ec = (sq_b[0], k_b[0], lf_b[0], ex_b[0], qg_b[0], kg_b[0], ke_b[0],
                                                          v_b[0], sg_b[0], ec_b[0])
                qgT, kgT = qgT_b[0], kgT_b[0]
                cs = slice(hb * WB, (hb + 1) * WB)
                for gname in "qfig":
                    if state_only and gname in "qg":
                        continue
                    gi = "qfig".index(gname)
                    wsl = W.take(w_in_d, 0, [(gi * D + hb * WB, WB)])
                    pms = optB(wsl, 0, WB)
                    for tt, pm in enumerate(pms):
                        if gname == "q":
                            ACT(sq[:, tt, :], pm[:, 0:WB], AF.Silu, [pm], [(sq, tt)])
                        elif gname == "i":
                            CP("act", vv[:, tt, :], pm[:, 0:WB], [pm], [(vv, tt)])
                        elif gname == "g":
                            ACT(sg[:, tt, :], pm[:, 0:WB], AF.Silu, [pm], [(sg, tt)])
                        else:
                            ACT(lf[:, tt, :], pm[:, 0:WB], AF.Sigmoid, [pm], [(lf, tt)], scale=-1.0)
                            TTo("dve", kk[:, tt, :], lf[:, tt, :], oml_b[:, cs], ALU.mult, [(lf, tt), oml_b], [(kk, tt)])
                            ACT(lf[:, tt, :], kk[:, tt, :], AF.Ln, [(kk, tt)], [(lf, tt)], bias=1.0, scale=-1.0)
                    if with_s:
                        for m in range(HU):
                            ch = hb * HU + m
                            pm = optA(wsl, m, ST, NC_)
                            if gname == "q":
                                ACT(qsT_s[:, ch, :], pm[:, 0:NS], AF.Silu, [pm], [qsT_s])
                            elif gname == "i":
                                CP("act", vT_s[:, ch, :], pm[:, 0:NS], [pm], [vT_s])
                            elif gname == "g":
                                ACT(sgT_s[:, ch, :], pm[:, 0:NS], AF.Silu, [pm], [sgT_s])
                            else:
                                ACT(kkT_s[:, ch, :], pm[:, 0:NS], AF.Sigmoid, [pm], [kkT_s], scale=-1.0)
                                TS("dve", kkT_s[:, ch, :], kkT_s[:, ch, :], omlc[:, ch:ch + 1], None, ALU.mult, None,
                                   [kkT_s, omlc], [kkT_s])
                                TS("dve", fT_s[:, ch, :], kkT_s[:, ch, :], -1.0, 1.0, ALU.mult, ALU.add, [kkT_s], [fT_s])
                    yield
                for tt in range(TT):
                    pa = HR.next(); pb = HR.next(); pc = QR.next()
                    MM(pa[:, 0:WB], C("triP"), lf[:, tt, :], [cst, (lf, tt)], [pa])
                    MM(pb[:, 0:WB], C("upper"), lf[:, tt, :], [cst, (lf, tt)], [pb])
                    for h in range(HU):
                        MM(pc[:, 2 * h:2 * h + 2], lf[:, tt, h * 128:(h + 1) * 128], hsel, [cst, (lf, tt)], [pc])
                    ACT(ec[:, tt, :, :].rearrange("p h c -> p (h c)"), pc[:, 0:2 * HU], AF.Exp, [pc], [(ec, tt)])
                    ACT(ex[:, 0, :], pa[:, 0:WB], AF.Exp, [pa], [(ex, 0)])
                    ACT(ex[:, 1, :], pa[:, 0:WB], AF.Exp, [pa], [(ex, 1)], scale=-1.0)
                    ACT(ex[:, 2, :], pb[:, 0:WB], AF.Exp, [pb], [(ex, 2)])
                    TTo("dve", kg[:, tt, :], kk[:, tt, :], ex[:, 1, :], ALU.mult, [(kk, tt), (ex, 1)], [(kg, tt)])
                    TTo("dve", ex[:, 2, :], ex[:, 2, :], ex[:, 1, :], ALU.mult, [(ex, 1), (ex, 2)], [(ex, 2)])
                    TTo("dve", ke[:, tt, :], kk[:, tt, :], ex[:, 2, :], ALU.mult, [(kk, tt), (ex, 2)], [(ke, tt)])
                    if not state_only:
                        TTo("dve", qg[:, tt, :], sq[:, tt, :], ex[:, 0, :], ALU.mult, [(sq, tt), (ex, 0)], [(qg, tt)])
                    yield
                    if not state_only:
                        ph = HR.next()
                        pv = ph[:].bitcast(BF16)
                        for h in range(HU):
                            TR(pv[:, h * 128:(h + 1) * 128], qg[:, tt, h * 128:(h + 1) * 128], identb[:],
                               [(qg, tt), identb], [ph])
                            TR(pv[:, (HU + h) * 128:(HU + h + 1) * 128], kg[:, tt, h * 128:(h + 1) * 128], identb[:],
                               [(kg, tt), identb], [ph])
                        CP("act", qgT[:, :, tt * 128:(tt + 1) * 128],
                           pv[:, 0:HU * 128].rearrange("p (h c) -> p h c", h=HU), [ph], [(qgT, tt)])
                        CP("dve", kgT[:, :, tt * 128:(tt + 1) * 128],
                           pv[:, HU * 128:2 * HU * 128].rearrange("p (h c) -> p h c", h=HU), [ph], [(kgT, tt)])
                        yield
                for tt in range(TT):
                    tc_ = slice(tt * 128, (tt + 1) * 128)
                    po = PO if not state_only else None
                    if not state_only:
                        for h in range(HU):
                            hd = hb * HU + h
                            AT, Sm = AT_b[h % 2], Sm_b[h % 2]
                            TS("dve", Sm[:], Sst[:, hd, :], ec[:, tt, h, 0:1], None, ALU.mult, None, [(Sst, hd), (ec, tt)], [Sm])
                            pA = QR.next()
                            MM(pA[:], kgT[:, h, tc_], qgT[:, h, tc_], [(kgT, tt), (qgT, tt)], [pA])
                            TTo("dve", AT[:], pA[:], C("cmask"), ALU.mult, [pA, cst], [AT])
                        yield
                    for h in range(HU):
                        hd = hb * HU + h
                        hc = slice(h * 128, (h + 1) * 128)
                        if not state_only:
                            AT, Sm = AT_b[h % 2], Sm_b[h % 2]
                            MM(po[:, hc], AT[:], vv[:, tt, hc], [AT, (vv, tt)], [po], start=True, stop=False)
                            MM(po[:, hc], qgT[:, h, tc_], Sm[:], [(qgT, tt), Sm], [po], start=False, stop=True)
                        pd = QR.next()
                        MM(pd[:], ke[:, tt, hc], vv[:, tt, hc], [(ke, tt), (vv, tt)], [pd])
                        STT("dve", Sst[:, hd, :], Sst[:, hd, :], ec[:, tt, h, 1:2], pd[:], ALU.mult, ALU.add,
                            [(Sst, hd), (ec, tt), pd], [(Sst, hd)])
                    yield
                    if not state_only:
                        on, onb, rs = on_b[tt % 2], onb_b[tt % 2], rs_b[tt % 2]
                        for h in range(HU):
                            ACT(junk[:, 0:128], po[:, h * 128:(h + 1) * 128], AF.Square, [po], [rs], accum=rs[:, h:h + 1])
                        rsqrt(rs[:, 2:2 + HU], rs[:, 0:HU], [rs], [rs], 1.0 / 128)
                        TTo("dve", on[:].rearrange("p (h c) -> p h c", h=HU), po[:, 0:WB].rearrange("p (h c) -> p h c", h=HU),
                            rs[:, 2:2 + HU].unsqueeze(2).to_broadcast([128, HU, 128]), ALU.mult, [po, rs], [on])
                        TTo("dve", onb[:], on[:], sg[:, tt, :], ALU.mult, [on, (sg, tt)], [onb])
                        yield
                        ph = HR.next()
                        pv = ph[:].bitcast(BF16)
                        for h in range(HU):
                            TR(pv[:, h * 128:(h + 1) * 128], onb[:, h * 128:(h + 1) * 128], identb[:], [onb, identb], [ph])
                        TTo("dve", oT[:, hb * HU:(hb + 1) * HU, tc_], pv[:, 0:HU * 128].rearrange("p (h c) -> p h c", h=HU),
                            col("hn", hb * HU, HU).unsqueeze(2).to_broadcast([128, HU, 128]), ALU.mult, [ph, colv],
                            [(oT, range(hb * HU, (hb + 1) * HU))])
                        yield

            def gen_ssd(g):
                sz, raw, acc, xa = sz_b[0], raw_b[0], acc_b[0], xa_b[0]
                hs = slice(g * 4, g * 4 + 4)
                if not state_only:
                    wsl = W.take(w_in_d, 0, [(4 * D + g * 256, 256)])
                    pms = optB(wsl, 0, 256)
                    for tt, pm in enumerate(pms):
                        ACT(sz[:, tt, :], pm[:, 0:256], AF.Silu, [pm], [(sz, tt)])
                    if with_s:
                        for m in range(2):
                            pm = optA(wsl, m, ST, NC_)
                            ACT(szT_s[:, g * 2 + m, :], pm[:, 0:NS], AF.Silu, [pm], [szT_s])
                    yield
                for blk in range(2):
                    if blk == 0:
                        wsl = W.take(w_in_d, 0, [(5 * D + g * 256, 256)])
                    else:
                        wsl = W.take(w_in_d, 0, [(5 * D + D + g * 128, 128), (5 * D + D + 1024 + g * 128, 128)])
                    chids = [(g * 2 + m) if blk == 0 else (16 + g if m == 0 else 24 + g) for m in range(2)]
                    for m in range(2):
                        chid = chids[m]
                        pm = optA(wsl, m, 0, ncol)
                        CP("act", raw[:, m, 3:3 + ST], pm[:, 0:ST], [pm], [(raw, m)])
                        CP("dve", raw[:, m, 0:3], halo[:, chid, :], [(halo, chid)], [(raw, m)])
                        CP("dve", halo[:, chid, :], raw[:, m, ST:ST + 3], [(raw, m)], [(halo, chid)])
                        if with_s:
                            CP("act", raw_s[:, chid, :], pm[:, ST:NC_], [pm], [raw_s])
                    cw = lambda m, i, chids=chids: col("cw", chids[m] * 4 + i, 1)
                    for m in range(2):
                        TS("dve", acc[:, m, :], raw[:, m, 0:ST], cw(m, 0), col("cb", chids[m], 1), ALU.mult, ALU.add,
                           [(raw, m), colv], [(acc, m)])
                    for i in (1, 2, 3):
                        for m in range(2):
                            STT("dve", acc[:, m, :], raw[:, m, i:i + ST], cw(m, i), acc[:, m, :], ALU.mult, ALU.add,
                                [(raw, m), (acc, m), colv], [(acc, m)])
                    for m in range(2):
                        ACT(xa[:, blk * 2 + m, :], acc[:, m, :], AF.Silu, [(acc, m)], [(xa, blk * 2 + m)])
                    yield

                def hb3(ap):
                    return ap.unsqueeze(2).to_broadcast([128, 4, 64])
                for tt in range(TT):
                    tc_ = slice(tt * 128, (tt + 1) * 128)
                    xst, Bt, xw = xst_b[0], Bt_b[0], xw_b[0]
                    ph = HR.next()
                    pv = ph[:].bitcast(BF16)
                    for m in range(3):
                        TR(pv[:, m * 128:(m + 1) * 128], xa[:, m, tc_], identb[:], [(xa, m), identb], [ph])
                    CP("act", xst[:], pv[:, 0:256], [ph], [xst])
                    CP("dve", Bt[:], pv[:, 256:384], [ph], [Bt])
                    x3 = xst[:].rearrange("p (h c) -> p h c", h=4)
                    TTo("dve", xw[:].rearrange("p (h c) -> p h c", h=4), x3, hb3(wend[:, tt, hs]), ALU.mult,
                        [xst, (wend, tt)], [xw])
                    if not state_only:
                        xdt, xsd, CBT, seg, dec, scTT = xdt_b[0], xsd_b[0], CBT_b[0], seg_b[0], dec_b[0], scT_b[0]
                        yi, yg, ob = yi_b[0], yg_b[0], ob_b[0]
                        TTo("dve", xdt[:].rearrange("p (h c) -> p h c", h=4), x3, hb3(dtt[:, tt, hs]), ALU.mult,
                            [xst, (dtt, tt)], [xdt])
                        TTo("dve", xsd[:].rearrange("p (h c) -> p h c", h=4), x3, hb3(rowb[:, 64 + g * 4:64 + g * 4 + 4]),
                            ALU.mult, [xst, rowb], [xsd])
                        yield
                        pcb = QR.next()
                        MM(pcb[:], xa[:, 2, tc_], xa[:, 3, tc_], [(xa, 2), (xa, 3)], [pcb])
                        CP("act", CBT[:], pcb[:], [pcb], [CBT])
                        pcr = MMR.next()
                        for hh in range(4):
                            hd = g * 4 + hh
                            MM(pcr[:, hh * 128:(hh + 1) * 128], dtA[:, tt, hd:hd + 1].to_broadcast([128, 128]), C("U"),
                               [(dtA, tt), cst], [pcr])
                        for hh in range(4):
                            hd = g * 4 + hh
                            STT("dve", seg[:, hh * 128:(hh + 1) * 128], pcr[:, hh * 128:(hh + 1) * 128], cum[:, tt, hd:hd + 1],
                                C("negmask"), ALU.subtract, ALU.add, [pcr, (cum, tt), cst], [seg])
                        ACT(dec[:], seg[:], AF.Exp, [seg], [dec])
                        TTo("dve", scTT[:].rearrange("p (h c) -> p h c", h=4), dec[:].rearrange("p (h c) -> p h c", h=4),
                            CBT[:].unsqueeze(1).to_broadcast([128, 4, 128]), ALU.mult, [dec, CBT], [scTT])
                        yield
                        py = HR.next(); pyi = HR.next()
                        MM(py[:, 0:256], identb[:], xsd[:], [identb, xsd], [py], start=True, stop=False)
                        for hh in range(4):
                            MM(py[:, hh * 64:(hh + 1) * 64], scTT[:, hh * 128:(hh + 1) * 128], xdt[:, hh * 64:(hh + 1) * 64],
                               [scTT, xdt], [py], start=False, stop=(hh == 3))
                        MM(pyi[:, 0:256], xa[:, 3, tc_], hstb[:, g * 256:(g + 1) * 256], [(xa, 3), (hstb, g)], [pyi])
                        TTo("dve", yi[:].rearrange("p (h c) -> p h c", h=4), pyi[:, 0:256].rearrange("p (h c) -> p h c", h=4),
                            hb3(expcum[:, tt, hs]), ALU.mult, [pyi, (expcum, tt)], [yi])
                        TTo("dve", yi[:], py[:, 0:256], yi[:], ALU.add, [py, yi], [yi])
                        TTo("dve", yg[:], yi[:], sz[:, tt, :], ALU.mult, [yi, (sz, tt)], [yg])
                        rs = rs2_b[0]
                        ACT(junk[:, 0:256], yg[:], AF.Square, [yg], [rs], accum=rs[:, 0:1])
                        rsqrt(rs[:, 1:2], rs[:, 0:1], [rs], [rs], 1.0 / 256)
                        TS("dve", ob[:], yg[:], rs[:, 1:2], None, ALU.mult, None, [yg, rs], [ob])
                    pdl = HR.next()
                    MM(pdl[:, 0:256], Bt[:], xw[:], [Bt, xw], [pdl])
                    h3 = hst[:, g * 256:(g + 1) * 256].rearrange("p (h c) -> p h c", h=4)
                    TTo("dve", h3, h3, hb3(Eend[:, tt, hs]), ALU.mult, [(hst, g), (Eend, tt)], [(hst, g)])
                    TTo("dve", hst[:, g * 256:(g + 1) * 256], hst[:, g * 256:(g + 1) * 256], pdl[:, 0:256], ALU.add,
                        [(hst, g), pdl], [(hst, g)])
                    CP("act", hstb[:, g * 256:(g + 1) * 256], hst[:, g * 256:(g + 1) * 256], [(hst, g)], [(hstb, g)])
                    yield
                    if not state_only:
                        ph2 = HR.next()
                        pv2 = ph2[:].bitcast(BF16)
                        for m in range(2):
                            TR(pv2[:, m * 128:(m + 1) * 128], ob[:, m * 128:(m + 1) * 128], identb[:], [ob, identb], [ph2])
                        TTo("dve", oT[:, 16 + g * 2:16 + g * 2 + 2, tc_], pv2[:, 0:256].rearrange("p (h c) -> p h c", h=2),
                            col("sn", g * 2, 2).unsqueeze(2).to_broadcast([128, 2, 128]), ALU.mult, [ph2, colv],
                            [(oT, range(16 + g * 2, 16 + g * 2 + 2))])
                        yield

            stepc = 0
            for u_ in range(8):
                gens = [gen_hgrn(u_), gen_ssd(u_)]
                while gens:
                    for gen in list(gens):
                        try:
                            next(gen)
                        except StopIteration:
                            gens.remove(gen)
                        stepc += 1
                        if preconv and stepc % 6 == 0:
                            W.preconvert()

            TAP("oT", oT[:, :, 0:ST], [oT])
            TAP("hst", hst[:], [hst])
            if state_only or stop == "ssd":
                return

            if with_s and do_samples:
                sample_phase()

            def fm_proj(wd, nkb, K_rhs, dstT):
                for cbk in range(D // WB):
                    pms = [MMR.next() for _ in range(WB // 128)]
                    for kb in range(nkb):
                        wsl = W.take(wd, kb * D, [(cbk * WB, WB)])
                        for m in range(WB // 128):
                            for kc in range(NKC):
                                MM(pms[m][:, 0:ncol], wsl[:, kc, m * 128:(m + 1) * 128], K_rhs[:, kb * NKC + kc, 0:ncol],
                                   [wsl, K_rhs], [pms[m]], start=(kb == 0 and kc == 0), stop=(kb == nkb - 1 and kc == NKC - 1))
                    for m in range(WB // 128):
                        dc = cbk * (WB // 128) + m
                        CP("act", dstT[:, dc, 0:ncol], pms[m][:, 0:ncol], [pms[m]], [(dstT, dc)])
                        ACT(junkf[:, 0:ncol], pms[m][:, 0:ncol], AF.Square, [pms[m]], [junkf])
                        MM(STAT[:, 0:ncol], C("ones"), junkf[:, 0:ncol], [cst, junkf], [STAT], start=(dc == 0), stop=(dc == NKC - 1))

            def stat_rstd():
                rsqrt(rstd_b[:, 0:ncol], STAT[:, 0:ncol], [STAT], [rstd_b], 1.0 / D)

            def resid_add(GT, srcT):
                for dc in range(NKC):
                    STT("dve", junkf[:, 0:ST], srcT[:, dc, 0:ST], GT[:, dc, 0:1], rstd_b[:, 0:ST], ALU.mult, ALU.mult,
                        [(srcT, dc), GT, rstd_b], [junkf])
                    TTo("dve", xT[:, dc, 0:ST], xT[:, dc, 0:ST], junkf[:, 0:ST], ALU.add, [(xT, dc), junkf], [(xT, dc)])
                    if with_s:
                        TTo("dve", junkf[:, ST:NC_], srcT[:, dc, ST:NC_], GT[:, dc, 1:NS + 1], ALU.mult, [(srcT, dc), GT], [junkf])
                        TTo("dve", junkf[:, ST:NC_], junkf[:, ST:NC_], rstd_b[:, ST:NC_], ALU.mult, [junkf, rstd_b], [junkf])
                        TTo("dve", xT[:, dc, ST:NC_], xT[:, dc, ST:NC_], junkf[:, ST:NC_], ALU.add, [(xT, dc), junkf], [(xT, dc)])

            fm_proj(w_out_d, 2, oT, mixT)
            stat_rstd()
            resid_add(GT1, mixT)
            TAP("x1T", xT[:, :, 0:ST], [xT])
            if stop == "out":
                return

            for dc in range(NKC):
                ACT(junkf[:, 0:ncol], xT[:, dc, 0:ncol], AF.Square, [(xT, dc)], [junkf])
                MM(STAT[:, 0:ncol], C("ones"), junkf[:, 0:ncol], [cst, junkf], [STAT], start=(dc == 0), stop=(dc == NKC - 1))
            stat_rstd()
            for dc in range(NKC):
                STT("dve", junkf[:, 0:ST], xT[:, dc, 0:ST], G2[:, dc, 0:1], rstd_b[:, 0:ST], ALU.mult, ALU.mult,
                    [(xT, dc), G2, rstd_b], [junkf])
                TS("dve", hT[:, dc, 0:ST], junkf[:, 0:ST], SH2[:, dc, 0:1], None, ALU.add, None, [junkf, SH2b], [hT])
                if with_s:
                    TTo("dve", junkf[:, ST:NC_], xT[:, dc, ST:NC_], G2[:, dc, 1:NS + 1], ALU.mult, [(xT, dc), G2], [junkf])
                    TTo("dve", junkf[:, ST:NC_], junkf[:, ST:NC_], rstd_b[:, ST:NC_], ALU.mult, [junkf, rstd_b], [junkf])
                    TTo("dve", hT[:, dc, ST:NC_], junkf[:, ST:NC_], SH2[:, dc, 1:NS + 1], ALU.add, [junkf, SH2b], [hT])

            S.barrier()
            for cbk in range(4 * D // WB):
                wsl = W.take(w_up_d, 0, [(cbk * WB, WB)])
                for m in range(WB // 128):
                    fc = cbk * (WB // 128) + m
                    pm = MMR.next()
                    for kc in range(NKC):
                        MM(pm[:, 0:ncol], wsl[:, kc, m * 128:(m + 1) * 128], hT[:, kc, 0:ncol], [wsl, hT], [pm],
                           start=(kc == 0), stop=(kc == NKC - 1))
                    ACT(junkf[:, 0:ncol], pm[:, 0:ncol], AF.Relu, [pm], [junkf])
                    TTo("dve", aT[:, fc, 0:ncol], junkf[:, 0:ncol], junkf[:, 0:ncol], ALU.mult, [junkf], [(aT, fc)])
            fm_proj(w_down_d, 4, aT, mixT)
            stat_rstd()
            resid_add(GT2, mixT)

            S.barrier()
            for tt in range(TT):
                yt = xtile[xcnt[0] % 2]
                xcnt[0] += 1
                for k4 in range(4):
                    pm = MMR.next()
                    for j in range(4):
                        dc = k4 * 4 + j
                        TR(pm[:, j * 128:(j + 1) * 128], xT[:, dc, tt * 128:(tt + 1) * 128], C("ident"), [(xT, dc), cst], [pm])
                    CP("act", yt[:, k4 * 512:(k4 + 1) * 512], pm[:], [pm], [yt])
                DMA("sp", yt.name, out_rows[0][tt * 128:(tt + 1) * 128, :], yt[:], [yt], ())
            if with_s:
                yt = xtile[xcnt[0] % 2]
                xcnt[0] += 1
                for k4 in range(4):
                    pm = MMR.next()
                    for j in range(4):
                        dc = k4 * 4 + j
                        TR(pm[0:NS, j * 128:(j + 1) * 128], xT[:, dc, ST:NC_], C("ident"), [(xT, dc), cst], [pm])
                    CP("act", yt[0:NS, k4 * 512:(k4 + 1) * 512], pm[0:NS, :], [pm], [yt])
                DMA("sp", yt.name, ys_d, yt[0:NS, :], [yt], ())

        rstd_b = sb("rstd_b", [128, NC_])
        dts = sb("dts", [32, 2, NS])
        out_rows = [None]

        def raview(off_b, ncols_f32, shape3=None):
            v = RA[:, off_b // 2:off_b // 2 + ncols_f32 * 2].bitcast(F32)
            if shape3 is not None:
                v = v.rearrange("p (a b) -> p a b", a=shape3)
            return v
        Sb_s = [Buf("Sb%d" % i, raview(i * 8192, 2048, 16), 16) for i in range(2)]
        hb_s = [Buf("hb%d" % i, raview(16384 + i * 8192, 2048, 16), 16) for i in range(2)]
        yT_s = sb("yT_s", [128, 16, NS], nparts=16)
        dE = sb("dE", [128, 16, 2 * NS])
        dtx = sb("dtx", [128, 16, NS])
        oS = sb("oS", [128, 16 * NS])
        tmpS = sb("tmpS", [128, 16 * NS])
        scrb = Buf("scrb", scr_d)
        vTb_s = sb("vTb_s", [128, 16, NS], BF16)
        bcb_s = sb("bcb_s", [128, 16, NS], BF16)

        def sample_phase():
            S.barrier()
            cstT = hb_s[0]
            cv = cstT[:].rearrange("p a b -> p (a b)")[:, 0:32 * 48].rearrange("p (c x) -> p c x", c=32)
            for hf in range(2):
                xt = xtile[hf]
                DMA("sp", xt.name, xt[0:3 * NS, :], scv_d[:, hf * D:(hf + 1) * D], (), [xt])
                for k4 in range(4):
                    pm = MMR.next()
                    for j in range(4):
                        TR(pm[:, j * 128:j * 128 + 3 * NS], xt[0:3 * NS, (k4 * 4 + j) * 128:(k4 * 4 + j + 1) * 128],
                           C("ident")[0:3 * NS, 0:3 * NS], [xt, cst], [pm])
                    CP("act", cv[:, hf * 16 + k4 * 4:hf * 16 + k4 * 4 + 4, :],
                       pm[:].rearrange("p (j c) -> p j c", j=4)[:, :, 0:3 * NS], [pm], [cstT])
            DMA("sp", "cvs01", cvs_d[:, 0:2, :], scv_d.rearrange("(b r) c -> b r c", r=3)[:, 1:3, :], (), ())
            for chid in range(32):
                c3 = cv[:, chid, :].rearrange("p (b r) -> p b r", r=3)
                cw = lambda i: col("cw", chid * 4 + i, 1)
                TS("dve", xbc_s[:, chid, :], c3[:, :, 0], cw(0), col("cb", chid, 1), ALU.mult, ALU.add, [cstT, colv], [xbc_s])
                for i in (1, 2):
                    STT("dve", xbc_s[:, chid, :], c3[:, :, i], cw(i), xbc_s[:, chid, :], ALU.mult, ALU.add,
                        [cstT, colv, xbc_s], [xbc_s])
                STT("dve", xbc_s[:, chid, :], raw_s[:, chid, :], cw(3), xbc_s[:, chid, :], ALU.mult, ALU.add,
                    [raw_s, colv, xbc_s], [xbc_s])
            ACT(xbc_s[:], xbc_s[:], AF.Silu, [xbc_s], [xbc_s])
            for hf in range(2):
                xt = xtile[hf]
                for k4 in range(4):
                    pm = MMR.next()
                    for j in range(4):
                        chid = hf * 16 + k4 * 4 + j
                        TR(pm[0:NS, j * 128:(j + 1) * 128], raw_s[:, chid, :], C("ident"), [raw_s, cst], [pm])
                    CP("act", xt[0:NS, k4 * 512:(k4 + 1) * 512], pm[0:NS, :], [pm], [xt])
                DMA("sp", xt.name, cvs_d[:, 2, hf * D:(hf + 1) * D], xt[0:NS, :], [xt], ())
            DMA("sp", "scrw", scr_d.rearrange("j two x -> (j two) x"), dts[:].rearrange("p q b -> p (q b)"), [dts], [scrb])
            for two in range(2):
                DMA("sp", "dE%d" % two, dE[two * 64:(two + 1) * 64, :, :], scr_d[:, two, :].partition_broadcast(64),
                    [scrb], [dE])
            TTo("dve", dtx[:], dE[:, :, 0:NS], xbc_s[:, 0:16, :], ALU.mult, [dE, xbc_s], [dtx])

            CP("dve", vTb_s[:], vT_s[:], [vT_s], [vTb_s])
            CP("dve", bcb_s[:], xbc_s[:, 16:32, :], [xbc_s], [bcb_s])

            def load(b):
                DMA("sp", Sb_s[b % 2].name, Sb_s[b % 2][:], shg_d[b].rearrange("h k v -> k h v"), (), [Sb_s[b % 2]])
                DMA("sp", hb_s[b % 2].name, hb_s[b % 2][:], ssm_d[b].rearrange("(j two) p n -> (two p) j n", two=2), (),
                    [hb_s[b % 2]])
            load(0)
            for b in range(NS):
                if b + 1 < NS:
                    load(b + 1)
                Sb, hb = Sb_s[b % 2], hb_s[b % 2]
                for h4 in range(4):
                    pm = MMR.next()
                    for j in range(4):
                        h = h4 * 4 + j
                        MM(pm[:, j * 128:(j + 1) * 128], vTb_s[:, h, b:b + 1].to_broadcast([128, 128]), identb[:],
                           [vTb_s, identb], [pm])
                    for j in range(4):
                        h = h4 * 4 + j
                        ACT(Sb[:, h, :], Sb[:, h, :], AF.Identity, [(Sb, h), fT_s], [(Sb, h)], scale=fT_s[:, h, b:b + 1])
                        STT("dve", Sb[:, h, :], pm[:, j * 128:(j + 1) * 128], kkT_s[:, h, b:b + 1], Sb[:, h, :], ALU.mult, ALU.add,
                            [pm, kkT_s, (Sb, h)], [(Sb, h)])
                        MM(STAT[:, h * NS + b:h * NS + b + 1], Sb[:, h, :], qsT_s[:, h, b:b + 1], [(Sb, h), qsT_s], [STAT])
                DMA("sp", Sb.name, hgs_d[b].rearrange("h k v -> k h v"), Sb[:], [Sb], ())
                for g2 in range(4):
                    pm = MMR.next()
                    for gg in range(2):
                        g = g2 * 2 + gg
                        MM(pm[:, (gg * 2) * 128:(gg * 2 + 1) * 128], bcb_s[:, g, b:b + 1].to_broadcast([128, 128]),
                           identb[:], [bcb_s, identb], [pm])
                        MM(pm[:, (gg * 2 + 1) * 128:(gg * 2 + 2) * 128], bcb_s[:, 8 + g, b:b + 1].to_broadcast([128, 128]),
                           identb[:], [bcb_s, identb], [pm])
                    for gg in range(2):
                        g = g2 * 2 + gg
                        for jj in range(2):
                            j = g * 2 + jj
                            ACT(hb[:, j, :], hb[:, j, :], AF.Identity, [(hb, j), dE], [(hb, j)], scale=dE[:, j, NS + b:NS + b + 1])
                            STT("dve", hb[:, j, :], pm[:, (gg * 2) * 128:(gg * 2 + 1) * 128], dtx[:, j, b:b + 1], hb[:, j, :],
                                ALU.mult, ALU.add, [pm, dtx, (hb, j)], [(hb, j)])
                            STT("dve", dumf[:, 0:128], hb[:, j, :], 1.0, pm[:, (gg * 2 + 1) * 128:(gg * 2 + 2) * 128],
                                ALU.mult, ALU.mult, [(hb, j), pm], [(yT_s, j)], accum=yT_s[:, j, b:b + 1])
                DMA("sp", hb.name, sss_d[b].rearrange("(j two) p n -> (two p) j n", two=2), hb[:], [hb], ())
            CP("act", oS[:], STAT[:, 0:16 * NS], [STAT], [oS])
            TTo("dve", tmpS[:], oS[:], oS[:], ALU.mult, [oS], [tmpS])
            MM(STAT[:, 0:16 * NS], C("ones"), tmpS[:], [cst, tmpS], [STAT])
            rsqrt(tmpS[:], STAT[:, 0:16 * NS], [STAT], [tmpS], 1.0 / 128)
            TTo("dve", oS[:], oS[:], tmpS[:], ALU.mult, [oS, tmpS], [oS])
            TTo("dve", oS[:], oS[:], sgT_s[:].rearrange("p h b -> p (h b)"), ALU.mult, [oS, sgT_s], [oS])
            TTo("dve", oT[:, 0:16, ST:NC_], oS[:].rearrange("p (h b) -> p h b", h=16),
                col("hn").unsqueeze(2).to_broadcast([128, 16, NS]), ALU.mult, [oS, colv], [(oT, range(0, 16))])
            TTo("dve", oS[:].rearrange("p (h b) -> p h b", h=16), xbc_s[:, 0:16, :],
                col("dskE").unsqueeze(2).to_broadcast([128, 16, NS]), ALU.mult, [xbc_s, colv], [oS])
            TTo("dve", oS[:], oS[:], yT_s[:].rearrange("p h b -> p (h b)"), ALU.add, [oS, yT_s], [oS])
            TTo("dve", oS[:], oS[:], szT_s[:].rearrange("p h b -> p (h b)"), ALU.mult, [oS, szT_s], [oS])
            TTo("dve", tmpS[:], oS[:], oS[:], ALU.mult, [oS], [tmpS])
            MM(STAT[:, 0:16 * NS], C("ones"), tmpS[:], [cst, tmpS], [STAT])
            t4 = tmpS[:].rearrange("p (g t b) -> p g t b", g=8, t=2)
            s4 = STAT[:, 0:16 * NS].rearrange("p (g t b) -> p g t b", g=8, t=2)
            CP("act", tmpS[:], STAT[:, 0:16 * NS], [STAT], [tmpS])
            TTo("dve", t4[:, :, 0, :], t4[:, :, 0, :], t4[:, :, 1, :], ALU.add, [tmpS], [tmpS])
            CP("dve", t4[:, :, 1, :], t4[:, :, 0, :], [tmpS], [tmpS])
            rsqrt(tmpS[:], tmpS[:], [tmpS], [tmpS], 1.0 / 256)
            TTo("dve", oS[:], oS[:], tmpS[:], ALU.mult, [oS, tmpS], [oS])
            TTo("dve", oT[:, 16:32, ST:NC_], oS[:].rearrange("p (h b) -> p h b", h=16),
                col("sn").unsqueeze(2).to_broadcast([128, 16, NS]), ALU.mult, [oS, colv], [(oT, range(16, 32))])
            S.barrier()


        def prefix_pass():
            NT8 = SEQ_HALF // 128
            S.barrier()
            hTp = Buf("hTp", XM[:].bitcast(BF16)[:, 0:NKC * SEQ_HALF].rearrange("p (a b) -> p a b", a=NKC), 1)
            off = [0]

            def pbuf(name, shape, dt=F32, np_=1):
                n = 1
                for d_ in shape[1:]:
                    n *= d_
                nb = (n * (4 if dt == F32 else 2) + 31) // 32 * 32
                o = off[0]
                off[0] += nb
                assert off[0] <= RA_BYTES, ("RA overflow (prefix)", name, off[0], RA_BYTES)
                v = RA[:, o // 2:(o + nb) // 2]
                if dt == F32:
                    v = v.bitcast(F32)
                v = v[:, 0:n]
                if len(shape) == 3:
                    v = v.rearrange("p (a b) -> p a b", a=shape[1])
                elif len(shape) == 4:
                    v = v.rearrange("p (a b c) -> p a b c", a=shape[1], b=shape[2])
                return Buf("pp_" + name, v, np_)
            p_dtt = pbuf("dtt", [128, NT8, 32], F32, NT8); p_dtA = pbuf("dtA", [128, NT8, 32], F32, NT8)
            p_cum = pbuf("cum", [128, NT8, 32], F32, NT8); p_Eend = pbuf("Eend", [128, NT8, 32], F32, NT8)
            p_wend = pbuf("wend", [128, NT8, 32], F32, NT8)
            p_lf = [pbuf("lf%d" % i, [128, WB]) for i in range(2)]
            p_kk = [pbuf("kk%d" % i, [128, WB]) for i in range(2)]
            p_ex = [pbuf("ex%d" % i, [128, 2, WB], F32, 2) for i in range(2)]
            p_ke = pbuf("ke", [128, NT8, WB], BF16, NT8)
            p_ec = pbuf("ec", [128, NT8, HU, 2], F32, NT8)
            p_vv = [pbuf("vv%d" % i, [128, WB], BF16) for i in range(2)]
            p_raw = pbuf("raw", [128, 3 + SEQ_HALF])
            p_acc = pbuf("acc", [128, SEQ_HALF])
            p_xa = pbuf("xa", [128, 3, SEQ_HALF], BF16, 3)
            p_xst = [pbuf("xst%d" % i, [128, 256], BF16) for i in range(2)]
            p_Bt = [pbuf("Bt%d" % i, [128, 128], BF16) for i in range(2)]
            p_xw = [pbuf("xw%d" % i, [128, 256], BF16) for i in range(2)]

            for t8 in range(NT8):
                xi = xcnt[0] % 2
                xcnt[0] += 1
                xt, xn = xtile[xi], xnb[xi]
                DMA("sp", xt.name, xt[:], xpre_d[t8 * 128:(t8 + 1) * 128, :], (), [xt])
                for q4 in range(4):
                    ACT(junk[:, :], xt[:, q4 * 512:(q4 + 1) * 512], AF.Square, [xt], [st4], accum=st4[:, q4:q4 + 1])
                S.op("dve", lambda e: e.tensor_reduce(st4[:, 4:5], st4[:, 0:4], AX.X, ALU.add), [st4], [st4])
                rsqrt(st4[:, 5:6], st4[:, 4:5], [st4], [st4], 1.0 / D)
                TS("dve", xn[:], xt[:], st4[:, 5:6], None, ALU.mult, None, [xt, st4], [xn])
                for k4 in range(4):
                    ph = HR.next()
                    pv = ph[:].bitcast(BF16)
                    for j in range(4):
                        kc = k4 * 4 + j
                        TR(pv[:, j * 128:(j + 1) * 128], xn[:, kc * 128:(kc + 1) * 128], identb[:], [xn, identb], [ph])
                    for j in range(4):
                        kc = k4 * 4 + j
                        if j % 2 == 0:
                            TS("dve", hTp[:, kc, t8 * 128:(t8 + 1) * 128], pv[:, j * 128:(j + 1) * 128], G1[:, kc, 0:1],
                               SH1[:, kc, 0:1], ALU.mult, ALU.add, [ph, G1, SH1b], [hTp])
                        else:
                            ACT(hTp[:, kc, t8 * 128:(t8 + 1) * 128], pv[:, j * 128:(j + 1) * 128], AF.Identity, [ph, G1, SH1b], [hTp],
                                bias=SH1[:, kc, 0:1], scale=G1[:, kc, 0:1])
            for t8 in range(NT8):
                pq = QR.next()
                for kc in range(NKC):
                    MM(pq[:, 0:32], hTp[:, kc, t8 * 128:(t8 + 1) * 128], wdt[:, kc, :], [hTp, wdt], [pq],
                       start=(kc == 0), stop=(kc == NKC - 1))
                TTo("dve", p_dtt[:, t8, :], pq[:, 0:32], rowb[:, 0:32], ALU.add, [pq, rowb], [(p_dtt, t8)])
                ACT(p_dtt[:, t8, :], p_dtt[:, t8, :], AF.Exp, [(p_dtt, t8)], [(p_dtt, t8)])
                ACT(p_dtt[:, t8, :], p_dtt[:, t8, :], AF.Ln, [(p_dtt, t8)], [(p_dtt, t8)], bias=1.0)
                TTo("dve", p_dtA[:, t8, :], p_dtt[:, t8, :], rowb[:, 32:64], ALU.mult, [(p_dtt, t8), rowb], [(p_dtA, t8)])
                p1 = QR.next(); p2 = QR.next()
                MM(p1[:, 0:32], C("U"), p_dtA[:, t8, :], [cst, (p_dtA, t8)], [p1])
                MM(p2[:, 0:32], C("ones"), p_dtA[:, t8, :], [cst, (p_dtA, t8)], [p2])
                CP("dve", p_cum[:, t8, :], p1[:, 0:32], [p1], [(p_cum, t8)])
                ACT(p_Eend[:, t8, :], p2[:, 0:32], AF.Exp, [p2], [(p_Eend, t8)])
                TTo("dve", p_wend[:, t8, :], p2[:, 0:32], p_cum[:, t8, :], ALU.subtract, [p2, (p_cum, t8)], [(p_wend, t8)])
                ACT(p_wend[:, t8, :], p_wend[:, t8, :], AF.Exp, [(p_wend, t8)], [(p_wend, t8)])
                TTo("dve", p_wend[:, t8, :], p_wend[:, t8, :], p_dtt[:, t8, :], ALU.mult, [(p_wend, t8), (p_dtt, t8)],
                    [(p_wend, t8)])

            def tokB(wsl, t8, nco):
                pm = MMR.next()
                for kc in range(NKC):
                    MM(pm[:, 0:nco], hTp[:, kc, t8 * 128:(t8 + 1) * 128], wsl[:, kc, 0:nco], [hTp, wsl], [pm],
                       start=(kc == 0), stop=(kc == NKC - 1))
                return pm

            def gh(hb):
                cs = slice(hb * WB, (hb + 1) * WB)
                wsl = W.take(w_in_d, 0, [(1 * D + hb * WB, WB)], hold=True)

                def stA(t8):
                    lf, kk = p_lf[t8 % 2], p_kk[t8 % 2]
                    pm = tokB(wsl, t8, WB)
                    ACT(lf[:], pm[:, 0:WB], AF.Sigmoid, [pm], [lf], scale=-1.0)
                    TTo("dve", kk[:], lf[:], oml_b[:, cs], ALU.mult, [lf, oml_b], [kk])
                    ACT(lf[:], kk[:], AF.Ln, [kk], [lf], bias=1.0, scale=-1.0)
                    if t8 == NT8 - 1:
                        W.release(wsl)

                def stB(t8):
                    lf, kk, ex = p_lf[t8 % 2], p_kk[t8 % 2], p_ex[t8 % 2]
                    pa = HR.next(); pb = HR.next(); pc = QR.next()
                    MM(pa[:, 0:WB], C("triP"), lf[:], [cst, lf], [pa])
                    MM(pb[:, 0:WB], C("upper"), lf[:], [cst, lf], [pb])
                    for h in range(HU):
                        MM(pc[:, 2 * h:2 * h + 2], lf[:, h * 128:(h + 1) * 128], hsel, [cst, lf], [pc])
                    ACT(p_ec[:, t8, :, :].rearrange("p h c -> p (h c)"), pc[:, 0:2 * HU], AF.Exp, [pc], [(p_ec, t8)])
                    ACT(ex[:, 0, :], pa[:, 0:WB], AF.Exp, [pa], [(ex, 0)], scale=-1.0)
                    ACT(ex[:, 1, :], pb[:, 0:WB], AF.Exp, [pb], [(ex, 1)])
                    TTo("dve", ex[:, 1, :], ex[:, 1, :], ex[:, 0, :], ALU.mult, [(ex, 0), (ex, 1)], [(ex, 1)])
                    TTo("dve", p_ke[:, t8, :], kk[:], ex[:, 1, :], ALU.mult, [kk, (ex, 1)], [(p_ke, t8)])
                stA(0)
                yield
                for t8 in range(NT8):
                    if t8 + 1 < NT8:
                        stA(t8 + 1)
                        yield
                    stB(t8)
                    yield
                wsl2 = W.take(w_in_d, 0, [(2 * D + hb * WB, WB)], hold=True)

                def stAi(t8):
                    vv = p_vv[t8 % 2]
                    pm = tokB(wsl2, t8, WB)
                    CP("act", vv[:], pm[:, 0:WB], [pm], [vv])
                    if t8 == NT8 - 1:
                        W.release(wsl2)

                def stBi(t8):
                    vv = p_vv[t8 % 2]
                    for h in range(HU):
                        hd = hb * HU + h
                        hc = slice(h * 128, (h + 1) * 128)
                        pd = QR.next()
                        MM(pd[:], p_ke[:, t8, hc], vv[:, hc], [(p_ke, t8), vv], [pd])
                        STT("dve", Sst[:, hd, :], Sst[:, hd, :], p_ec[:, t8, h, 1:2], pd[:], ALU.mult, ALU.add,
                            [(Sst, hd), (p_ec, t8), pd], [(Sst, hd)])
                stAi(0)
                yield
                for t8 in range(NT8):
                    if t8 + 1 < NT8:
                        stAi(t8 + 1)
                        yield
                    stBi(t8)
                    yield

            def gs(g):
                hs = slice(g * 4, g * 4 + 4)

                def hb3(ap):
                    return ap.unsqueeze(2).to_broadcast([128, 4, 64])
                for blk in range(2):
                    if blk == 0:
                        wsl = W.take(w_in_d, 0, [(5 * D + g * 256, 256)], hold=True)
                    else:
                        wsl = W.take(w_in_d, 0, [(5 * D + D + g * 128, 128), (5 * D + D + 1024 + g * 128, 128)], hold=True)
                    for m in range(2):
                        chid = (g * 2 + m) if blk == 0 else (16 + g if m == 0 else 24 + g)
                        isC = (blk == 1 and m == 1)
                        for hf in ((1,) if isC else (0, 1)):
                            pm = MMR.next()
                            for kc in range(NKC):
                                MM(pm[:, 0:512], wsl[:, kc, m * 128:(m + 1) * 128], hTp[:, kc, hf * 512:(hf + 1) * 512],
                                   [hTp, wsl], [pm], start=(kc == 0), stop=(kc == NKC - 1))
                            CP("act", p_raw[:, 3 + hf * 512:3 + (hf + 1) * 512], pm[:, 0:512], [pm], [p_raw])
                        if m == 1:
                            W.release(wsl)
                        if isC:
                            CP("dve", halo[:, chid, :], p_raw[:, SEQ_HALF:SEQ_HALF + 3], [p_raw], [(halo, chid)])
                            yield
                            continue
                        CP("dve", p_raw[:, 0:3], halo[:, chid, :], [(halo, chid)], [p_raw])
                        CP("dve", halo[:, chid, :], p_raw[:, SEQ_HALF:SEQ_HALF + 3], [p_raw], [(halo, chid)])
                        cw = lambda i, chid=chid: col("cw", chid * 4 + i, 1)
                        TS("dve", p_acc[:], p_raw[:, 0:SEQ_HALF], cw(0), col("cb", chid, 1), ALU.mult, ALU.add, [p_raw, colv], [p_acc])
                        for i in (1, 2, 3):
                            STT("dve", p_acc[:], p_raw[:, i:i + SEQ_HALF], cw(i), p_acc[:], ALU.mult, ALU.add,
                                [p_raw, p_acc, colv], [p_acc])
                        ci = m if blk == 0 else 2
                        ACT(p_xa[:, ci, :], p_acc[:], AF.Silu, [p_acc], [(p_xa, ci)])
                        yield
                def stT(t8):
                    tc_ = slice(t8 * 128, (t8 + 1) * 128)
                    xst, Bt, xw = p_xst[t8 % 2], p_Bt[t8 % 2], p_xw[t8 % 2]
                    ph = HR.next()
                    pv = ph[:].bitcast(BF16)
                    for m in range(3):
                        TR(pv[:, m * 128:(m + 1) * 128], p_xa[:, m, tc_], identb[:], [(p_xa, m), identb], [ph])
                    CP("act", xst[:], pv[:, 0:256], [ph], [xst])
                    CP("dve", Bt[:], pv[:, 256:384], [ph], [Bt])
                    TTo("dve", xw[:].rearrange("p (h c) -> p h c", h=4), xst[:].rearrange("p (h c) -> p h c", h=4),
                        hb3(p_wend[:, t8, hs]), ALU.mult, [xst, (p_wend, t8)], [xw])

                def stP(t8):
                    Bt, xw = p_Bt[t8 % 2], p_xw[t8 % 2]
                    pdl = HR.next()
                    MM(pdl[:, 0:256], Bt[:], xw[:], [Bt, xw], [pdl])
                    h3 = hst[:, g * 256:(g + 1) * 256].rearrange("p (h c) -> p h c", h=4)
                    TTo("dve", h3, h3, hb3(p_Eend[:, t8, hs]), ALU.mult, [(hst, g), (p_Eend, t8)], [(hst, g)])
                    TTo("dve", hst[:, g * 256:(g + 1) * 256], hst[:, g * 256:(g + 1) * 256], pdl[:, 0:256], ALU.add,
                        [(hst, g), pdl], [(hst, g)])
                stT(0)
                yield
                for t8 in range(NT8):
                    if t8 + 1 < NT8:
                        stT(t8 + 1)
                        yield
                    stP(t8)
                    yield

            stepc = 0
            ga = gada()
            for u_ in range(8):
                gens = [gh(u_), gs(u_)]
                while gens:
                    for gen in list(gens):
                        try:
                            next(gen)
                        except StopIteration:
                            gens.remove(gen)
                        stepc += 1
                        if stepc % 6 == 0:
                            W.preconvert()
                        if stepc % 12 == 0:
                            next(ga, None)
            for _ in ga:
                pass
            S.barrier()

        DMA("sp", "rowb", rowb[:], rows3_d.partition_broadcast(128), (), [rowb])
        ACT(rowb[:, 32:64], rowb[:, 32:64], AF.Exp, [rowb], [rowb])
        TS("dve", rowb[:, 32:64], rowb[:, 32:64], -1.0, None, ALU.mult, None, [rowb], [rowb])
        ACT(hcol[:, 1:2], hcol[:, 1:2], AF.Exp, [hcol], [hcol])
        TS("dve", hcol[:, 1:2], hcol[:, 1:2], -1.0, None, ALU.mult, None, [hcol], [hcol])

        if stop == "p0":
            nst = npre = 0
        if npre:
            prefix_pass()
        else:
            for _ in gada():
                pass
        ada_finish()
        if npre:
            for hd in range(16):
                TS("dve", Sst[:, hd, :], Sst[:, hd, :], flag[:, 0:1], None, ALU.mult, None, [(Sst, hd), flag], [(Sst, hd)])
            for g in range(8):
                gsl = slice(g * 256, (g + 1) * 256)
                TS("dve", hst[:, gsl], hst[:, gsl], flag[:, 0:1], None, ALU.mult, None, [(hst, g), flag], [(hst, g)])
                CP("act", hstb[:, gsl], hst[:, gsl], [(hst, g)], [(hstb, g)])
            TS("dve", halo[:], halo[:], flag[:, 0:1], None, ALU.mult, None, [halo, flag], [halo])
        for s_ in range(nst):
            out_rows[0] = yp_d[s_ * ST:(s_ + 1) * ST, :]
            supertile(xp_d[s_ * ST:(s_ + 1) * ST, :], False, (s_ == 0) and do_samples, s_ == nst - 1, preconv=(s_ == 0))

        DMA("sp", "hgp", hgp_d.rearrange("h k v -> k h v"), Sst[:], [Sst], ())
        sso = xtile[0]
        for j in range(16):
            pq = QR.next()
            TR(pq[:], hst[:, j * 128:(j + 1) * 128], C("ident"), [(hst, j // 2), cst], [pq])
            CP("act", sso[:, j * 128:(j + 1) * 128], pq[:], [pq], [sso])
        DMA("sp", "xtile0", ssp_d.rearrange("(j two) p n -> (two p) j n", two=2),
            sso[:].rearrange("p (j n) -> p j n", j=16), [sso], ())
        cvo = xtile[1]
        for hf in range(2):
            for k4 in range(4):
                pm = MMR.next()
                for j in range(4):
                    chid = hf * 16 + k4 * 4 + j
                    TR(pm[0:3, j * 128:(j + 1) * 128], halo[:, chid, :], C("ident"), [(halo, chid), cst], [pm])
                CP("act", cvo[0:3, k4 * 512:(k4 + 1) * 512], pm[0:3, :], [pm], [cvo])
            DMA("sp", "xtile1", cvp_d[:, hf * D:(hf + 1) * D], cvo[0:3, :], [cvo], ())

        if W.record is not None:
            return W.record
        assert stop is not None or W.taken == len(W.reqs), (W.taken, len(W.reqs))
        S.emit(st)
    return nc


def build2(taps=None, **kw):
    reqs = build(bass.Bass("TRN2", target_bir_lowering=False), taps=taps, wreqs=None, **kw)
    return build(bass.Bass("TRN2", target_bir_lowering=False), taps=taps, wreqs=reqs, **kw)


def make_in_maps(inp, n_cores=8):
    f32 = lambda a: np.ascontiguousarray(np.asarray(a), dtype=np.float32)
    cst = host_consts()
    colv = np.zeros((128, NCOL), np.float32)

    def put(name, arr):
        o, w = COLS[name]
        assert arr.shape == (128, w), (name, arr.shape)
        colv[:, o:o + w] = arr
    put("g1", fm(inp["norm_pre_mix"][0], 16)); put("g2", fm(inp["norm_pre_mlp"][0], 16))
    put("gp1", fm(inp["norm_post_mix"][0], 16)); put("gp2", fm(inp["norm_post_mlp"][0], 16))
    put("bada", fm(inp["b_ada"][0], 96)); put("hn", fm(inp["hgrn_norm"][0], 16)); put("sn", fm(inp["ssd_norm"][0], 16))
    put("cb", fm(inp["conv_b"][0], 32))
    cw = np.asarray(inp["conv_w"][0], np.float32)
    put("cw", np.ascontiguousarray(cw.reshape(4, 32, 128).transpose(2, 1, 0).reshape(128, 128)))
    put("lb0", fm(inp["hgrn_lb_logits"][0], 16)); put("lb1", fm(inp["hgrn_lb_logits"][1], 16))
    put("dskE", fm(np.repeat(np.asarray(inp["d_skip"][0], np.float32), 64), 16))
    rowv = f32(np.concatenate([inp["hgrn_lb_logits"][0], inp["hgrn_lb_logits"][1]])[None, :])
    rows3 = f32(np.concatenate([inp["dt_bias"][0], inp["a_log"][0], inp["d_skip"][0]])[None, :])
    hcol = f32(np.stack([inp["dt_bias"][0], inp["a_log"][0]], axis=1))
    shared = dict(cst=cst, colv=colv, rowv=rowv, rows3=rows3, hcol=hcol,
                  w_ada=f32(inp["w_ada"][0]), w_in=f32(inp["w_in"][0]), w_out=f32(inp["w_out"][0]),
                  w_up=f32(inp["w_up"][0]), w_down=f32(inp["w_down"][0]))
    maps = []
    for c in range(n_cores):
        b, half = c // 2, c % 2
        sl = slice(c * NS, (c + 1) * NS)
        m = dict(shared)
        m["xp"] = f32(inp["x_prompt"][b, half * SEQ_HALF:(half + 1) * SEQ_HALF])
        m["xpre"] = f32(inp["x_prompt"][b, 0:SEQ_HALF])
        m["xs"] = f32(inp["x_sample"][sl, 0])
        m["c17"] = f32(np.concatenate([inp["c_prompt"][b:b + 1], inp["c_sample"][sl]], axis=0))
        m["flag"] = np.full((128, 1), float(half), np.float32)
        m["shg"] = f32(inp["state_hgrn"][0, sl])
        m["ssm"] = f32(inp["state_ssm"][0, sl])
        m["scv"] = f32(np.asarray(inp["state_conv"][0, sl]).reshape(NS * 3, 4096))
        maps.append(m)
    return maps


_NC_CACHE = {}


def kernel(**inputs):
    inp = {k: np.asarray(v) for k, v in inputs.items()}
    if "nc" not in _NC_CACHE:
        _NC_CACHE["nc"] = build2()
    nc = _NC_CACHE["nc"]
    maps = make_in_maps(inp)
    res = run_bass_kernel_spmd(nc, maps, core_ids=list(range(8)))
    r = res.results
    yp = np.stack([np.concatenate([r[2 * b]["yp"], r[2 * b + 1]["yp"]], axis=0) for b in range(4)])
    ys = np.concatenate([r[c]["ys"] for c in range(8)], axis=0)[:, None, :]
    hgp = np.stack([r[2 * b + 1]["hgp"] for b in range(4)])[None]
    ssp = np.stack([r[2 * b + 1]["ssp"] for b in range(4)])[None]
    cvp = np.stack([r[2 * b + 1]["cvp"] for b in range(4)])[None]
    hgs = np.concatenate([r[c]["hgs"] for c in range(8)], axis=0)[None]
    sss = np.concatenate([r[c]["sss"] for c in range(8)], axis=0)[None]
    cvs = np.concatenate([r[c]["cvs"] for c in range(8)], axis=0)[None]
    return tuple(np.ascontiguousarray(a, dtype=np.float32) for a in (yp, ys, hgp, ssp, cvp, hgs, sss, cvs))
```

```python
import os
from contextlib import ExitStack
import numpy as np
import concourse.bass as bass
import concourse.mybir as mybir
from concourse.bass_utils import run_bass_kernel_spmd

F32 = mybir.dt.float32
BF16 = mybir.dt.bfloat16
AF = mybir.ActivationFunctionType
ALU = mybir.AluOpType
AX = mybir.AxisListType

D = 2048
NKC = 16
SEQ_HALF = 1024
NS = 16
ST = 256
TT = ST // 128
WB = 256
NW = 4
EPS = 1e-6
NST = SEQ_HALF // ST
NEG = -1.0e5


class Buf:
    def __init__(self, name, t, nparts=1, share=None, excl=False):
        self.name = name
        self.t = t
        self.nparts = nparts
        self.excl = excl
        if share is not None:
            self.nparts = share.nparts
            self.excl = share.excl
            self.last_w = share.last_w
            self.readers = share.readers
        else:
            self.last_w = [None] * nparts
            self.readers = [[] for _ in range(nparts)]

    def __getitem__(self, k):
        return self.t[k]


def _parts(acc):
    if isinstance(acc, Buf):
        return acc, range(acc.nparts)
    b, p = acc
    if p is None:
        return b, range(b.nparts)
    if isinstance(p, int):
        return b, (p,)
    return b, tuple(p)


class Sched:
    ENGS = ("pe", "act", "dve", "pool", "sp")

    def __init__(self, nc):
        self.nc = nc
        self.ops = {e: [] for e in self.ENGS}
        self.waited = {e: {} for e in self.ENGS}
        self.dma_count = {}
        self.pending = {e: [] for e in self.ENGS}

    def barrier(self):
        tgt = []
        for e in self.ENGS:
            for i in range(len(self.ops[e]) - 1, -1, -1):
                if self.ops[e][i][2] is None:
                    tgt.append(("eng", e, i))
                    break
        for c, n in self.dma_count.items():
            tgt.append(("dma", c, n))
        for e in self.ENGS:
            self.pending[e] = list(tgt)

    def op(self, eng, fn, reads=(), writes=(), dma=None):
        idx = len(self.ops[eng])
        deps = self.pending[eng]
        self.pending[eng] = []
        for acc in reads:
            b, ps = _parts(acc)
            for p in ps:
                if b.last_w[p] is not None:
                    deps.append(b.last_w[p])
                if b.excl:
                    deps.extend(r for r in b.readers[p] if not (r[0] == "eng" and r[1] == eng))
        for acc in writes:
            b, ps = _parts(acc)
            for p in ps:
                if b.last_w[p] is not None:
                    deps.append(b.last_w[p])
                deps.extend(b.readers[p])
        if dma is not None:
            n = self.dma_count.get(dma, 0)
            if n > 0:
                deps.append(("dma", dma, n))
            self.dma_count[dma] = n + 1
            me = ("dma", dma, n + 1)
        else:
            me = ("eng", eng, idx)
        waits = {}
        wd = self.waited[eng]
        for kind, src, val in deps:
            if kind == "eng" and src == eng and eng == "pe":
                continue
            key = (kind, src)
            if wd.get(key, -1) >= val:
                continue
            if waits.get(key, -1) < val:
                waits[key] = val
        for key, val in waits.items():
            wd[key] = val
        self.ops[eng].append([fn, waits, dma, False])
        for acc in reads:
            b, ps = _parts(acc)
            for p in ps:
                b.readers[p].append(me)
        for acc in writes:
            b, ps = _parts(acc)
            for p in ps:
                b.last_w[p] = me
                b.readers[p] = []
        return me

    def emit(self, stack):
        nc = self.nc
        for e in self.ENGS:
            for rec in self.ops[e]:
                for (kind, src), val in rec[1].items():
                    if kind == "eng":
                        self.ops[src][val][3] = True
        rank = {}
        for e in self.ENGS:
            r = 0
            rk = []
            for rec in self.ops[e]:
                if rec[2] is None and rec[3]:
                    r += 1
                rk.append(r)
            rank[e] = rk
        esem = {e: stack.enter_context(nc.semaphore("s_" + e)) for e in self.ENGS}
        dsem = {c: stack.enter_context(nc.semaphore("d_%s" % (c,))) for c in self.dma_count}
        final_waits = dict(self.dma_count)
        block = stack.enter_context(nc.Block())
        sched = self

        def body(ename):
            def run(eng):
                for fn, waits, dma, sig in sched.ops[ename]:
                    for (kind, src), val in waits.items():
                        if kind == "eng":
                            eng.wait_ge(esem[src], rank[src][val])
                        else:
                            eng.wait_ge(dsem[src], 16 * val)
                    ins = fn(eng)
                    if dma is not None:
                        ins.then_inc(dsem[dma], 16)
                    elif sig:
                        ins.then_inc(esem[ename], 1)
                if ename == "sp":
                    for c, n in final_waits.items():
                        eng.wait_ge(dsem[c], 16 * n)
            return run

        block.tensor(body("pe"))
        block.scalar(body("act"))
        block.vector(body("dve"))
        block.gpsimd(body("pool"))
        block.sync(body("sp"))


CST_NAMES = ["ident", "triP", "U", "ones", "upper", "cmask", "negmask"]


def host_consts():
    s = np.arange(128)[:, None]
    t = np.arange(128)[None, :]
    c = {}
    c["ident"] = (s == t)
    c["triP"] = (s <= t).astype(np.float32) - (s <= 63).astype(np.float32) * np.ones_like(t)
    c["U"] = (s <= t)
    c["ones"] = np.ones((128, 128))
    c["upper"] = (s > 63) * np.ones_like(t)
    c["cmask"] = (t >= s)
    c["negmask"] = np.where(t >= s, 0.0, NEG)
    arr = np.concatenate([np.asarray(c[n], np.float32) for n in CST_NAMES], axis=1)
    hsel = np.stack([(np.arange(128) <= 63).astype(np.float32), np.ones(128, np.float32)], axis=1)
    return np.ascontiguousarray(np.concatenate([arr, hsel], axis=1), dtype=np.float32)


NCST = len(CST_NAMES) * 128 + 2

COLS = {}
_o = 0
for _n, _w in [("g1", 16), ("g2", 16), ("gp1", 16), ("gp2", 16), ("bada", 96), ("hn", 16), ("sn", 16),
               ("cb", 32), ("cw", 128), ("lb0", 16), ("lb1", 16), ("dskE", 16)]:
    COLS[_n] = (_o, _w)
    _o += _w
NCOL = _o


def fm(v, nch):
    return np.ascontiguousarray(np.asarray(v, np.float32).reshape(nch, 128).T)


def build(nc, taps=None, nst=NST, npre=NST, do_samples=True, stop=None, wreqs=None):
    taps = taps or []
    dt_in = lambda name, shape: nc.dram_tensor(name, shape, F32, kind="ExternalInput").ap()
    dt_out = lambda name, shape: nc.dram_tensor(name, shape, F32, kind="ExternalOutput").ap()
    xp_d = dt_in("xp", [SEQ_HALF, D])
    xpre_d = dt_in("xpre", [SEQ_HALF, D])
    xs_d = dt_in("xs", [NS, D])
    c17_d = dt_in("c17", [NS + 1, D])
    flag_d = dt_in("flag", [128, 1])
    shg_d = dt_in("shg", [NS, 16, 128, 128])
    ssm_d = dt_in("ssm", [NS, 32, 64, 128])
    scv_d = dt_in("scv", [NS * 3, 4096])
    cst_d = dt_in("cst", [128, NCST])
    colv_d = dt_in("colv", [128, NCOL])
    rowv_d = dt_in("rowv", [1, 2 * D])
    hcol_d = dt_in("hcol", [32, 2])
    rows3_d = dt_in("rows3", [1, 96])
    w_ada_d = dt_in("w_ada", [D, 6 * D])
    w_in_d = dt_in("w_in", [D, 14368])
    w_out_d = dt_in("w_out", [2 * D, D])
    w_up_d = dt_in("w_up", [D, 4 * D])
    w_down_d = dt_in("w_down", [4 * D, D])
    yp_d = dt_out("yp", [SEQ_HALF, D])
    ys_d = dt_out("ys", [NS, D])
    hgp_d = dt_out("hgp", [16, 128, 128])
    ssp_d = dt_out("ssp", [32, 64, 128])
    cvp_d = dt_out("cvp", [3, 4096])
    hgs_d = dt_out("hgs", [NS, 16, 128, 128])
    sss_d = dt_out("sss", [NS, 32, 64, 128])
    cvs_d = dt_out("cvs", [NS, 3, 4096])
    scr_d = nc.dram_tensor("scr", [16, 2, 2 * NS], F32).ap()
    tap_d = {t[0]: nc.dram_tensor("tap_" + t[0], t[1], BF16 if (len(t) > 2 and t[2] == "bf16") else F32,
                                  kind="ExternalOutput").ap() for t in taps}

    S = Sched(nc)
    st = ExitStack()
    with st:
        def sb(name, shape, dt=F32, nparts=1):
            return Buf(name, st.enter_context(nc.sbuf_tensor("sb_" + name, shape, dt)), nparts)

        def MM(out, lhsT, rhs, R, W, start=True, stop=True):
            S.op("pe", lambda e: e.matmul(out, lhsT, rhs, start=start, stop=stop), R, W)

        def TR(out, in_, idn, R, W):
            S.op("pe", lambda e: e.transpose(out, in_, idn), R, W)

        def ACT(out, in_, func, R, W, bias=None, scale=None, accum=None):
            kw = {}
            if bias is not None:
                kw["bias"] = bias
            if scale is not None:
                kw["scale"] = scale
            if accum is not None:
                kw["accum_out"] = accum
            S.op("act", lambda e: e.activation(out, in_, func, **kw), R, W)

        def TS(eng, out, in0, s1, s2, op0, op1, R, W):
            if s2 is None:
                S.op(eng, lambda e: e.tensor_scalar(out, in0, s1, None, op0), R, W)
            else:
                S.op(eng, lambda e: e.tensor_scalar(out, in0, s1, s2, op0, op1), R, W)

        def TTo(eng, out, in0, in1, op, R, W):
            S.op(eng, lambda e: e.tensor_tensor(out, in0, in1, op), R, W)

        def STT(eng, out, in0, sc, in1, op0, op1, R, W, accum=None):
            if accum is None:
                S.op(eng, lambda e: e.scalar_tensor_tensor(out, in0, sc, in1, op0, op1), R, W)
            else:
                S.op(eng, lambda e: e.scalar_tensor_tensor(out, in0, sc, in1, op0, op1, accum_out=accum), R, W)

        def CP(eng, out, in_, R, W):
            if eng == "act":
                S.op("act", lambda e: e.copy(out, in_), R, W)
            else:
                S.op(eng, lambda e: e.tensor_copy(out, in_), R, W)

        def MSET(eng, ap, val, W):
            S.op(eng, lambda e: e.memset(ap, val), (), W)

        def RECIP(out, in_, R, W):
            S.op("dve", lambda e: e.reciprocal(out, in_), R, W)

        def DMA(eng, chan, out, in_, R, W):
            S.op(eng, lambda e: e.dma_start(out=out, in_=in_), R, W, dma=chan)

        def TAP(name, ap, R):
            if name in tap_d:
                DMA("sp", "tap_" + name, tap_d[name], ap, R, ())

        def rsqrt(out, in_, R, W, scale):
            ACT(out, in_, AF.Sqrt, R, W, bias=epsc[0:out.shape[0], 0:1], scale=scale)
            RECIP(out, out, W, W)

        class Ring:
            def __init__(self, bufs):
                self.bufs = bufs
                self.i = 0

            def next(self):
                b = self.bufs[self.i % len(self.bufs)]
                self.i += 1
                return b

        psb = [st.enter_context(nc.psum_tensor("psb%d" % i, [128, 512], F32)) for i in range(8)]
        bank = [Buf("bank%d" % i, psb[i], excl=True) for i in range(8)]
        MMR = Ring([Buf("pmm%d" % i, psb[i], share=bank[i]) for i in range(3)])
        HR = Ring([Buf("ph%d" % i, psb[3 + i % 2][:, (i // 2) * 256:(i // 2) * 256 + 256], share=bank[3 + i % 2])
                   for i in range(3)])
        PO = Buf("ppo", psb[4][:, 256:512], share=bank[4])
        QR = Ring([Buf("pq%d" % i, psb[5 + i % 2][:, (i // 2) * 128:(i // 2 + 1) * 128], share=bank[5 + i % 2])
                   for i in range(4)])
        STAT = Buf("pstat", psb[7], share=bank[7])

        cst = sb("cst", [128, NCST])
        colv = sb("colv", [128, NCOL])
        flag = sb("flag", [128, 1])
        epsc = sb("epsc", [128, 1])
        identb = sb("identb", [128, 128], BF16)
        DMA("sp", "cst", cst[:], cst_d, (), [cst])
        DMA("sp", "colv", colv[:], colv_d, (), [colv])
        DMA("sp", "flag", flag[:], flag_d, (), [flag])
        MSET("dve", epsc[:], EPS, [epsc])

        def C(name):
            i = CST_NAMES.index(name)
            return cst[:, i * 128:(i + 1) * 128]
        hsel = cst[:, len(CST_NAMES) * 128:len(CST_NAMES) * 128 + 2]
        CP("dve", identb[:], C("ident"), [cst], [identb])

        def col(name, j0=0, n=None):
            o, w = COLS[name]
            n = w - j0 if n is None else n
            return colv[:, o + j0:o + j0 + n]

        oml_b = sb("oml_b", [128, D])
        omlc = sb("omlc", [128, 16])
        lbc = sb("lbc", [128, 16])
        xtile = [sb("xtile%d" % i, [128, D]) for i in range(1)] * 2
        xnb = [sb("xnb%d" % i, [128, D], BF16) for i in range(1)] * 2
        DMA("sp", "oml_b", oml_b[:], rowv_d[:, D:2 * D].partition_broadcast(128), (), [oml_b])
        DMA("sp", "xtile0", xtile[0][:], rowv_d[:, 0:D].partition_broadcast(128), (), [xtile[0]])
        TTo("dve", oml_b[:], oml_b[:], xtile[0][:], ALU.subtract, [oml_b, xtile[0]], [oml_b])
        ACT(oml_b[:], oml_b[:], AF.Sigmoid, [oml_b], [oml_b])
        TTo("dve", omlc[:], col("lb1"), col("lb0"), ALU.subtract, [colv], [omlc])
        ACT(omlc[:], omlc[:], AF.Sigmoid, [omlc], [omlc])
        TTo("dve", lbc[:], col("lb0"), col("lb1"), ALU.subtract, [colv], [lbc])
        ACT(lbc[:], lbc[:], AF.Sigmoid, [lbc], [lbc])
        hcol = sb("hcol", [32, 2])
        DMA("sp", "hcol", hcol[:], hcol_d, (), [hcol])

        wslots = [sb("w%d" % i, [128, NKC, WB], BF16) for i in range(NW)]

        def w_requests():
            req = []
            for cbk in range(6 * D // WB):
                req.append((w_ada_d, 0, [(cbk * WB, WB)]))

            def inproj(state_only):
                r = []
                for hb in range(D // WB):
                    for gi, gname in enumerate("qfig"):
                        if state_only and gname in "qg":
                            continue
                        r.append((w_in_d, 0, [(gi * D + hb * WB, WB)]))
                for g in range(8):
                    if not state_only:
                        r.append((w_in_d, 0, [(4 * D + g * 256, 256)]))
                    r.append((w_in_d, 0, [(5 * D + g * 256, 256)]))
                    r.append((w_in_d, 0, [(5 * D + D + g * 128, 128), (5 * D + D + 1024 + g * 128, 128)]))
                return r
            for _ in range(npre):
                req += inproj(True)
            for _ in range(nst):
                req += inproj(False)
                for cbk in range(D // WB):
                    for kh in range(2):
                        req.append((w_out_d, kh * D, [(cbk * WB, WB)]))
                for cbk in range(4 * D // WB):
                    req.append((w_up_d, 0, [(cbk * WB, WB)]))
                for cbk in range(D // WB):
                    for kq in range(4):
                        req.append((w_down_d, kq * D, [(cbk * WB, WB)]))
            return req

        class WStream:
            def __init__(self, reqs):
                self.issued = 0
                self.taken = 0
                self.record = [] if reqs is None else None
                self.released = set()
                self.auto = []
                self.held = {}
                self.conv_ptr = 0
                self.nconv = 0
                self.free = list(range(NW))
                self.slot_of = {}
                if reqs is None:
                    return
                wmap = {a.tensor.name: a for a in (w_ada_d, w_in_d, w_out_d, w_up_d, w_down_d)}
                reqs = [(wmap[n], r0, list(segs)) for (n, r0, segs) in reqs]
                self.reqs = reqs
                self.keys = [(id(r[0]), r[1], tuple(r[2])) for r in reqs]
                uses = {}
                for k in self.keys:
                    uses[k] = uses.get(k, 0) + 1
                self.cidx = {}
                for k in self.keys:
                    if uses[k] > 1 and k not in self.cidx:
                        self.cidx[k] = len(self.cidx)
                self.cached = set()
                ncache = max(1, len(self.cidx))
                self.cache_d = nc.dram_tensor("wcache", [ncache, 128, NKC * WB], BF16).ap()
                self.cbuf = Buf("wcache", self.cache_d, ncache)

            def _issue(self, i):
                wd, r0, segs = self.reqs[i]
                k = self.keys[i]
                slot = wslots[self.slot_of[i]]
                chan = "w%d" % self.slot_of[i]
                if k in self.cached:
                    ci = self.cidx[k]
                    DMA("pool", chan, slot[:].rearrange("p a b -> p (a b)"), self.cache_d[ci], [(self.cbuf, ci)], [slot])
                    return
                off = 0
                for c0, ncol in segs:
                    src = wd[r0:r0 + D, c0:c0 + ncol].rearrange("(kc p) c -> p kc c", p=128)
                    DMA("pool", chan, slot[:, :, off:off + ncol], src, (), [slot])
                    off += ncol
                if k in self.cidx:
                    ci = self.cidx[k]
                    DMA("sp", "wb%d" % self.slot_of[i], self.cache_d[ci], slot[:].rearrange("p a b -> p (a b)"), [slot], [(self.cbuf, ci)])
                    self.cached.add(k)

            def _pump(self, upto):
                while self.issued < min(len(self.reqs), upto) and self.free:
                    j = self.issued
                    self.slot_of[j] = self.free.pop(0)
                    self._issue(j)
                    self.issued += 1

            def take(self, wd, r0=0, segs=None, hold=False):
                i = self.taken
                if self.record is not None:
                    self.record.append((wd.tensor.name, r0, tuple(segs)))
                    self.taken += 1
                    return wslots[i % NW]
                assert i < len(self.reqs), "weight stream exhausted"
                assert self.reqs[i][0] is wd and self.reqs[i][1] == r0 and tuple(self.reqs[i][2]) == tuple(segs), \
                    ("weight stream order mismatch", i, self.reqs[i][1:], r0, segs)
                for j in self.auto:
                    self.free.append(self.slot_of[j])
                self.auto = []
                self._pump(i + NW)
                assert self.issued > i, ("no free weight slot", i)
                if hold:
                    self.held[id(wslots[self.slot_of[i]])] = i
                else:
                    self.auto.append(i)
                self.taken += 1
                return wslots[self.slot_of[i]]

            def preconvert(self):
                if self.record is not None:
                    return
                j = max(self.conv_ptr, self.issued)
                while j < len(self.reqs):
                    k = self.keys[j]
                    if k in self.cidx and k not in self.cached:
                        break
                    j += 1
                self.conv_ptr = j
                if j >= len(self.reqs):
                    return
                wd, r0, segs = self.reqs[j]
                ci = self.cidx[k]
                dst = self.cache_d[ci].rearrange("p (a b) -> p a b", a=NKC)
                off = 0
                for c0, ncol in segs:
                    src = wd[r0:r0 + D, c0:c0 + ncol].rearrange("(kc p) c -> p kc c", p=128)
                    DMA("pool", "cv%d" % (self.nconv % 4), dst[:, :, off:off + ncol], src, (), [(self.cbuf, ci)])
                    off += ncol
                self.nconv += 1
                self.cached.add(k)

            def release(self, slot):
                if self.record is not None:
                    return
                self.free.append(self.slot_of[self.held.pop(id(slot))])
                self._pump(self.taken + NW - 1)

        W = WStream(wreqs)
        wdt = sb("wdt", [128, NKC, 32], BF16)
        DMA("pool", "wdt", wdt[:], w_in_d[:, 14336:14368].rearrange("(kc p) c -> p kc c", p=128), (), [wdt])

        NC_ = ST + NS
        XM = st.enter_context(nc.sbuf_tensor("sb_XM", [128, 2 * NKC * NC_], F32))
        xT = Buf("xT", XM[:, 0:NKC * NC_].rearrange("p (a b) -> p a b", a=NKC), NKC)
        hT = sb("hT", [128, NKC, NC_], BF16)
        oT = sb("oT", [128, 32, NC_], BF16, nparts=32)
        modT = Buf("modT", oT[:].rearrange("p a b -> p (a b)")[:, 0:96 * (NS + 1) * 2].bitcast(F32)
                   .rearrange("p (a b) -> p a b", a=96), 96)
        mixT = Buf("mixT", XM[:, NKC * NC_:2 * NKC * NC_].rearrange("p (a b) -> p a b", a=NKC), NKC)
        Sst = sb("Sst", [128, 16, 128], F32, nparts=16)
        hst = sb("hst", [128, D], F32, nparts=8)
        hstb = sb("hstb", [128, D], BF16, nparts=8)
        halo = sb("halo", [128, 32, 3], F32, nparts=32)
        RA_BYTES = 64 * NC_ * 2 + 4096
        RA = st.enter_context(nc.sbuf_tensor("sb_RA", [128, RA_BYTES // 2], BF16))
        SH1b = sb("SH1b", [128, NKC, NS + 1]); SH2b = sb("SH2b", [128, NKC, NS + 1])
        MSET("dve", Sst[:], 0.0, [Sst])
        MSET("dve", hst[:], 0.0, [hst])
        MSET("dve", hstb[:], 0.0, [hstb])
        MSET("dve", halo[:], 0.0, [halo])

        ctok = xtile[0]
        ctb = xnb[0]
        scT = sb("scT", [128, NKC, NS + 1], BF16)
        DMA("sp", "xtile0", ctok[0:NS + 1, :], c17_d, (), [ctok])
        ACT(ctb[0:NS + 1, :], ctok[0:NS + 1, :], AF.Silu, [ctok], [ctb])
        for kc in range(NKC):
            ph = HR.next()
            pv = ph[:].bitcast(BF16)
            TR(pv[:, 0:NS + 1], ctb[0:NS + 1, kc * 128:(kc + 1) * 128], identb[0:NS + 1, 0:NS + 1], [ctb, identb], [ph])
            CP("dve", scT[:, kc, :], pv[:, 0:NS + 1], [ph], [scT])
        def ada_block(cbk):
            wsl = W.take(w_ada_d, 0, [(cbk * WB, WB)])
            for m in range(WB // 128):
                ch = cbk * (WB // 128) + m
                pq = QR.next()
                for kc in range(NKC):
                    MM(pq[:, 0:NS + 1], wsl[:, kc, m * 128:(m + 1) * 128], scT[:, kc, :], [wsl, scT], [pq],
                       start=(kc == 0), stop=(kc == NKC - 1))
                ACT(modT[:, ch, :], pq[:, 0:NS + 1], AF.Identity, [pq, colv], [(modT, ch)], bias=col("bada", ch, 1))
        NADA = 6 * D // WB
        NADA1 = 2 * D // WB
        for cbk in range(NADA1):
            ada_block(cbk)
        G1 = sb("G1", [128, NKC, NS + 1]); G2 = sb("G2", [128, NKC, NS + 1])
        GT1 = sb("GT1", [128, NKC, NS + 1]); GT2 = sb("GT2", [128, NKC, NS + 1])

        def bc3(ap2):
            return ap2.unsqueeze(2).to_broadcast([128, NKC, NS + 1])

        def derive_G(Gx, sci, gname):
            TS("dve", Gx[:], modT[:, sci * 16:(sci + 1) * 16, :], 1.0, None, ALU.add, None, [modT], [Gx])
            TTo("dve", Gx[:], Gx[:], bc3(col(gname)), ALU.mult, [Gx, colv], [Gx])
        derive_G(G1, 1, "g1")
        CP("act", SH1b[:], modT[:, 0:16, :], [modT], [SH1b])
        SH1 = SH1b
        SH2 = SH2b

        def gada():
            for cbk in range(NADA1, NADA):
                ada_block(cbk)
                yield

        def ada_finish():
            derive_G(G2, 4, "g2")
            for (Gx, gti, gname) in ((GT1, 2, "gp1"), (GT2, 5, "gp2")):
                TTo("dve", Gx[:], modT[:, gti * 16:(gti + 1) * 16, :], bc3(col(gname)), ALU.mult, [modT, colv], [Gx])
            CP("act", SH2b[:], modT[:, 48:64, :], [modT], [SH2b])
            TAP("modT", modT[:], [modT])
            S.barrier()

        junk = sb("junk", [128, 512], BF16)
        junkf = sb("junkf", [128, 512], F32)
        dumf = sb("dumf", [128, 128], F32)
        st4 = sb("st4", [128, 8])
        xcnt = [0]
        dtt = sb("dtt", [128, TT, 32], nparts=TT); dtA = sb("dtA", [128, TT, 32], nparts=TT)
        cum = sb("cum", [128, TT, 32], nparts=TT); expcum = sb("expcum", [128, TT, 32], nparts=TT)
        Eend = sb("Eend", [128, TT, 32], nparts=TT); wend = sb("wend", [128, TT, 32], nparts=TT)
        rowb = sb("rowb", [128, 96])


        HU = WB // 128

        ra_off = [0]

        def ubuf(name, shape, dt=F32, np_=1):
            n = 1
            for d_ in shape[1:]:
                n *= d_
            nb = n * (4 if dt == F32 else 2)
            nb = (nb + 31) // 32 * 32
            o = ra_off[0]
            ra_off[0] += nb
            assert ra_off[0] <= RA_BYTES, ("RA overflow", name, ra_off[0], RA_BYTES)
            v = RA[:, o // 2:(o + nb) // 2]
            if dt == F32:
                v = v.bitcast(F32)
            v = v[:, 0:n]
            if len(shape) == 3:
                v = v.rearrange("p (a b) -> p a b", a=shape[1])
            elif len(shape) == 4:
                v = v.rearrange("p (a b c) -> p a b c", a=shape[1], b=shape[2])
            b_ = Buf("ra_" + name, v, np_)
            return [b_, b_]
        sq_b = ubuf("sq", [128, TT, WB], F32, TT)
        k_b = ubuf("kk", [128, TT, WB], F32, TT)
        lf_b = ubuf("lf", [128, TT, WB], F32, TT)
        ex_b = ubuf("ex", [128, 3, WB], F32, 3)
        qg_b = ubuf("qg", [128, TT, WB], BF16, TT)
        kg_b = ubuf("kg", [128, TT, WB], BF16, TT)
        ke_b = ubuf("ke", [128, TT, WB], BF16, TT)
        v_b = ubuf("vv", [128, TT, WB], BF16, TT)
        sg_b = ubuf("sg", [128, TT, WB], F32, TT)
        ec_b = ubuf("ec", [128, TT, HU, 2], F32, TT)
        qgT_b = ubuf("qgT", [128, HU, ST], BF16, TT)
        kgT_b = ubuf("kgT", [128, HU, ST], BF16, TT)
        AT_b = ubuf("AT", [128, 128], BF16)
        Sm_b = ubuf("Sm", [128, 128], BF16)
        on_b = ubuf("on", [128, WB], F32)
        onb_b = ubuf("onb", [128, WB], BF16)
        rs_b = ubuf("rs", [128, 4], F32)
        rs2_b = ubuf("rs2", [128, 4], F32)
        assert TT == 2 and WB == 256 and ST == 256
        sz_b = ubuf("sz", [128, TT, 256], F32, TT)
        raw_b = ubuf("raw", [128, 2, 3 + ST], F32, 2)
        acc_b = ubuf("acc", [128, 2, ST], F32, 2)
        xa_b = ubuf("xa", [128, 4, ST], BF16, 4)
        xst_b = ubuf("xst", [128, 256], BF16)
        Bt_b = ubuf("Bt", [128, 128], BF16)
        xw_b = ubuf("xw", [128, 256], BF16)
        xdt_b = ubuf("xdt", [128, 256], BF16)
        xsd_b = ubuf("xsd", [128, 256], BF16)
        CBT_b = ubuf("CBT", [128, 128], F32)
        seg_b = ubuf("seg", [128, 512], F32)
        dec_b = ubuf("dec", [128, 512], F32)
        scT_b = ubuf("scTT", [128, 512], BF16)
        yi_b = ubuf("yi", [128, 256], F32)
        yg_b = ubuf("yg", [128, 256], F32)
        ob_b = ubuf("ob", [128, 256], BF16)
        qsT_s = sb("qsT_s", [128, 16, NS]); fT_s = sb("fT_s", [128, 16, NS]); kkT_s = sb("kkT_s", [128, 16, NS])
        vT_s = sb("vT_s", [128, 16, NS]); sgT_s = sb("sgT_s", [128, 16, NS]); szT_s = sb("szT_s", [128, 16, NS])
        xbc_s = sb("xbc_s", [128, 32, NS]); raw_s = sb("raw_s", [128, 32, NS])
        cnt = {"u": 0}
        aT = Buf("aT", RA[:, 0:64 * NC_].rearrange("p (a b) -> p a b", a=64), 64)

        def supertile(x_rows, state_only, with_s, is_last, preconv=False):
            ncol = NC_ if with_s else ST
            tiles = [(x_rows[tt * 128:(tt + 1) * 128, :], 128, tt * 128) for tt in range(TT)]
            if with_s:
                tiles.append((xs_d, NS, ST))
            for (src, nr, c0) in tiles:
                xi = xcnt[0] % 2
                xcnt[0] += 1
                xt, xn = xtile[xi], xnb[xi]
                DMA("sp", xt.name, xt[0:nr, :], src, (), [xt])
                for q4 in range(4):
                    ACT(junk[0:nr, :], xt[0:nr, q4 * 512:(q4 + 1) * 512], AF.Square, [xt], [st4],
                        accum=st4[0:nr, q4:q4 + 1])
                S.op("dve", lambda e, nr=nr: e.tensor_reduce(st4[0:nr, 4:5], st4[0:nr, 0:4], AX.X, ALU.add), [st4], [st4])
                rsqrt(st4[0:nr, 5:6], st4[0:nr, 4:5], [st4], [st4], 1.0 / D)
                TS("dve", xn[0:nr, :], xt[0:nr, :], st4[0:nr, 5:6], None, ALU.mult, None, [xt, st4], [xn])
                for k4 in range(4):
                    if not state_only:
                        pm = MMR.next()
                        for j in range(4):
                            kc = k4 * 4 + j
                            TR(pm[:, j * 128:j * 128 + nr], xt[0:nr, kc * 128:(kc + 1) * 128], C("ident")[0:nr, 0:nr],
                               [xt, cst], [pm])
                        CP("act", xT[:, k4 * 4:k4 * 4 + 4, c0:c0 + nr],
                           pm[:].rearrange("p (j c) -> p j c", j=4)[:, :, 0:nr], [pm], [(xT, range(k4 * 4, k4 * 4 + 4))])
                    ph = HR.next()
                    pv = ph[:].bitcast(BF16)
                    for j in range(4):
                        kc = k4 * 4 + j
                        TR(pv[:, j * 128:j * 128 + nr], xn[0:nr, kc * 128:(kc + 1) * 128], identb[0:nr, 0:nr],
                           [xn, identb], [ph])
                    for j in range(4):
                        kc = k4 * 4 + j
                        if nr == 128:
                            TS("dve", hT[:, kc, c0:c0 + nr], pv[:, j * 128:j * 128 + nr], G1[:, kc, 0:1], SH1[:, kc, 0:1],
                               ALU.mult, ALU.add, [ph, G1, SH1b], [hT])
                        else:
                            TTo("dve", hT[:, kc, c0:c0 + nr], pv[:, j * 128:j * 128 + nr], G1[:, kc, 1:NS + 1], ALU.mult,
                                [ph, G1], [hT])
                            TTo("dve", hT[:, kc, c0:c0 + nr], hT[:, kc, c0:c0 + nr], SH1[:, kc, 1:NS + 1], ALU.add,
                                [hT, SH1b], [hT])
            TAP("hT", hT[:, :, 0:ST], [hT])
            TAP("xT", xT[:, :, 0:ST], [xT])
            if stop == "A":
                return

            def optB(wsl, wcols, nco):
                outs = []
                for tt in range(TT):
                    pm = MMR.next()
                    for kc in range(NKC):
                        MM(pm[:, 0:nco], hT[:, kc, tt * 128:(tt + 1) * 128], wsl[:, kc, wcols:wcols + nco], [hT, wsl], [pm],
                           start=(kc == 0), stop=(kc == NKC - 1))
                    outs.append(pm)
                return outs

            def optA(wsl, m, c0, c1):
                pm = MMR.next()
                for kc in range(NKC):
                    MM(pm[:, 0:c1 - c0], wsl[:, kc, m * 128:(m + 1) * 128], hT[:, kc, c0:c1], [hT, wsl], [pm],
                       start=(kc == 0), stop=(kc == NKC - 1))
                return pm

            for tt in range(TT):
                pq = QR.next()
                for kc in range(NKC):
                    MM(pq[:, 0:32], hT[:, kc, tt * 128:(tt + 1) * 128], wdt[:, kc, :], [hT, wdt], [pq],
                       start=(kc == 0), stop=(kc == NKC - 1))
                TTo("dve", dtt[:, tt, :], pq[:, 0:32], rowb[:, 0:32], ALU.add, [pq, rowb], [(dtt, tt)])
                ACT(dtt[:, tt, :], dtt[:, tt, :], AF.Exp, [(dtt, tt)], [(dtt, tt)])
                ACT(dtt[:, tt, :], dtt[:, tt, :], AF.Ln, [(dtt, tt)], [(dtt, tt)], bias=1.0)
                TTo("dve", dtA[:, tt, :], dtt[:, tt, :], rowb[:, 32:64], ALU.mult, [(dtt, tt), rowb], [(dtA, tt)])
                p1 = QR.next(); p2 = QR.next()
                MM(p1[:, 0:32], C("U"), dtA[:, tt, :], [cst, (dtA, tt)], [p1])
                MM(p2[:, 0:32], C("ones"), dtA[:, tt, :], [cst, (dtA, tt)], [p2])
                CP("dve", cum[:, tt, :], p1[:, 0:32], [p1], [(cum, tt)])
                ACT(expcum[:, tt, :], p1[:, 0:32], AF.Exp, [p1], [(expcum, tt)])
                ACT(Eend[:, tt, :], p2[:, 0:32], AF.Exp, [p2], [(Eend, tt)])
                TTo("dve", wend[:, tt, :], p2[:, 0:32], cum[:, tt, :], ALU.subtract, [p2, (cum, tt)], [(wend, tt)])
                ACT(wend[:, tt, :], wend[:, tt, :], AF.Exp, [(wend, tt)], [(wend, tt)])
                TTo("dve", wend[:, tt, :], wend[:, tt, :], dtt[:, tt, :], ALU.mult, [(wend, tt), (dtt, tt)], [(wend, tt)])
            if with_s:
                pq = QR.next()
                for kc in range(NKC):
                    MM(pq[0:32, 0:NS], wdt[:, kc, :], hT[:, kc, ST:NC_], [hT, wdt], [pq], start=(kc == 0), stop=(kc == NKC - 1))
                ACT(dts[:, 0, :], pq[0:32, 0:NS], AF.Exp, [pq, hcol], [dts], bias=hcol[:, 0:1])
                ACT(dts[:, 0, :], dts[:, 0, :], AF.Ln, [dts], [dts], bias=1.0)
                TS("dve", dts[:, 1, :], dts[:, 0, :], hcol[:, 1:2], None, ALU.mult, None, [dts, hcol], [dts])
                ACT(dts[:, 1, :], dts[:, 1, :], AF.Exp, [dts], [dts])

            def gen_hgrn(hb):
                sq, kk, lf, ex, qg, kg, ke, vv, sg, ec = (sq_b[0], k_b[0], lf_b[0], ex_b[0], qg_b[0], kg_b[0], ke_b[0],
                                                          v_b[0], sg_b[0], ec_b[0])
                qgT, kgT = qgT_b[0], kgT_b[0]
                cs = slice(hb * WB, (hb + 1) * WB)
                for gname in "qfig":
                    if state_only and gname in "qg":
                        continue
                    gi = "qfig".index(gname)
                    wsl = W.take(w_in_d, 0, [(gi * D + hb * WB, WB)])
                    pms = optB(wsl, 0, WB)
                    for tt, pm in enumerate(pms):
                        if gname == "q":
                            ACT(sq[:, tt, :], pm[:, 0:WB], AF.Silu, [pm], [(sq, tt)])
                        elif gname == "i":
                            CP("act", vv[:, tt, :], pm[:, 0:WB], [pm], [(vv, tt)])
                        elif gname == "g":
                            ACT(sg[:, tt, :], pm[:, 0:WB], AF.Silu, [pm], [(sg, tt)])
                        else:
                            ACT(lf[:, tt, :], pm[:, 0:WB], AF.Sigmoid, [pm], [(lf, tt)], scale=-1.0)
                    if gname == "f":
                        for tt in range(TT):
                            TTo("dve", kk[:, tt, :], lf[:, tt, :], oml_b[:, cs], ALU.mult, [(lf, tt), oml_b], [(kk, tt)])
                        for tt in range(TT):
                            ACT(lf[:, tt, :], kk[:, tt, :], AF.Ln, [(kk, tt)], [(lf, tt)], bias=1.0, scale=-1.0)
                    if with_s:
                        for m in range(HU):
                            ch = hb * HU + m
                            pm = optA(wsl, m, ST, NC_)
                            if gname == "q":
                                ACT(qsT_s[:, ch, :], pm[:, 0:NS], AF.Silu, [pm], [qsT_s])
                            elif gname == "i":
                                CP("act", vT_s[:, ch, :], pm[:, 0:NS], [pm], [vT_s])
                            elif gname == "g":
                                ACT(sgT_s[:, ch, :], pm[:, 0:NS], AF.Silu, [pm], [sgT_s])
                            else:
                                ACT(kkT_s[:, ch, :], pm[:, 0:NS], AF.Sigmoid, [pm], [kkT_s], scale=-1.0)
                                TS("dve", kkT_s[:, ch, :], kkT_s[:, ch, :], omlc[:, ch:ch + 1], None, ALU.mult, None,
                                   [kkT_s, omlc], [kkT_s])
                                TS("dve", fT_s[:, ch, :], kkT_s[:, ch, :], -1.0, 1.0, ALU.mult, ALU.add, [kkT_s], [fT_s])
                    yield
                for tt in range(TT):
                    pa = HR.next(); pb = HR.next(); pc = QR.next()
                    MM(pa[:, 0:WB], C("triP"), lf[:, tt, :], [cst, (lf, tt)], [pa])
                    MM(pb[:, 0:WB], C("upper"), lf[:, tt, :], [cst, (lf, tt)], [pb])
                    for h in range(HU):
                        MM(pc[:, 2 * h:2 * h + 2], lf[:, tt, h * 128:(h + 1) * 128], hsel, [cst, (lf, tt)], [pc])
                    ACT(ec[:, tt, :, :].rearrange("p h c -> p (h c)"), pc[:, 0:2 * HU], AF.Exp, [pc], [(ec, tt)])
                    ACT(ex[:, 0, :], pa[:, 0:WB], AF.Exp, [pa], [(ex, 0)])
                    ACT(ex[:, 1, :], pa[:, 0:WB], AF.Exp, [pa], [(ex, 1)], scale=-1.0)
                    ACT(ex[:, 2, :], pb[:, 0:WB], AF.Exp, [pb], [(ex, 2)])
                    TTo("dve", kg[:, tt, :], kk[:, tt, :], ex[:, 1, :], ALU.mult, [(kk, tt), (ex, 1)], [(kg, tt)])
                    TTo("dve", ex[:, 2, :], ex[:, 2, :], ex[:, 1, :], ALU.mult, [(ex, 1), (ex, 2)], [(ex, 2)])
                    TTo("dve", ke[:, tt, :], kk[:, tt, :], ex[:, 2, :], ALU.mult, [(kk, tt), (ex, 2)], [(ke, tt)])
                    if not state_only:
                        TTo("dve", qg[:, tt, :], sq[:, tt, :], ex[:, 0, :], ALU.mult, [(sq, tt), (ex, 0)], [(qg, tt)])
                    yield
                    if not state_only:
                        ph = HR.next()
                        pv = ph[:].bitcast(BF16)
                        for h in range(HU):
                            TR(pv[:, h * 128:(h + 1) * 128], qg[:, tt, h * 128:(h + 1) * 128], identb[:],
                               [(qg, tt), identb], [ph])
                            TR(pv[:, (HU + h) * 128:(HU + h + 1) * 128], kg[:, tt, h * 128:(h + 1) * 128], identb[:],
                               [(kg, tt), identb], [ph])
                        CP("act", qgT[:, :, tt * 128:(tt + 1) * 128],
                           pv[:, 0:HU * 128].rearrange("p (h c) -> p h c", h=HU), [ph], [(qgT, tt)])
                        CP("dve", kgT[:, :, tt * 128:(tt + 1) * 128],
                           pv[:, HU * 128:2 * HU * 128].rearrange("p (h c) -> p h c", h=HU), [ph], [(kgT, tt)])
                        yield
                for tt in range(TT):
                    tc_ = slice(tt * 128, (tt + 1) * 128)
                    po = PO if not state_only else None
                    for h in range(HU):
                        hd = hb * HU + h
                        hc = slice(h * 128, (h + 1) * 128)
                        if not state_only:
                            AT, Sm = AT_b[(tt * HU + h) % 2], Sm_b[(tt * HU + h) % 2]
                            TS("dve", Sm[:], Sst[:, hd, :], ec[:, tt, h, 0:1], None, ALU.mult, None, [(Sst, hd), (ec, tt)], [Sm])
                            pA = QR.next()
                            MM(pA[:], kgT[:, h, tc_], qgT[:, h, tc_], [(kgT, tt), (qgT, tt)], [pA])
                            TTo("dve", AT[:], pA[:], C("cmask"), ALU.mult, [pA, cst], [AT])
                            yield
                            MM(po[:, hc], AT[:], vv[:, tt, hc], [AT, (vv, tt)], [po], start=True, stop=False)
                            MM(po[:, hc], qgT[:, h, tc_], Sm[:], [(qgT, tt), Sm], [po], start=False, stop=True)
                        pd = QR.next()
                        MM(pd[:], ke[:, tt, hc], vv[:, tt, hc], [(ke, tt), (vv, tt)], [pd])
                        STT("dve", Sst[:, hd, :], Sst[:, hd, :], ec[:, tt, h, 1:2], pd[:], ALU.mult, ALU.add,
                            [(Sst, hd), (ec, tt), pd], [(Sst, hd)])
                        yield
                    if not state_only:
                        on, onb, rs = on_b[tt % 2], onb_b[tt % 2], rs_b[tt % 2]
                        for h in range(HU):
                            ACT(junk[:, 0:128], po[:, h * 128:(h + 1) * 128], AF.Square, [po], [rs], accum=rs[:, h:h + 1])
                        rsqrt(rs[:, 2:2 + HU], rs[:, 0:HU], [rs], [rs], 1.0 / 128)
                        TTo("dve", on[:].rearrange("p (h c) -> p h c", h=HU), po[:, 0:WB].rearrange("p (h c) -> p h c", h=HU),
                            rs[:, 2:2 + HU].unsqueeze(2).to_broadcast([128, HU, 128]), ALU.mult, [po, rs], [on])
                        TTo("dve", onb[:], on[:], sg[:, tt, :], ALU.mult, [on, (sg, tt)], [onb])
                        yield
                        ph = HR.next()
                        pv = ph[:].bitcast(BF16)
                        for h in range(HU):
                            TR(pv[:, h * 128:(h + 1) * 128], onb[:, h * 128:(h + 1) * 128], identb[:], [onb, identb], [ph])
                        TTo("dve", oT[:, hb * HU:(hb + 1) * HU, tc_], pv[:, 0:HU * 128].rearrange("p (h c) -> p h c", h=HU),
                            col("hn", hb * HU, HU).unsqueeze(2).to_broadcast([128, HU, 128]), ALU.mult, [ph, colv],
                            [(oT, range(hb * HU, (hb + 1) * HU))])
                        yield

            def gen_ssd(g):
                sz, raw, acc, xa = sz_b[0], raw_b[0], acc_b[0], xa_b[0]
                hs = slice(g * 4, g * 4 + 4)
                if not state_only:
                    wsl = W.take(w_in_d, 0, [(4 * D + g * 256, 256)])
                    pms = optB(wsl, 0, 256)
                    for tt, pm in enumerate(pms):
                        ACT(sz[:, tt, :], pm[:, 0:256], AF.Silu, [pm], [(sz, tt)])
                    if with_s:
                        for m in range(2):
                            pm = optA(wsl, m, ST, NC_)
                            ACT(szT_s[:, g * 2 + m, :], pm[:, 0:NS], AF.Silu, [pm], [szT_s])
                    yield
                for blk in range(2):
                    if blk == 0:
                        wsl = W.take(w_in_d, 0, [(5 * D + g * 256, 256)])
                    else:
                        wsl = W.take(w_in_d, 0, [(5 * D + D + g * 128, 128), (5 * D + D + 1024 + g * 128, 128)])
                    chids = [(g * 2 + m) if blk == 0 else (16 + g if m == 0 else 24 + g) for m in range(2)]
                    for m in range(2):
                        chid = chids[m]
                        pm = optA(wsl, m, 0, ncol)
                        CP("act", raw[:, m, 3:3 + ST], pm[:, 0:ST], [pm], [(raw, m)])
                        CP("dve", raw[:, m, 0:3], halo[:, chid, :], [(halo, chid)], [(raw, m)])
                        CP("dve", halo[:, chid, :], raw[:, m, ST:ST + 3], [(raw, m)], [(halo, chid)])
                        if with_s:
                            CP("act", raw_s[:, chid, :], pm[:, ST:NC_], [pm], [raw_s])
                    cw = lambda m, i, chids=chids: col("cw", chids[m] * 4 + i, 1)
                    for m in range(2):
                        TS("dve", acc[:, m, :], raw[:, m, 0:ST], cw(m, 0), col("cb", chids[m], 1), ALU.mult, ALU.add,
                           [(raw, m), colv], [(acc, m)])
                    for i in (1, 2, 3):
                        for m in range(2):
                            STT("dve", acc[:, m, :], raw[:, m, i:i + ST], cw(m, i), acc[:, m, :], ALU.mult, ALU.add,
                                [(raw, m), (acc, m), colv], [(acc, m)])
                    for m in range(2):
                        ACT(xa[:, blk * 2 + m, :], acc[:, m, :], AF.Silu, [(acc, m)], [(xa, blk * 2 + m)])
                    yield

                def hb3(ap):
                    return ap.unsqueeze(2).to_broadcast([128, 4, 64])
                for tt in range(TT):
                    tc_ = slice(tt * 128, (tt + 1) * 128)
                    xst, Bt, xw = xst_b[0], Bt_b[0], xw_b[0]
                    ph = HR.next()
                    pv = ph[:].bitcast(BF16)
                    for m in range(3):
                        TR(pv[:, m * 128:(m + 1) * 128], xa[:, m, tc_], identb[:], [(xa, m), identb], [ph])
                    CP("act", xst[:], pv[:, 0:256], [ph], [xst])
                    CP("dve", Bt[:], pv[:, 256:384], [ph], [Bt])
                    x3 = xst[:].rearrange("p (h c) -> p h c", h=4)
                    TTo("dve", xw[:].rearrange("p (h c) -> p h c", h=4), x3, hb3(wend[:, tt, hs]), ALU.mult,
                        [xst, (wend, tt)], [xw])
                    if not state_only:
                        xdt, xsd, CBT, seg, dec, scTT = xdt_b[0], xsd_b[0], CBT_b[0], seg_b[0], dec_b[0], scT_b[0]
                        yi, yg, ob = yi_b[0], yg_b[0], ob_b[0]
                        TTo("dve", xdt[:].rearrange("p (h c) -> p h c", h=4), x3, hb3(dtt[:, tt, hs]), ALU.mult,
                            [xst, (dtt, tt)], [xdt])
                        TTo("dve", xsd[:].rearrange("p (h c) -> p h c", h=4), x3, hb3(rowb[:, 64 + g * 4:64 + g * 4 + 4]),
                            ALU.mult, [xst, rowb], [xsd])
                        yield
                        pcb = QR.next()
                        MM(pcb[:], xa[:, 2, tc_], xa[:, 3, tc_], [(xa, 2), (xa, 3)], [pcb])
                        CP("act", CBT[:], pcb[:], [pcb], [CBT])
                        pcr = MMR.next()
                        for hh in range(4):
                            hd = g * 4 + hh
                            MM(pcr[:, hh * 128:(hh + 1) * 128], dtA[:, tt, hd:hd + 1].to_broadcast([128, 128]), C("U"),
                               [(dtA, tt), cst], [pcr])
                        for hh in range(4):
                            hd = g * 4 + hh
                            STT("dve", seg[:, hh * 128:(hh + 1) * 128], pcr[:, hh * 128:(hh + 1) * 128], cum[:, tt, hd:hd + 1],
                                C("negmask"), ALU.subtract, ALU.add, [pcr, (cum, tt), cst], [seg])
                        ACT(dec[:], seg[:], AF.Exp, [seg], [dec])
                        TTo("dve", scTT[:].rearrange("p (h c) -> p h c", h=4), dec[:].rearrange("p (h c) -> p h c", h=4),
                            CBT[:].unsqueeze(1).to_broadcast([128, 4, 128]), ALU.mult, [dec, CBT], [scTT])
                        yield
                        py = HR.next(); pyi = HR.next()
                        MM(py[:, 0:256], identb[:], xsd[:], [identb, xsd], [py], start=True, stop=False)
                        for hh in range(4):
                            MM(py[:, hh * 64:(hh + 1) * 64], scTT[:, hh * 128:(hh + 1) * 128], xdt[:, hh * 64:(hh + 1) * 64],
                               [scTT, xdt], [py], start=False, stop=(hh == 3))
                        MM(pyi[:, 0:256], xa[:, 3, tc_], hstb[:, g * 256:(g + 1) * 256], [(xa, 3), (hstb, g)], [pyi])
                        TTo("dve", yi[:].rearrange("p (h c) -> p h c", h=4), pyi[:, 0:256].rearrange("p (h c) -> p h c", h=4),
                            hb3(expcum[:, tt, hs]), ALU.mult, [pyi, (expcum, tt)], [yi])
                        TTo("dve", yi[:], py[:, 0:256], yi[:], ALU.add, [py, yi], [yi])
                        TTo("dve", yg[:], yi[:], sz[:, tt, :], ALU.mult, [yi, (sz, tt)], [yg])
                        rs = rs2_b[0]
                        ACT(junk[:, 0:256], yg[:], AF.Square, [yg], [rs], accum=rs[:, 0:1])
                        rsqrt(rs[:, 1:2], rs[:, 0:1], [rs], [rs], 1.0 / 256)
                        TS("dve", ob[:], yg[:], rs[:, 1:2], None, ALU.mult, None, [yg, rs], [ob])
                    pdl = HR.next()
                    MM(pdl[:, 0:256], Bt[:], xw[:], [Bt, xw], [pdl])
                    h3 = hst[:, g * 256:(g + 1) * 256].rearrange("p (h c) -> p h c", h=4)
                    TTo("dve", h3, h3, hb3(Eend[:, tt, hs]), ALU.mult, [(hst, g), (Eend, tt)], [(hst, g)])
                    TTo("dve", hst[:, g * 256:(g + 1) * 256], hst[:, g * 256:(g + 1) * 256], pdl[:, 0:256], ALU.add,
                        [(hst, g), pdl], [(hst, g)])
                    CP("act", hstb[:, g * 256:(g + 1) * 256], hst[:, g * 256:(g + 1) * 256], [(hst, g)], [(hstb, g)])
                    yield
                    if not state_only:
                        ph2 = HR.next()
                        pv2 = ph2[:].bitcast(BF16)
                        for m in range(2):
                            TR(pv2[:, m * 128:(m + 1) * 128], ob[:, m * 128:(m + 1) * 128], identb[:], [ob, identb], [ph2])
                        TTo("dve", oT[:, 16 + g * 2:16 + g * 2 + 2, tc_], pv2[:, 0:256].rearrange("p (h c) -> p h c", h=2),
                            col("sn", g * 2, 2).unsqueeze(2).to_broadcast([128, 2, 128]), ALU.mult, [ph2, colv],
                            [(oT, range(16 + g * 2, 16 + g * 2 + 2))])
                        yield

            stepc = 0
            for u_ in range(8):
                gens = [gen_hgrn(u_), gen_ssd(u_)]
                while gens:
                    for gen in list(gens):
                        try:
                            next(gen)
                        except StopIteration:
                            gens.remove(gen)
                        stepc += 1
                        if preconv and stepc % 6 == 0:
                            W.preconvert()

            TAP("oT", oT[:, :, 0:ST], [oT])
            TAP("hst", hst[:], [hst])
            if state_only or stop == "ssd":
                return

            if with_s and do_samples:
                sample_phase()

            def fm_proj(wd, nkb, K_rhs, dstT):
                for cbk in range(D // WB):
                    pms = [MMR.next() for _ in range(WB // 128)]
                    for kb in range(nkb):
                        wsl = W.take(wd, kb * D, [(cbk * WB, WB)])
                        for m in range(WB // 128):
                            for kc in range(NKC):
                                MM(pms[m][:, 0:ncol], wsl[:, kc, m * 128:(m + 1) * 128], K_rhs[:, kb * NKC + kc, 0:ncol],
                                   [wsl, K_rhs], [pms[m]], start=(kb == 0 and kc == 0), stop=(kb == nkb - 1 and kc == NKC - 1))
                    for m in range(WB // 128):
                        dc = cbk * (WB // 128) + m
                        CP("act", dstT[:, dc, 0:ncol], pms[m][:, 0:ncol], [pms[m]], [(dstT, dc)])
                        ACT(junkf[:, 0:ncol], pms[m][:, 0:ncol], AF.Square, [pms[m]], [junkf])
                        MM(STAT[:, 0:ncol], C("ones"), junkf[:, 0:ncol], [cst, junkf], [STAT], start=(dc == 0), stop=(dc == NKC - 1))

            def stat_rstd():
                rsqrt(rstd_b[:, 0:ncol], STAT[:, 0:ncol], [STAT], [rstd_b], 1.0 / D)

            def resid_add(GT, srcT):
                for dc in range(NKC):
                    STT("dve", junkf[:, 0:ST], srcT[:, dc, 0:ST], GT[:, dc, 0:1], rstd_b[:, 0:ST], ALU.mult, ALU.mult,
                        [(srcT, dc), GT, rstd_b], [junkf])
                    TTo("dve", xT[:, dc, 0:ST], xT[:, dc, 0:ST], junkf[:, 0:ST], ALU.add, [(xT, dc), junkf], [(xT, dc)])
                    if with_s:
                        TTo("dve", junkf[:, ST:NC_], srcT[:, dc, ST:NC_], GT[:, dc, 1:NS + 1], ALU.mult, [(srcT, dc), GT], [junkf])
                        TTo("dve", junkf[:, ST:NC_], junkf[:, ST:NC_], rstd_b[:, ST:NC_], ALU.mult, [junkf, rstd_b], [junkf])
                        TTo("dve", xT[:, dc, ST:NC_], xT[:, dc, ST:NC_], junkf[:, ST:NC_], ALU.add, [(xT, dc), junkf], [(xT, dc)])

            fm_proj(w_out_d, 2, oT, mixT)
            stat_rstd()
            resid_add(GT1, mixT)
            TAP("x1T", xT[:, :, 0:ST], [xT])
            if stop == "out":
                return

            for dc in range(NKC):
                ACT(junkf[:, 0:ncol], xT[:, dc, 0:ncol], AF.Square, [(xT, dc)], [junkf])
                MM(STAT[:, 0:ncol], C("ones"), junkf[:, 0:ncol], [cst, junkf], [STAT], start=(dc == 0), stop=(dc == NKC - 1))
            stat_rstd()
            for dc in range(NKC):
                STT("dve", junkf[:, 0:ST], xT[:, dc, 0:ST], G2[:, dc, 0:1], rstd_b[:, 0:ST], ALU.mult, ALU.mult,
                    [(xT, dc), G2, rstd_b], [junkf])
                TS("dve", hT[:, dc, 0:ST], junkf[:, 0:ST], SH2[:, dc, 0:1], None, ALU.add, None, [junkf, SH2b], [hT])
                if with_s:
                    TTo("dve", junkf[:, ST:NC_], xT[:, dc, ST:NC_], G2[:, dc, 1:NS + 1], ALU.mult, [(xT, dc), G2], [junkf])
                    TTo("dve", junkf[:, ST:NC_], junkf[:, ST:NC_], rstd_b[:, ST:NC_], ALU.mult, [junkf, rstd_b], [junkf])
                    TTo("dve", hT[:, dc, ST:NC_], junkf[:, ST:NC_], SH2[:, dc, 1:NS + 1], ALU.add, [junkf, SH2b], [hT])

            S.barrier()
            for cbk in range(4 * D // WB):
                wsl = W.take(w_up_d, 0, [(cbk * WB, WB)])
                for m in range(WB // 128):
                    fc = cbk * (WB // 128) + m
                    pm = MMR.next()
                    for kc in range(NKC):
                        MM(pm[:, 0:ncol], wsl[:, kc, m * 128:(m + 1) * 128], hT[:, kc, 0:ncol], [wsl, hT], [pm],
                           start=(kc == 0), stop=(kc == NKC - 1))
                    ACT(junkf[:, 0:ncol], pm[:, 0:ncol], AF.Relu, [pm], [junkf])
                    TTo("dve", aT[:, fc, 0:ncol], junkf[:, 0:ncol], junkf[:, 0:ncol], ALU.mult, [junkf], [(aT, fc)])
            fm_proj(w_down_d, 4, aT, mixT)
            stat_rstd()
            resid_add(GT2, mixT)

            S.barrier()
            for tt in range(TT):
                yt = xtile[xcnt[0] % 2]
                xcnt[0] += 1
                for k4 in range(4):
                    pm = MMR.next()
                    for j in range(4):
                        dc = k4 * 4 + j
                        TR(pm[:, j * 128:(j + 1) * 128], xT[:, dc, tt * 128:(tt + 1) * 128], C("ident"), [(xT, dc), cst], [pm])
                    CP("act", yt[:, k4 * 512:(k4 + 1) * 512], pm[:], [pm], [yt])
                DMA("sp", yt.name, out_rows[0][tt * 128:(tt + 1) * 128, :], yt[:], [yt], ())
            if with_s:
                yt = xtile[xcnt[0] % 2]
                xcnt[0] += 1
                for k4 in range(4):
                    pm = MMR.next()
                    for j in range(4):
                        dc = k4 * 4 + j
                        TR(pm[0:NS, j * 128:(j + 1) * 128], xT[:, dc, ST:NC_], C("ident"), [(xT, dc), cst], [pm])
                    CP("act", yt[0:NS, k4 * 512:(k4 + 1) * 512], pm[0:NS, :], [pm], [yt])
                DMA("sp", yt.name, ys_d, yt[0:NS, :], [yt], ())

        rstd_b = sb("rstd_b", [128, NC_])
        dts = sb("dts", [32, 2, NS])
        out_rows = [None]

        def raview(off_b, ncols_f32, shape3=None):
            v = RA[:, off_b // 2:off_b // 2 + ncols_f32 * 2].bitcast(F32)
            if shape3 is not None:
                v = v.rearrange("p (a b) -> p a b", a=shape3)
            return v
        Sb_s = [Buf("Sb%d" % i, raview(i * 8192, 2048, 16), 16) for i in range(2)]
        hb_s = [Buf("hb%d" % i, raview(16384 + i * 8192, 2048, 16), 16) for i in range(2)]
        yT_s = sb("yT_s", [128, 16, NS], nparts=16)
        dE = sb("dE", [128, 16, 2 * NS])
        dtx = sb("dtx", [128, 16, NS])
        oS = sb("oS", [128, 16 * NS])
        tmpS = sb("tmpS", [128, 16 * NS])
        scrb = Buf("scrb", scr_d)
        vTb_s = sb("vTb_s", [128, 16, NS], BF16)
        bcb_s = sb("bcb_s", [128, 16, NS], BF16)

        def sample_phase():
            S.barrier()
            cstT = hb_s[0]
            cv = cstT[:].rearrange("p a b -> p (a b)")[:, 0:32 * 48].rearrange("p (c x) -> p c x", c=32)
            for hf in range(2):
                xt = xtile[hf]
                DMA("sp", xt.name, xt[0:3 * NS, :], scv_d[:, hf * D:(hf + 1) * D], (), [xt])
                for k4 in range(4):
                    pm = MMR.next()
                    for j in range(4):
                        TR(pm[:, j * 128:j * 128 + 3 * NS], xt[0:3 * NS, (k4 * 4 + j) * 128:(k4 * 4 + j + 1) * 128],
                           C("ident")[0:3 * NS, 0:3 * NS], [xt, cst], [pm])
                    CP("act", cv[:, hf * 16 + k4 * 4:hf * 16 + k4 * 4 + 4, :],
                       pm[:].rearrange("p (j c) -> p j c", j=4)[:, :, 0:3 * NS], [pm], [cstT])
            DMA("sp", "cvs01", cvs_d[:, 0:2, :], scv_d.rearrange("(b r) c -> b r c", r=3)[:, 1:3, :], (), ())
            for chid in range(32):
                c3 = cv[:, chid, :].rearrange("p (b r) -> p b r", r=3)
                cw = lambda i: col("cw", chid * 4 + i, 1)
                TS("dve", xbc_s[:, chid, :], c3[:, :, 0], cw(0), col("cb", chid, 1), ALU.mult, ALU.add, [cstT, colv], [xbc_s])
                for i in (1, 2):
                    STT("dve", xbc_s[:, chid, :], c3[:, :, i], cw(i), xbc_s[:, chid, :], ALU.mult, ALU.add,
                        [cstT, colv, xbc_s], [xbc_s])
                STT("dve", xbc_s[:, chid, :], raw_s[:, chid, :], cw(3), xbc_s[:, chid, :], ALU.mult, ALU.add,
                    [raw_s, colv, xbc_s], [xbc_s])
            ACT(xbc_s[:], xbc_s[:], AF.Silu, [xbc_s], [xbc_s])
            for hf in range(2):
                xt = xtile[hf]
                for k4 in range(4):
                    pm = MMR.next()
                    for j in range(4):
                        chid = hf * 16 + k4 * 4 + j
                        TR(pm[0:NS, j * 128:(j + 1) * 128], raw_s[:, chid, :], C("ident"), [raw_s, cst], [pm])
                    CP("act", xt[0:NS, k4 * 512:(k4 + 1) * 512], pm[0:NS, :], [pm], [xt])
                DMA("sp", xt.name, cvs_d[:, 2, hf * D:(hf + 1) * D], xt[0:NS, :], [xt], ())
            DMA("sp", "scrw", scr_d.rearrange("j two x -> (j two) x"), dts[:].rearrange("p q b -> p (q b)"), [dts], [scrb])
            for two in range(2):
                DMA("sp", "dE%d" % two, dE[two * 64:(two + 1) * 64, :, :], scr_d[:, two, :].partition_broadcast(64),
                    [scrb], [dE])
            TTo("dve", dtx[:], dE[:, :, 0:NS], xbc_s[:, 0:16, :], ALU.mult, [dE, xbc_s], [dtx])

            CP("dve", vTb_s[:], vT_s[:], [vT_s], [vTb_s])
            CP("dve", bcb_s[:], xbc_s[:, 16:32, :], [xbc_s], [bcb_s])

            def load(b):
                DMA("sp", Sb_s[b % 2].name, Sb_s[b % 2][:], shg_d[b].rearrange("h k v -> k h v"), (), [Sb_s[b % 2]])
                DMA("sp", hb_s[b % 2].name, hb_s[b % 2][:], ssm_d[b].rearrange("(j two) p n -> (two p) j n", two=2), (),
                    [hb_s[b % 2]])
            load(0)
            for b in range(NS):
                if b + 1 < NS:
                    load(b + 1)
                Sb, hb = Sb_s[b % 2], hb_s[b % 2]
                for h4 in range(4):
                    pm = MMR.next()
                    for j in range(4):
                        h = h4 * 4 + j
                        MM(pm[:, j * 128:(j + 1) * 128], vTb_s[:, h, b:b + 1].to_broadcast([128, 128]), identb[:],
                           [vTb_s, identb], [pm])
                    for j in range(4):
                        h = h4 * 4 + j
                        ACT(Sb[:, h, :], Sb[:, h, :], AF.Identity, [(Sb, h), fT_s], [(Sb, h)], scale=fT_s[:, h, b:b + 1])
                        STT("dve", Sb[:, h, :], pm[:, j * 128:(j + 1) * 128], kkT_s[:, h, b:b + 1], Sb[:, h, :], ALU.mult, ALU.add,
                            [pm, kkT_s, (Sb, h)], [(Sb, h)])
                        MM(STAT[:, h * NS + b:h * NS + b + 1], Sb[:, h, :], qsT_s[:, h, b:b + 1], [(Sb, h), qsT_s], [STAT])
                DMA("sp", Sb.name, hgs_d[b].rearrange("h k v -> k h v"), Sb[:], [Sb], ())
                for g2 in range(4):
                    pm = MMR.next()
                    for gg in range(2):
                        g = g2 * 2 + gg
                        MM(pm[:, (gg * 2) * 128:(gg * 2 + 1) * 128], bcb_s[:, g, b:b + 1].to_broadcast([128, 128]),
                           identb[:], [bcb_s, identb], [pm])
                        MM(pm[:, (gg * 2 + 1) * 128:(gg * 2 + 2) * 128], bcb_s[:, 8 + g, b:b + 1].to_broadcast([128, 128]),
                           identb[:], [bcb_s, identb], [pm])
                    for gg in range(2):
                        g = g2 * 2 + gg
                        for jj in range(2):
                            j = g * 2 + jj
                            ACT(hb[:, j, :], hb[:, j, :], AF.Identity, [(hb, j), dE], [(hb, j)], scale=dE[:, j, NS + b:NS + b + 1])
                            STT("dve", hb[:, j, :], pm[:, (gg * 2) * 128:(gg * 2 + 1) * 128], dtx[:, j, b:b + 1], hb[:, j, :],
                                ALU.mult, ALU.add, [pm, dtx, (hb, j)], [(hb, j)])
                            STT("dve", dumf[:, 0:128], hb[:, j, :], 1.0, pm[:, (gg * 2 + 1) * 128:(gg * 2 + 2) * 128],
                                ALU.mult, ALU.mult, [(hb, j), pm], [(yT_s, j)], accum=yT_s[:, j, b:b + 1])
                DMA("sp", hb.name, sss_d[b].rearrange("(j two) p n -> (two p) j n", two=2), hb[:], [hb], ())
            CP("act", oS[:], STAT[:, 0:16 * NS], [STAT], [oS])
            TTo("dve", tmpS[:], oS[:], oS[:], ALU.mult, [oS], [tmpS])
            MM(STAT[:, 0:16 * NS], C("ones"), tmpS[:], [cst, tmpS], [STAT])
            rsqrt(tmpS[:], STAT[:, 0:16 * NS], [STAT], [tmpS], 1.0 / 128)
            TTo("dve", oS[:], oS[:], tmpS[:], ALU.mult, [oS, tmpS], [oS])
            TTo("dve", oS[:], oS[:], sgT_s[:].rearrange("p h b -> p (h b)"), ALU.mult, [oS, sgT_s], [oS])
            TTo("dve", oT[:, 0:16, ST:NC_], oS[:].rearrange("p (h b) -> p h b", h=16),
                col("hn").unsqueeze(2).to_broadcast([128, 16, NS]), ALU.mult, [oS, colv], [(oT, range(0, 16))])
            TTo("dve", oS[:].rearrange("p (h b) -> p h b", h=16), xbc_s[:, 0:16, :],
                col("dskE").unsqueeze(2).to_broadcast([128, 16, NS]), ALU.mult, [xbc_s, colv], [oS])
            TTo("dve", oS[:], oS[:], yT_s[:].rearrange("p h b -> p (h b)"), ALU.add, [oS, yT_s], [oS])
            TTo("dve", oS[:], oS[:], szT_s[:].rearrange("p h b -> p (h b)"), ALU.mult, [oS, szT_s], [oS])
            TTo("dve", tmpS[:], oS[:], oS[:], ALU.mult, [oS], [tmpS])
            MM(STAT[:, 0:16 * NS], C("ones"), tmpS[:], [cst, tmpS], [STAT])
            t4 = tmpS[:].rearrange("p (g t b) -> p g t b", g=8, t=2)
            s4 = STAT[:, 0:16 * NS].rearrange("p (g t b) -> p g t b", g=8, t=2)
            CP("act", tmpS[:], STAT[:, 0:16 * NS], [STAT], [tmpS])
            TTo("dve", t4[:, :, 0, :], t4[:, :, 0, :], t4[:, :, 1, :], ALU.add, [tmpS], [tmpS])
            CP("dve", t4[:, :, 1, :], t4[:, :, 0, :], [tmpS], [tmpS])
            rsqrt(tmpS[:], tmpS[:], [tmpS], [tmpS], 1.0 / 256)
            TTo("dve", oS[:], oS[:], tmpS[:], ALU.mult, [oS, tmpS], [oS])
            TTo("dve", oT[:, 16:32, ST:NC_], oS[:].rearrange("p (h b) -> p h b", h=16),
                col("sn").unsqueeze(2).to_broadcast([128, 16, NS]), ALU.mult, [oS, colv], [(oT, range(16, 32))])
            S.barrier()


        def prefix_pass():
            NT8 = SEQ_HALF // 128
            S.barrier()
            hTp = Buf("hTp", XM[:].bitcast(BF16)[:, 0:NKC * SEQ_HALF].rearrange("p (a b) -> p a b", a=NKC), 1)
            off = [0]

            def pbuf(name, shape, dt=F32, np_=1):
                n = 1
                for d_ in shape[1:]:
                    n *= d_
                nb = (n * (4 if dt == F32 else 2) + 31) // 32 * 32
                o = off[0]
                off[0] += nb
                assert off[0] <= RA_BYTES, ("RA overflow (prefix)", name, off[0], RA_BYTES)
                v = RA[:, o // 2:(o + nb) // 2]
                if dt == F32:
                    v = v.bitcast(F32)
                v = v[:, 0:n]
                if len(shape) == 3:
                    v = v.rearrange("p (a b) -> p a b", a=shape[1])
                elif len(shape) == 4:
                    v = v.rearrange("p (a b c) -> p a b c", a=shape[1], b=shape[2])
                return Buf("pp_" + name, v, np_)
            p_dtt = pbuf("dtt", [128, NT8, 32], F32, NT8); p_dtA = pbuf("dtA", [128, NT8, 32], F32, NT8)
            p_cum = pbuf("cum", [128, NT8, 32], F32, NT8); p_Eend = pbuf("Eend", [128, NT8, 32], F32, NT8)
            p_wend = pbuf("wend", [128, NT8, 32], F32, NT8)
            p_lf = [pbuf("lf%d" % i, [128, WB]) for i in range(2)]
            p_kk = [pbuf("kk%d" % i, [128, WB]) for i in range(2)]
            p_ex = [pbuf("ex%d" % i, [128, 2, WB], F32, 2) for i in range(2)]
            p_ke = pbuf("ke", [128, NT8, WB], BF16, NT8)
            p_ec = pbuf("ec", [128, NT8, HU, 2], F32, NT8)
            p_vv = [pbuf("vv%d" % i, [128, WB], BF16) for i in range(2)]
            p_raw = pbuf("raw", [128, 3 + SEQ_HALF])
            p_acc = pbuf("acc", [128, SEQ_HALF])
            p_xa = pbuf("xa", [128, 3, SEQ_HALF], BF16, 3)
            p_xst = [pbuf("xst%d" % i, [128, 256], BF16) for i in range(2)]
            p_Bt = [pbuf("Bt%d" % i, [128, 128], BF16) for i in range(2)]
            p_xw = [pbuf("xw%d" % i, [128, 256], BF16) for i in range(2)]

            for t8 in range(NT8):
                xi = xcnt[0] % 2
                xcnt[0] += 1
                xt, xn = xtile[xi], xnb[xi]
                DMA("sp", xt.name, xt[:], xpre_d[t8 * 128:(t8 + 1) * 128, :], (), [xt])
                for q4 in range(4):
                    ACT(junk[:, :], xt[:, q4 * 512:(q4 + 1) * 512], AF.Square, [xt], [st4], accum=st4[:, q4:q4 + 1])
                S.op("dve", lambda e: e.tensor_reduce(st4[:, 4:5], st4[:, 0:4], AX.X, ALU.add), [st4], [st4])
                rsqrt(st4[:, 5:6], st4[:, 4:5], [st4], [st4], 1.0 / D)
                TS("dve", xn[:], xt[:], st4[:, 5:6], None, ALU.mult, None, [xt, st4], [xn])
                for k4 in range(4):
                    ph = HR.next()
                    pv = ph[:].bitcast(BF16)
                    for j in range(4):
                        kc = k4 * 4 + j
                        TR(pv[:, j * 128:(j + 1) * 128], xn[:, kc * 128:(kc + 1) * 128], identb[:], [xn, identb], [ph])
                    for j in range(4):
                        kc = k4 * 4 + j
                        if j % 2 == 0:
                            TS("dve", hTp[:, kc, t8 * 128:(t8 + 1) * 128], pv[:, j * 128:(j + 1) * 128], G1[:, kc, 0:1],
                               SH1[:, kc, 0:1], ALU.mult, ALU.add, [ph, G1, SH1b], [hTp])
                        else:
                            ACT(hTp[:, kc, t8 * 128:(t8 + 1) * 128], pv[:, j * 128:(j + 1) * 128], AF.Identity, [ph, G1, SH1b], [hTp],
                                bias=SH1[:, kc, 0:1], scale=G1[:, kc, 0:1])
            for t8 in range(NT8):
                pq = QR.next()
                for kc in range(NKC):
                    MM(pq[:, 0:32], hTp[:, kc, t8 * 128:(t8 + 1) * 128], wdt[:, kc, :], [hTp, wdt], [pq],
                       start=(kc == 0), stop=(kc == NKC - 1))
                TTo("dve", p_dtt[:, t8, :], pq[:, 0:32], rowb[:, 0:32], ALU.add, [pq, rowb], [(p_dtt, t8)])
                ACT(p_dtt[:, t8, :], p_dtt[:, t8, :], AF.Exp, [(p_dtt, t8)], [(p_dtt, t8)])
                ACT(p_dtt[:, t8, :], p_dtt[:, t8, :], AF.Ln, [(p_dtt, t8)], [(p_dtt, t8)], bias=1.0)
                TTo("dve", p_dtA[:, t8, :], p_dtt[:, t8, :], rowb[:, 32:64], ALU.mult, [(p_dtt, t8), rowb], [(p_dtA, t8)])
                p1 = QR.next(); p2 = QR.next()
                MM(p1[:, 0:32], C("U"), p_dtA[:, t8, :], [cst, (p_dtA, t8)], [p1])
                MM(p2[:, 0:32], C("ones"), p_dtA[:, t8, :], [cst, (p_dtA, t8)], [p2])
                CP("dve", p_cum[:, t8, :], p1[:, 0:32], [p1], [(p_cum, t8)])
                ACT(p_Eend[:, t8, :], p2[:, 0:32], AF.Exp, [p2], [(p_Eend, t8)])
                TTo("dve", p_wend[:, t8, :], p2[:, 0:32], p_cum[:, t8, :], ALU.subtract, [p2, (p_cum, t8)], [(p_wend, t8)])
                ACT(p_wend[:, t8, :], p_wend[:, t8, :], AF.Exp, [(p_wend, t8)], [(p_wend, t8)])
                TTo("dve", p_wend[:, t8, :], p_wend[:, t8, :], p_dtt[:, t8, :], ALU.mult, [(p_wend, t8), (p_dtt, t8)],
                    [(p_wend, t8)])

            def tokB(wsl, t8, nco):
                pm = MMR.next()
                for kc in range(NKC):
                    MM(pm[:, 0:nco], hTp[:, kc, t8 * 128:(t8 + 1) * 128], wsl[:, kc, 0:nco], [hTp, wsl], [pm],
                       start=(kc == 0), stop=(kc == NKC - 1))
                return pm

            def gh(hb):
                cs = slice(hb * WB, (hb + 1) * WB)
                wsl = W.take(w_in_d, 0, [(1 * D + hb * WB, WB)], hold=True)

                def stA(t8):
                    lf, kk = p_lf[t8 % 2], p_kk[t8 % 2]
                    pm = tokB(wsl, t8, WB)
                    ACT(lf[:], pm[:, 0:WB], AF.Sigmoid, [pm], [lf], scale=-1.0)
                    TTo("dve", kk[:], lf[:], oml_b[:, cs], ALU.mult, [lf, oml_b], [kk])
                    ACT(lf[:], kk[:], AF.Ln, [kk], [lf], bias=1.0, scale=-1.0)
                    if t8 == NT8 - 1:
                        W.release(wsl)

                def stB(t8):
                    lf, kk, ex = p_lf[t8 % 2], p_kk[t8 % 2], p_ex[t8 % 2]
                    pa = HR.next(); pb = HR.next(); pc = QR.next()
                    MM(pa[:, 0:WB], C("triP"), lf[:], [cst, lf], [pa])
                    MM(pb[:, 0:WB], C("upper"), lf[:], [cst, lf], [pb])
                    for h in range(HU):
                        MM(pc[:, 2 * h:2 * h + 2], lf[:, h * 128:(h + 1) * 128], hsel, [cst, lf], [pc])
                    ACT(p_ec[:, t8, :, :].rearrange("p h c -> p (h c)"), pc[:, 0:2 * HU], AF.Exp, [pc], [(p_ec, t8)])
                    ACT(ex[:, 0, :], pa[:, 0:WB], AF.Exp, [pa], [(ex, 0)], scale=-1.0)
                    ACT(ex[:, 1, :], pb[:, 0:WB], AF.Exp, [pb], [(ex, 1)])
                    TTo("dve", ex[:, 1, :], ex[:, 1, :], ex[:, 0, :], ALU.mult, [(ex, 0), (ex, 1)], [(ex, 1)])
                    TTo("dve", p_ke[:, t8, :], kk[:], ex[:, 1, :], ALU.mult, [kk, (ex, 1)], [(p_ke, t8)])
                stA(0)
                yield
                for t8 in range(NT8):
                    if t8 + 1 < NT8:
                        stA(t8 + 1)
                        yield
                    stB(t8)
                    yield
                wsl2 = W.take(w_in_d, 0, [(2 * D + hb * WB, WB)], hold=True)

                def stAi(t8):
                    vv = p_vv[t8 % 2]
                    pm = tokB(wsl2, t8, WB)
                    CP("act", vv[:], pm[:, 0:WB], [pm], [vv])
                    if t8 == NT8 - 1:
                        W.release(wsl2)

                def stBi(t8):
                    vv = p_vv[t8 % 2]
                    for h in range(HU):
                        hd = hb * HU + h
                        hc = slice(h * 128, (h + 1) * 128)
                        pd = QR.next()
                        MM(pd[:], p_ke[:, t8, hc], vv[:, hc], [(p_ke, t8), vv], [pd])
                        STT("dve", Sst[:, hd, :], Sst[:, hd, :], p_ec[:, t8, h, 1:2], pd[:], ALU.mult, ALU.add,
                            [(Sst, hd), (p_ec, t8), pd], [(Sst, hd)])
                stAi(0)
                yield
                for t8 in range(NT8):
                    if t8 + 1 < NT8:
                        stAi(t8 + 1)
                        yield
                    stBi(t8)
                    yield

            def gs(g):
                hs = slice(g * 4, g * 4 + 4)

                def hb3(ap):
                    return ap.unsqueeze(2).to_broadcast([128, 4, 64])
                for blk in range(2):
                    if blk == 0:
                        wsl = W.take(w_in_d, 0, [(5 * D + g * 256, 256)], hold=True)
                    else:
                        wsl = W.take(w_in_d, 0, [(5 * D + D + g * 128, 128), (5 * D + D + 1024 + g * 128, 128)], hold=True)
                    for m in range(2):
                        chid = (g * 2 + m) if blk == 0 else (16 + g if m == 0 else 24 + g)
                        isC = (blk == 1 and m == 1)
                        for hf in ((1,) if isC else (0, 1)):
                            pm = MMR.next()
                            for kc in range(NKC):
                                MM(pm[:, 0:512], wsl[:, kc, m * 128:(m + 1) * 128], hTp[:, kc, hf * 512:(hf + 1) * 512],
                                   [hTp, wsl], [pm], start=(kc == 0), stop=(kc == NKC - 1))
                            CP("act", p_raw[:, 3 + hf * 512:3 + (hf + 1) * 512], pm[:, 0:512], [pm], [p_raw])
                        if m == 1:
                            W.release(wsl)
                        if isC:
                            CP("dve", halo[:, chid, :], p_raw[:, SEQ_HALF:SEQ_HALF + 3], [p_raw], [(halo, chid)])
                            yield
                            continue
                        CP("dve", p_raw[:, 0:3], halo[:, chid, :], [(halo, chid)], [p_raw])
                        CP("dve", halo[:, chid, :], p_raw[:, SEQ_HALF:SEQ_HALF + 3], [p_raw], [(halo, chid)])
                        cw = lambda i, chid=chid: col("cw", chid * 4 + i, 1)
                        TS("dve", p_acc[:], p_raw[:, 0:SEQ_HALF], cw(0), col("cb", chid, 1), ALU.mult, ALU.add, [p_raw, colv], [p_acc])
                        for i in (1, 2, 3):
                            STT("dve", p_acc[:], p_raw[:, i:i + SEQ_HALF], cw(i), p_acc[:], ALU.mult, ALU.add,
                                [p_raw, p_acc, colv], [p_acc])
                        ci = m if blk == 0 else 2
                        ACT(p_xa[:, ci, :], p_acc[:], AF.Silu, [p_acc], [(p_xa, ci)])
                        yield
                def stT(t8):
                    tc_ = slice(t8 * 128, (t8 + 1) * 128)
                    xst, Bt, xw = p_xst[t8 % 2], p_Bt[t8 % 2], p_xw[t8 % 2]
                    ph = HR.next()
                    pv = ph[:].bitcast(BF16)
                    for m in range(3):
                        TR(pv[:, m * 128:(m + 1) * 128], p_xa[:, m, tc_], identb[:], [(p_xa, m), identb], [ph])
                    CP("act", xst[:], pv[:, 0:256], [ph], [xst])
                    CP("dve", Bt[:], pv[:, 256:384], [ph], [Bt])
                    TTo("dve", xw[:].rearrange("p (h c) -> p h c", h=4), xst[:].rearrange("p (h c) -> p h c", h=4),
                        hb3(p_wend[:, t8, hs]), ALU.mult, [xst, (p_wend, t8)], [xw])

                def stP(t8):
                    Bt, xw = p_Bt[t8 % 2], p_xw[t8 % 2]
                    pdl = HR.next()
                    MM(pdl[:, 0:256], Bt[:], xw[:], [Bt, xw], [pdl])
                    h3 = hst[:, g * 256:(g + 1) * 256].rearrange("p (h c) -> p h c", h=4)
                    TTo("dve", h3, h3, hb3(p_Eend[:, t8, hs]), ALU.mult, [(hst, g), (p_Eend, t8)], [(hst, g)])
                    TTo("dve", hst[:, g * 256:(g + 1) * 256], hst[:, g * 256:(g + 1) * 256], pdl[:, 0:256], ALU.add,
                        [(hst, g), pdl], [(hst, g)])
                stT(0)
                yield
                for t8 in range(NT8):
                    if t8 + 1 < NT8:
                        stT(t8 + 1)
                        yield
                    stP(t8)
                    yield

            stepc = 0
            ga = gada()
            for u_ in range(8):
                gens = [gh(u_), gs(u_)]
                while gens:
                    for gen in list(gens):
                        try:
                            next(gen)
                        except StopIteration:
                            gens.remove(gen)
                        stepc += 1
                        if stepc % 6 == 0:
                            W.preconvert()
                        if stepc % 12 == 0:
                            next(ga, None)
            for _ in ga:
                pass
            S.barrier()

        DMA("sp", "rowb", rowb[:], rows3_d.partition_broadcast(128), (), [rowb])
        ACT(rowb[:, 32:64], rowb[:, 32:64], AF.Exp, [rowb], [rowb])
        TS("dve", rowb[:, 32:64], rowb[:, 32:64], -1.0, None, ALU.mult, None, [rowb], [rowb])
        ACT(hcol[:, 1:2], hcol[:, 1:2], AF.Exp, [hcol], [hcol])
        TS("dve", hcol[:, 1:2], hcol[:, 1:2], -1.0, None, ALU.mult, None, [hcol], [hcol])

        if stop == "p0":
            nst = npre = 0
        if npre:
            prefix_pass()
        else:
            for _ in gada():
                pass
        ada_finish()
        if npre:
            for hd in range(16):
                TS("dve", Sst[:, hd, :], Sst[:, hd, :], flag[:, 0:1], None, ALU.mult, None, [(Sst, hd), flag], [(Sst, hd)])
            for g in range(8):
                gsl = slice(g * 256, (g + 1) * 256)
                TS("dve", hst[:, gsl], hst[:, gsl], flag[:, 0:1], None, ALU.mult, None, [(hst, g), flag], [(hst, g)])
                CP("act", hstb[:, gsl], hst[:, gsl], [(hst, g)], [(hstb, g)])
            TS("dve", halo[:], halo[:], flag[:, 0:1], None, ALU.mult, None, [halo, flag], [halo])
        for s_ in range(nst):
            out_rows[0] = yp_d[s_ * ST:(s_ + 1) * ST, :]
            supertile(xp_d[s_ * ST:(s_ + 1) * ST, :], False, (s_ == 0) and do_samples, s_ == nst - 1, preconv=(s_ == 0))

        DMA("sp", "hgp", hgp_d.rearrange("h k v -> k h v"), Sst[:], [Sst], ())
        sso = xtile[0]
        for j in range(16):
            pq = QR.next()
            TR(pq[:], hst[:, j * 128:(j + 1) * 128], C("ident"), [(hst, j // 2), cst], [pq])
            CP("act", sso[:, j * 128:(j + 1) * 128], pq[:], [pq], [sso])
        DMA("sp", "xtile0", ssp_d.rearrange("(j two) p n -> (two p) j n", two=2),
            sso[:].rearrange("p (j n) -> p j n", j=16), [sso], ())
        cvo = xtile[1]
        for hf in range(2):
            for k4 in range(4):
                pm = MMR.next()
                for j in range(4):
                    chid = hf * 16 + k4 * 4 + j
                    TR(pm[0:3, j * 128:(j + 1) * 128], halo[:, chid, :], C("ident"), [(halo, chid), cst], [pm])
                CP("act", cvo[0:3, k4 * 512:(k4 + 1) * 512], pm[0:3, :], [pm], [cvo])
            DMA("sp", "xtile1", cvp_d[:, hf * D:(hf + 1) * D], cvo[0:3, :], [cvo], ())

        if W.record is not None:
            return W.record
        assert stop is not None or W.taken == len(W.reqs), (W.taken, len(W.reqs))
        S.emit(st)
    return nc


def build2(taps=None, **kw):
    reqs = build(bass.Bass("TRN2", target_bir_lowering=False), taps=taps, wreqs=None, **kw)
    return build(bass.Bass("TRN2", target_bir_lowering=False), taps=taps, wreqs=reqs, **kw)


def make_in_maps(inp, n_cores=8):
    f32 = lambda a: np.ascontiguousarray(np.asarray(a), dtype=np.float32)
    cst = host_consts()
    colv = np.zeros((128, NCOL), np.float32)

    def put(name, arr):
        o, w = COLS[name]
        assert arr.shape == (128, w), (name, arr.shape)
        colv[:, o:o + w] = arr
    put("g1", fm(inp["norm_pre_mix"][0], 16)); put("g2", fm(inp["norm_pre_mlp"][0], 16))
    put("gp1", fm(inp["norm_post_mix"][0], 16)); put("gp2", fm(inp["norm_post_mlp"][0], 16))
    put("bada", fm(inp["b_ada"][0], 96)); put("hn", fm(inp["hgrn_norm"][0], 16)); put("sn", fm(inp["ssd_norm"][0], 16))
    put("cb", fm(inp["conv_b"][0], 32))
    cw = np.asarray(inp["conv_w"][0], np.float32)
    put("cw", np.ascontiguousarray(cw.reshape(4, 32, 128).transpose(2, 1, 0).reshape(128, 128)))
    put("lb0", fm(inp["hgrn_lb_logits"][0], 16)); put("lb1", fm(inp["hgrn_lb_logits"][1], 16))
    put("dskE", fm(np.repeat(np.asarray(inp["d_skip"][0], np.float32), 64), 16))
    rowv = f32(np.concatenate([inp["hgrn_lb_logits"][0], inp["hgrn_lb_logits"][1]])[None, :])
    rows3 = f32(np.concatenate([inp["dt_bias"][0], inp["a_log"][0], inp["d_skip"][0]])[None, :])
    hcol = f32(np.stack([inp["dt_bias"][0], inp["a_log"][0]], axis=1))
    shared = dict(cst=cst, colv=colv, rowv=rowv, rows3=rows3, hcol=hcol,
                  w_ada=f32(inp["w_ada"][0]), w_in=f32(inp["w_in"][0]), w_out=f32(inp["w_out"][0]),
                  w_up=f32(inp["w_up"][0]), w_down=f32(inp["w_down"][0]))
    maps = []
    for c in range(n_cores):
        b, half = c // 2, c % 2
        sl = slice(c * NS, (c + 1) * NS)
        m = dict(shared)
        m["xp"] = f32(inp["x_prompt"][b, half * SEQ_HALF:(half + 1) * SEQ_HALF])
        m["xpre"] = f32(inp["x_prompt"][b, 0:SEQ_HALF])
        m["xs"] = f32(inp["x_sample"][sl, 0])
        m["c17"] = f32(np.concatenate([inp["c_prompt"][b:b + 1], inp["c_sample"][sl]], axis=0))
        m["flag"] = np.full((128, 1), float(half), np.float32)
        m["shg"] = f32(inp["state_hgrn"][0, sl])
        m["ssm"] = f32(inp["state_ssm"][0, sl])
        m["scv"] = f32(np.asarray(inp["state_conv"][0, sl]).reshape(NS * 3, 4096))
        maps.append(m)
    return maps


_NC_CACHE = {}


def kernel(**inputs):
    inp = {k: np.asarray(v) for k, v in inputs.items()}
    if "nc" not in _NC_CACHE:
        _NC_CACHE["nc"] = build2()
    nc = _NC_CACHE["nc"]
    maps = make_in_maps(inp)
    res = run_bass_kernel_spmd(nc, maps, core_ids=list(range(8)))
    r = res.results
    yp = np.stack([np.concatenate([r[2 * b]["yp"], r[2 * b + 1]["yp"]], axis=0) for b in range(4)])
    ys = np.concatenate([r[c]["ys"] for c in range(8)], axis=0)[:, None, :]
    hgp = np.stack([r[2 * b + 1]["hgp"] for b in range(4)])[None]
    ssp = np.stack([r[2 * b + 1]["ssp"] for b in range(4)])[None]
    cvp = np.stack([r[2 * b + 1]["cvp"] for b in range(4)])[None]
    hgs = np.concatenate([r[c]["hgs"] for c in range(8)], axis=0)[None]
    sss = np.concatenate([r[c]["sss"] for c in range(8)], axis=0)[None]
    cvs = np.concatenate([r[c]["cvs"] for c in range(8)], axis=0)[None]
    return tuple(np.ascontiguousarray(a, dtype=np.float32) for a in (yp, ys, hgp, ssp, cvp, hgs, sss, cvs))
```

```python
import os
from contextlib import ExitStack
import numpy as np
import concourse.bass as bass
import concourse.mybir as mybir
from concourse.bass_utils import run_bass_kernel_spmd

F32 = mybir.dt.float32
BF16 = mybir.dt.bfloat16
AF = mybir.ActivationFunctionType
ALU = mybir.AluOpType
AX = mybir.AxisListType

D = 2048
NKC = 16
SEQ_HALF = 1024
NS = 16
ST = 256
TT = ST // 128
WB = 256
NW = 4
EPS = 1e-6
NST = SEQ_HALF // ST
NEG = -1.0e5


class Buf:
    def __init__(self, name, t, nparts=1, share=None, excl=False):
        self.name = name
        self.t = t
        self.nparts = nparts
        self.excl = excl
        if share is not None:
            self.nparts = share.nparts
            self.excl = share.excl
            self.last_w = share.last_w
            self.readers = share.readers
        else:
            self.last_w = [None] * nparts
            self.readers = [[] for _ in range(nparts)]

    def __getitem__(self, k):
        return self.t[k]


def _parts(acc):
    if isinstance(acc, Buf):
        return acc, range(acc.nparts)
    b, p = acc
    if p is None:
        return b, range(b.nparts)
    if isinstance(p, int):
        return b, (p,)
    return b, tuple(p)


class Sched:
    ENGS = ("pe", "act", "dve", "pool", "sp")

    def __init__(self, nc):
        self.nc = nc
        self.ops = {e: [] for e in self.ENGS}
        self.waited = {e: {} for e in self.ENGS}
        self.dma_count = {}
        self.pending = {e: [] for e in self.ENGS}

    def barrier(self):
        tgt = []
        for e in self.ENGS:
            for i in range(len(self.ops[e]) - 1, -1, -1):
                if self.ops[e][i][2] is None:
                    tgt.append(("eng", e, i))
                    break
        for c, n in self.dma_count.items():
            tgt.append(("dma", c, n))
        for e in self.ENGS:
            self.pending[e] = list(tgt)

    def op(self, eng, fn, reads=(), writes=(), dma=None):
        idx = len(self.ops[eng])
        deps = self.pending[eng]
        self.pending[eng] = []
        for acc in reads:
            b, ps = _parts(acc)
            for p in ps:
                if b.last_w[p] is not None:
                    deps.append(b.last_w[p])
                if b.excl:
                    deps.extend(r for r in b.readers[p] if not (r[0] == "eng" and r[1] == eng))
        for acc in writes:
            b, ps = _parts(acc)
            for p in ps:
                if b.last_w[p] is not None:
                    deps.append(b.last_w[p])
                deps.extend(b.readers[p])
        if dma is not None:
            n = self.dma_count.get(dma, 0)
            if n > 0:
                deps.append(("dma", dma, n))
            self.dma_count[dma] = n + 1
            me = ("dma", dma, n + 1)
        else:
            me = ("eng", eng, idx)
        waits = {}
        wd = self.waited[eng]
        for kind, src, val in deps:
            if kind == "eng" and src == eng and eng == "pe":
                continue
            key = (kind, src)
            if wd.get(key, -1) >= val:
                continue
            if waits.get(key, -1) < val:
                waits[key] = val
        for key, val in waits.items():
            wd[key] = val
        self.ops[eng].append([fn, waits, dma, False])
        for acc in reads:
            b, ps = _parts(acc)
            for p in ps:
                b.readers[p].append(me)
        for acc in writes:
            b, ps = _parts(acc)
            for p in ps:
                b.last_w[p] = me
                b.readers[p] = []
        return me

    def emit(self, stack):
        nc = self.nc
        for e in self.ENGS:
            for rec in self.ops[e]:
                for (kind, src), val in rec[1].items():
                    if kind == "eng":
                        self.ops[src][val][3] = True
        rank = {}
        for e in self.ENGS:
            r = 0
            rk = []
            for rec in self.ops[e]:
                if rec[2] is None and rec[3]:
                    r += 1
                rk.append(r)
            rank[e] = rk
        esem = {e: stack.enter_context(nc.semaphore("s_" + e)) for e in self.ENGS}
        dsem = {c: stack.enter_context(nc.semaphore("d_%s" % (c,))) for c in self.dma_count}
        final_waits = dict(self.dma_count)
        block = stack.enter_context(nc.Block())
        sched = self

        def body(ename):
            def run(eng):
                for fn, waits, dma, sig in sched.ops[ename]:
                    for (kind, src), val in waits.items():
                        if kind == "eng":
                            eng.wait_ge(esem[src], rank[src][val])
                        else:
                            eng.wait_ge(dsem[src], 16 * val)
                    ins = fn(eng)
                    if dma is not None:
                        ins.then_inc(dsem[dma], 16)
                    elif sig:
                        ins.then_inc(esem[ename], 1)
                if ename == "sp":
                    for c, n in final_waits.items():
                        eng.wait_ge(dsem[c], 16 * n)
            return run

        block.tensor(body("pe"))
        block.scalar(body("act"))
        block.vector(body("dve"))
        block.gpsimd(body("pool"))
        block.sync(body("sp"))


CST_NAMES = ["ident", "triP", "U", "ones", "upper", "cmask", "negmask"]


def host_consts():
    s = np.arange(128)[:, None]
    t = np.arange(128)[None, :]
    c = {}
    c["ident"] = (s == t)
    c["triP"] = (s <= t).astype(np.float32) - (s <= 63).astype(np.float32) * np.ones_like(t)
    c["U"] = (s <= t)
    c["ones"] = np.ones((128, 128))
    c["upper"] = (s > 63) * np.ones_like(t)
    c["cmask"] = (t >= s)
    c["negmask"] = np.where(t >= s, 0.0, NEG)
    arr = np.concatenate([np.asarray(c[n], np.float32) for n in CST_NAMES], axis=1)
    hsel = np.stack([(np.arange(128) <= 63).astype(np.float32), np.ones(128, np.float32)], axis=1)
    return np.ascontiguousarray(np.concatenate([arr, hsel], axis=1), dtype=np.float32)


NCST = len(CST_NAMES) * 128 + 2

COLS = {}
_o = 0
for _n, _w in [("g1", 16), ("g2", 16), ("gp1", 16), ("gp2", 16), ("bada", 96), ("hn", 16), ("sn", 16),
               ("cb", 32), ("cw", 128), ("lb0", 16), ("lb1", 16), ("dskE", 16)]:
    COLS[_n] = (_o, _w)
    _o += _w
NCOL = _o


def fm(v, nch):
    return np.ascontiguousarray(np.asarray(v, np.float32).reshape(nch, 128).T)


def build(nc, taps=None, nst=NST, npre=NST, do_samples=True, stop=None, wreqs=None):
    taps = taps or []
    dt_in = lambda name, shape: nc.dram_tensor(name, shape, F32, kind="ExternalInput").ap()
    dt_out = lambda name, shape: nc.dram_tensor(name, shape, F32, kind="ExternalOutput").ap()
    xp_d = dt_in("xp", [SEQ_HALF, D])
    xpre_d = dt_in("xpre", [SEQ_HALF, D])
    xs_d = dt_in("xs", [NS, D])
    c17_d = dt_in("c17", [NS + 1, D])
    flag_d = dt_in("flag", [128, 1])
    shg_d = dt_in("shg", [NS, 16, 128, 128])
    ssm_d = dt_in("ssm", [NS, 32, 64, 128])
    scv_d = dt_in("scv", [NS * 3, 4096])
    cst_d = dt_in("cst", [128, NCST])
    colv_d = dt_in("colv", [128, NCOL])
    rowv_d = dt_in("rowv", [1, 2 * D])
    hcol_d = dt_in("hcol", [32, 2])
    rows3_d = dt_in("rows3", [1, 96])
    w_ada_d = dt_in("w_ada", [D, 6 * D])
    w_in_d = dt_in("w_in", [D, 14368])
    w_out_d = dt_in("w_out", [2 * D, D])
    w_up_d = dt_in("w_up", [D, 4 * D])
    w_down_d = dt_in("w_down", [4 * D, D])
    yp_d = dt_out("yp", [SEQ_HALF, D])
    ys_d = dt_out("ys", [NS, D])
    hgp_d = dt_out("hgp", [16, 128, 128])
    ssp_d = dt_out("ssp", [32, 64, 128])
    cvp_d = dt_out("cvp", [3, 4096])
    hgs_d = dt_out("hgs", [NS, 16, 128, 128])
    sss_d = dt_out("sss", [NS, 32, 64, 128])
    cvs_d = dt_out("cvs", [NS, 3, 4096])
    scr_d = nc.dram_tensor("scr", [16, 2, 2 * NS], F32).ap()
    tap_d = {t[0]: nc.dram_tensor("tap_" + t[0], t[1], BF16 if (len(t) > 2 and t[2] == "bf16") else F32,
                                  kind="ExternalOutput").ap() for t in taps}

    S = Sched(nc)
    st = ExitStack()
    with st:
        def sb(name, shape, dt=F32, nparts=1):
            return Buf(name, st.enter_context(nc.sbuf_tensor("sb_" + name, shape, dt)), nparts)

        def MM(out, lhsT, rhs, R, W, start=True, stop=True):
            S.op("pe", lambda e: e.matmul(out, lhsT, rhs, start=start, stop=stop), R, W)

        def TR(out, in_, idn, R, W):
            S.op("pe", lambda e: e.transpose(out, in_, idn), R, W)

        def ACT(out, in_, func, R, W, bias=None, scale=None, accum=None):
            kw = {}
            if bias is not None:
                kw["bias"] = bias
            if scale is not None:
                kw["scale"] = scale
            if accum is not None:
                kw["accum_out"] = accum
            S.op("act", lambda e: e.activation(out, in_, func, **kw), R, W)

        def TS(eng, out, in0, s1, s2, op0, op1, R, W):
            if s2 is None:
                S.op(eng, lambda e: e.tensor_scalar(out, in0, s1, None, op0), R, W)
            else:
                S.op(eng, lambda e: e.tensor_scalar(out, in0, s1, s2, op0, op1), R, W)

        def TTo(eng, out, in0, in1, op, R, W):
            S.op(eng, lambda e: e.tensor_tensor(out, in0, in1, op), R, W)

        def STT(eng, out, in0, sc, in1, op0, op1, R, W, accum=None):
            if accum is None:
                S.op(eng, lambda e: e.scalar_tensor_tensor(out, in0, sc, in1, op0, op1), R, W)
            else:
                S.op(eng, lambda e: e.scalar_tensor_tensor(out, in0, sc, in1, op0, op1, accum_out=accum), R, W)

        def CP(eng, out, in_, R, W):
            if eng == "act":
                S.op("act", lambda e: e.copy(out, in_), R, W)
            else:
                S.op(eng, lambda e: e.tensor_copy(out, in_), R, W)

        def MSET(eng, ap, val, W):
            S.op(eng, lambda e: e.memset(ap, val), (), W)

        def RECIP(out, in_, R, W):
            S.op("dve", lambda e: e.reciprocal(out, in_), R, W)

        def DMA(eng, chan, out, in_, R, W):
            S.op(eng, lambda e: e.dma_start(out=out, in_=in_), R, W, dma=chan)

        def TAP(name, ap, R):
            if name in tap_d:
                DMA("sp", "tap_" + name, tap_d[name], ap, R, ())

        def rsqrt(out, in_, R, W, scale):
            ACT(out, in_, AF.Ln, R, W, bias=epsc[0:out.shape[0], 0:1], scale=scale)
            ACT(out, out, AF.Exp, W, W, scale=-0.5)

        class Ring:
            def __init__(self, bufs):
                self.bufs = bufs
                self.i = 0

            def next(self):
                b = self.bufs[self.i % len(self.bufs)]
                self.i += 1
                return b

        psb = [st.enter_context(nc.psum_tensor("psb%d" % i, [128, 512], F32)) for i in range(8)]
        bank = [Buf("bank%d" % i, psb[i], excl=True) for i in range(8)]
        MMR = Ring([Buf("pmm%d" % i, psb[i], share=bank[i]) for i in range(3)])
        HR = Ring([Buf("ph%d" % i, psb[3 + i % 2][:, (i // 2) * 256:(i // 2) * 256 + 256], share=bank[3 + i % 2])
                   for i in range(3)])
        PO = Buf("ppo", psb[4][:, 256:512], share=bank[4])
        QR = Ring([Buf("pq%d" % i, psb[5 + i % 2][:, (i // 2) * 128:(i // 2 + 1) * 128], share=bank[5 + i % 2])
                   for i in range(4)])
        STAT = Buf("pstat", psb[7], share=bank[7])

        cst = sb("cst", [128, NCST])
        colv = sb("colv", [128, NCOL])
        flag = sb("flag", [128, 1])
        epsc = sb("epsc", [128, 1])
        identb = sb("identb", [128, 128], BF16)
        DMA("sp", "cst", cst[:], cst_d, (), [cst])
        DMA("sp", "colv", colv[:], colv_d, (), [colv])
        DMA("sp", "flag", flag[:], flag_d, (), [flag])
        MSET("dve", epsc[:], EPS, [epsc])

        def C(name):
            i = CST_NAMES.index(name)
            return cst[:, i * 128:(i + 1) * 128]
        hsel = cst[:, len(CST_NAMES) * 128:len(CST_NAMES) * 128 + 2]
        CP("dve", identb[:], C("ident"), [cst], [identb])

        def col(name, j0=0, n=None):
            o, w = COLS[name]
            n = w - j0 if n is None else n
            return colv[:, o + j0:o + j0 + n]

        oml_b = sb("oml_b", [128, D])
        omlc = sb("omlc", [128, 16])
        lbc = sb("lbc", [128, 16])
        xtile = [sb("xtile%d" % i, [128, D]) for i in range(1)] * 2
        xnb = [sb("xnb%d" % i, [128, D], BF16) for i in range(1)] * 2
        DMA("sp", "oml_b", oml_b[:], rowv_d[:, D:2 * D].partition_broadcast(128), (), [oml_b])
        DMA("sp", "xtile0", xtile[0][:], rowv_d[:, 0:D].partition_broadcast(128), (), [xtile[0]])
        TTo("dve", oml_b[:], oml_b[:], xtile[0][:], ALU.subtract, [oml_b, xtile[0]], [oml_b])
        ACT(oml_b[:], oml_b[:], AF.Sigmoid, [oml_b], [oml_b])
        TTo("dve", omlc[:], col("lb1"), col("lb0"), ALU.subtract, [colv], [omlc])
        ACT(omlc[:], omlc[:], AF.Sigmoid, [omlc], [omlc])
        TTo("dve", lbc[:], col("lb0"), col("lb1"), ALU.subtract, [colv], [lbc])
        ACT(lbc[:], lbc[:], AF.Sigmoid, [lbc], [lbc])
        hcol = sb("hcol", [32, 2])
        DMA("sp", "hcol", hcol[:], hcol_d, (), [hcol])

        wslots = [sb("w%d" % i, [128, NKC, WB], BF16) for i in range(NW)]

        def w_requests():
            req = []
            for cbk in range(6 * D // WB):
                req.append((w_ada_d, 0, [(cbk * WB, WB)]))

            def inproj(state_only):
                r = []
                for hb in range(D // WB):
                    for gi, gname in enumerate("qfig"):
                        if state_only and gname in "qg":
                            continue
                        r.append((w_in_d, 0, [(gi * D + hb * WB, WB)]))
                for g in range(8):
                    if not state_only:
                        r.append((w_in_d, 0, [(4 * D + g * 256, 256)]))
                    r.append((w_in_d, 0, [(5 * D + g * 256, 256)]))
                    r.append((w_in_d, 0, [(5 * D + D + g * 128, 128), (5 * D + D + 1024 + g * 128, 128)]))
                return r
            for _ in range(npre):
                req += inproj(True)
            for _ in range(nst):
                req += inproj(False)
                for cbk in range(D // WB):
                    for kh in range(2):
                        req.append((w_out_d, kh * D, [(cbk * WB, WB)]))
                for cbk in range(4 * D // WB):
                    req.append((w_up_d, 0, [(cbk * WB, WB)]))
                for cbk in range(D // WB):
                    for kq in range(4):
                        req.append((w_down_d, kq * D, [(cbk * WB, WB)]))
            return req

        class WStream:
            def __init__(self, reqs):
                self.issued = 0
                self.taken = 0
                self.record = [] if reqs is None else None
                self.released = set()
                self.auto = []
                self.held = {}
                self.conv_ptr = 0
                self.nconv = 0
                self.free = list(range(NW))
                self.slot_of = {}
                if reqs is None:
                    return
                wmap = {a.tensor.name: a for a in (w_ada_d, w_in_d, w_out_d, w_up_d, w_down_d)}
                reqs = [(wmap[n], r0, list(segs)) for (n, r0, segs) in reqs]
                self.reqs = reqs
                self.keys = [(id(r[0]), r[1], tuple(r[2])) for r in reqs]
                uses = {}
                for k in self.keys:
                    uses[k] = uses.get(k, 0) + 1
                self.cidx = {}
                for k in self.keys:
                    if uses[k] > 1 and k not in self.cidx:
                        self.cidx[k] = len(self.cidx)
                self.cached = set()
                ncache = max(1, len(self.cidx))
                self.cache_d = nc.dram_tensor("wcache", [ncache, 128, NKC * WB], BF16).ap()
                self.cbuf = Buf("wcache", self.cache_d, ncache)

            def _issue(self, i):
                wd, r0, segs = self.reqs[i]
                k = self.keys[i]
                slot = wslots[self.slot_of[i]]
                chan = "w%d" % self.slot_of[i]
                if k in self.cached:
                    ci = self.cidx[k]
                    DMA("pool", chan, slot[:].rearrange("p a b -> p (a b)"), self.cache_d[ci], [(self.cbuf, ci)], [slot])
                    return
                off = 0
                for c0, ncol in segs:
                    src = wd[r0:r0 + D, c0:c0 + ncol].rearrange("(kc p) c -> p kc c", p=128)
                    DMA("pool", chan, slot[:, :, off:off + ncol], src, (), [slot])
                    off += ncol
                if k in self.cidx:
                    ci = self.cidx[k]
                    DMA("sp", "wb%d" % self.slot_of[i], self.cache_d[ci], slot[:].rearrange("p a b -> p (a b)"), [slot], [(self.cbuf, ci)])
                    self.cached.add(k)

            def _pump(self, upto):
                while self.issued < min(len(self.reqs), upto) and self.free:
                    j = self.issued
                    self.slot_of[j] = self.free.pop(0)
                    self._issue(j)
                    self.issued += 1

            def take(self, wd, r0=0, segs=None, hold=False):
                i = self.taken
                if self.record is not None:
                    self.record.append((wd.tensor.name, r0, tuple(segs)))
                    self.taken += 1
                    return wslots[i % NW]
                assert i < len(self.reqs), "weight stream exhausted"
                assert self.reqs[i][0] is wd and self.reqs[i][1] == r0 and tuple(self.reqs[i][2]) == tuple(segs), \
                    ("weight stream order mismatch", i, self.reqs[i][1:], r0, segs)
                for j in self.auto:
                    self.free.append(self.slot_of[j])
                self.auto = []
                self._pump(i + NW)
                assert self.issued > i, ("no free weight slot", i)
                if hold:
                    self.held[id(wslots[self.slot_of[i]])] = i
                else:
                    self.auto.append(i)
                self.taken += 1
                return wslots[self.slot_of[i]]

            def preconvert(self):
                if self.record is not None:
                    return
                j = max(self.conv_ptr, self.issued)
                while j < len(self.reqs):
                    k = self.keys[j]
                    if k in self.cidx and k not in self.cached:
                        break
                    j += 1
                self.conv_ptr = j
                if j >= len(self.reqs):
                    return
                wd, r0, segs = self.reqs[j]
                ci = self.cidx[k]
                dst = self.cache_d[ci].rearrange("p (a b) -> p a b", a=NKC)
                off = 0
                for c0, ncol in segs:
                    src = wd[r0:r0 + D, c0:c0 + ncol].rearrange("(kc p) c -> p kc c", p=128)
                    DMA("pool", "cv%d" % (self.nconv % 4), dst[:, :, off:off + ncol], src, (), [(self.cbuf, ci)])
                    off += ncol
                self.nconv += 1
                self.cached.add(k)

            def release(self, slot):
                if self.record is not None:
                    return
                self.free.append(self.slot_of[self.held.pop(id(slot))])
                self._pump(self.taken + NW - 1)

        W = WStream(wreqs)
        wdt = sb("wdt", [128, NKC, 32], BF16)
        DMA("pool", "wdt", wdt[:], w_in_d[:, 14336:14368].rearrange("(kc p) c -> p kc c", p=128), (), [wdt])

        NC_ = ST + NS
        XM = st.enter_context(nc.sbuf_tensor("sb_XM", [128, 2 * NKC * NC_], F32))
        xT = Buf("xT", XM[:, 0:NKC * NC_].rearrange("p (a b) -> p a b", a=NKC), NKC)
        hT = sb("hT", [128, NKC, NC_], BF16)
        oT = sb("oT", [128, 32, NC_], BF16, nparts=32)
        modT = Buf("modT", oT[:].rearrange("p a b -> p (a b)")[:, 0:96 * (NS + 1) * 2].bitcast(F32)
                   .rearrange("p (a b) -> p a b", a=96), 96)
        mixT = Buf("mixT", XM[:, NKC * NC_:2 * NKC * NC_].rearrange("p (a b) -> p a b", a=NKC), NKC)
        Sst = sb("Sst", [128, 16, 128], F32, nparts=16)
        hst = sb("hst", [128, D], F32, nparts=8)
        hstb = sb("hstb", [128, D], BF16, nparts=8)
        halo = sb("halo", [128, 32, 3], F32, nparts=32)
        RA_BYTES = 64 * NC_ * 2 + 4096
        RA = st.enter_context(nc.sbuf_tensor("sb_RA", [128, RA_BYTES // 2], BF16))
        SH1b = sb("SH1b", [128, NKC, NS + 1]); SH2b = sb("SH2b", [128, NKC, NS + 1])
        MSET("dve", Sst[:], 0.0, [Sst])
        MSET("dve", hst[:], 0.0, [hst])
        MSET("dve", hstb[:], 0.0, [hstb])
        MSET("dve", halo[:], 0.0, [halo])

        ctok = xtile[0]
        ctb = xnb[0]
        scT = sb("scT", [128, NKC, NS + 1], BF16)
        DMA("sp", "xtile0", ctok[0:NS + 1, :], c17_d, (), [ctok])
        ACT(ctb[0:NS + 1, :], ctok[0:NS + 1, :], AF.Silu, [ctok], [ctb])
        for kc in range(NKC):
            ph = HR.next()
            pv = ph[:].bitcast(BF16)
            TR(pv[:, 0:NS + 1], ctb[0:NS + 1, kc * 128:(kc + 1) * 128], identb[0:NS + 1, 0:NS + 1], [ctb, identb], [ph])
            CP("dve", scT[:, kc, :], pv[:, 0:NS + 1], [ph], [scT])
        def ada_block(cbk):
            wsl = W.take(w_ada_d, 0, [(cbk * WB, WB)])
            for m in range(WB // 128):
                ch = cbk * (WB // 128) + m
                pq = QR.next()
                for kc in range(NKC):
                    MM(pq[:, 0:NS + 1], wsl[:, kc, m * 128:(m + 1) * 128], scT[:, kc, :], [wsl, scT], [pq],
                       start=(kc == 0), stop=(kc == NKC - 1))
                ACT(modT[:, ch, :], pq[:, 0:NS + 1], AF.Identity, [pq, colv], [(modT, ch)], bias=col("bada", ch, 1))
        NADA = 6 * D // WB
        NADA1 = 2 * D // WB
        for cbk in range(NADA1):
            ada_block(cbk)
        G1 = sb("G1", [128, NKC, NS + 1]); G2 = sb("G2", [128, NKC, NS + 1])
        GT1 = sb("GT1", [128, NKC, NS + 1]); GT2 = sb("GT2", [128, NKC, NS + 1])

        def bc3(ap2):
            return ap2.unsqueeze(2).to_broadcast([128, NKC, NS + 1])

        def derive_G(Gx, sci, gname):
            TS("dve", Gx[:], modT[:, sci * 16:(sci + 1) * 16, :], 1.0, None, ALU.add, None, [modT], [Gx])
            TTo("dve", Gx[:], Gx[:], bc3(col(gname)), ALU.mult, [Gx, colv], [Gx])
        derive_G(G1, 1, "g1")
        CP("act", SH1b[:], modT[:, 0:16, :], [modT], [SH1b])
        SH1 = SH1b
        SH2 = SH2b

        def gada():
            for cbk in range(NADA1, NADA):
                ada_block(cbk)
                yield

        def ada_finish():
            derive_G(G2, 4, "g2")
            for (Gx, gti, gname) in ((GT1, 2, "gp1"), (GT2, 5, "gp2")):
                TTo("dve", Gx[:], modT[:, gti * 16:(gti + 1) * 16, :], bc3(col(gname)), ALU.mult, [modT, colv], [Gx])
            CP("act", SH2b[:], modT[:, 48:64, :], [modT], [SH2b])
            TAP("modT", modT[:], [modT])
            S.barrier()

        junk = sb("junk", [128, 512], BF16)
        junkf = sb("junkf", [128, 512], F32)
        dumf = sb("dumf", [128, 128], F32)
        st4 = sb("st4", [128, 8])
        xcnt = [0]
        dtt = sb("dtt", [128, TT, 32], nparts=TT); dtA = sb("dtA", [128, TT, 32], nparts=TT)
        cum = sb("cum", [128, TT, 32], nparts=TT); expcum = sb("expcum", [128, TT, 32], nparts=TT)
        Eend = sb("Eend", [128, TT, 32], nparts=TT); wend = sb("wend", [128, TT, 32], nparts=TT)
        rowb = sb("rowb", [128, 96])


        HU = WB // 128

        ra_off = [0]

        def ubuf(name, shape, dt=F32, np_=1):
            n = 1
            for d_ in shape[1:]:
                n *= d_
            nb = n * (4 if dt == F32 else 2)
            nb = (nb + 31) // 32 * 32
            o = ra_off[0]
            ra_off[0] += nb
            assert ra_off[0] <= RA_BYTES, ("RA overflow", name, ra_off[0], RA_BYTES)
            v = RA[:, o // 2:(o + nb) // 2]
            if dt == F32:
                v = v.bitcast(F32)
            v = v[:, 0:n]
            if len(shape) == 3:
                v = v.rearrange("p (a b) -> p a b", a=shape[1])
            elif len(shape) == 4:
                v = v.rearrange("p (a b c) -> p a b c", a=shape[1], b=shape[2])
            b_ = Buf("ra_" + name, v, np_)
            return [b_, b_]
        sq_b = ubuf("sq", [128, TT, WB], F32, TT)
        k_b = ubuf("kk", [128, TT, WB], F32, TT)
        lf_b = ubuf("lf", [128, TT, WB], F32, TT)
        ex_b = ubuf("ex", [128, 3, WB], F32, 3)
        qg_b = ubuf("qg", [128, TT, WB], BF16, TT)
        kg_b = ubuf("kg", [128, TT, WB], BF16, TT)
        ke_b = ubuf("ke", [128, TT, WB], BF16, TT)
        v_b = ubuf("vv", [128, TT, WB], BF16, TT)
        sg_b = ubuf("sg", [128, TT, WB], F32, TT)
        ec_b = ubuf("ec", [128, TT, HU, 2], F32, TT)
        qgT_b = ubuf("qgT", [128, HU, ST], BF16, TT)
        kgT_b = ubuf("kgT", [128, HU, ST], BF16, TT)
        AT_b = ubuf("AT", [128, 128], BF16)
        Sm_b = ubuf("Sm", [128, 128], BF16)
        on_b = ubuf("on", [128, WB], F32)
        onb_b = ubuf("onb", [128, WB], BF16)
        rs_b = ubuf("rs", [128, 4], F32)
        rs2_b = ubuf("rs2", [128, 4], F32)
        assert TT == 2 and WB == 256 and ST == 256
        sz_b = ubuf("sz", [128, TT, 256], F32, TT)
        raw_b = ubuf("raw", [128, 2, 3 + ST], F32, 2)
        acc_b = ubuf("acc", [128, 2, ST], F32, 2)
        xa_b = ubuf("xa", [128, 4, ST], BF16, 4)
        xst_b = ubuf("xst", [128, 256], BF16)
        Bt_b = ubuf("Bt", [128, 128], BF16)
        xw_b = ubuf("xw", [128, 256], BF16)
        xdt_b = ubuf("xdt", [128, 256], BF16)
        xsd_b = ubuf("xsd", [128, 256], BF16)
        CBT_b = ubuf("CBT", [128, 128], F32)
        seg_b = ubuf("seg", [128, 512], F32)
        dec_b = ubuf("dec", [128, 512], F32)
        scT_b = ubuf("scTT", [128, 512], BF16)
        yi_b = ubuf("yi", [128, 256], F32)
        yg_b = ubuf("yg", [128, 256], F32)
        ob_b = ubuf("ob", [128, 256], BF16)
        qsT_s = sb("qsT_s", [128, 16, NS]); fT_s = sb("fT_s", [128, 16, NS]); kkT_s = sb("kkT_s", [128, 16, NS])
        vT_s = sb("vT_s", [128, 16, NS]); sgT_s = sb("sgT_s", [128, 16, NS]); szT_s = sb("szT_s", [128, 16, NS])
        xbc_s = sb("xbc_s", [128, 32, NS]); raw_s = sb("raw_s", [128, 32, NS])
        cnt = {"u": 0}
        aT = Buf("aT", RA[:, 0:64 * NC_].rearrange("p (a b) -> p a b", a=64), 64)

        def supertile(x_rows, state_only, with_s, is_last, preconv=False):
            ncol = NC_ if with_s else ST
            tiles = [(x_rows[tt * 128:(tt + 1) * 128, :], 128, tt * 128) for tt in range(TT)]
            if with_s:
                tiles.append((xs_d, NS, ST))
            for (src, nr, c0) in tiles:
                xi = xcnt[0] % 2
                xcnt[0] += 1
                xt, xn = xtile[xi], xnb[xi]
                DMA("sp", xt.name, xt[0:nr, :], src, (), [xt])
                for q4 in range(4):
                    ACT(junk[0:nr, :], xt[0:nr, q4 * 512:(q4 + 1) * 512], AF.Square, [xt], [st4],
                        accum=st4[0:nr, q4:q4 + 1])
                S.op("dve", lambda e, nr=nr: e.tensor_reduce(st4[0:nr, 4:5], st4[0:nr, 0:4], AX.X, ALU.add), [st4], [st4])
                rsqrt(st4[0:nr, 5:6], st4[0:nr, 4:5], [st4], [st4], 1.0 / D)
                TS("dve", xn[0:nr, :], xt[0:nr, :], st4[0:nr, 5:6], None, ALU.mult, None, [xt, st4], [xn])
                for k4 in range(4):
                    if not state_only:
                        pm = MMR.next()
                        for j in range(4):
                            kc = k4 * 4 + j
                            TR(pm[:, j * 128:j * 128 + nr], xt[0:nr, kc * 128:(kc + 1) * 128], C("ident")[0:nr, 0:nr],
                               [xt, cst], [pm])
                        CP("act", xT[:, k4 * 4:k4 * 4 + 4, c0:c0 + nr],
                           pm[:].rearrange("p (j c) -> p j c", j=4)[:, :, 0:nr], [pm], [(xT, range(k4 * 4, k4 * 4 + 4))])
                    ph = HR.next()
                    pv = ph[:].bitcast(BF16)
                    for j in range(4):
                        kc = k4 * 4 + j
                        TR(pv[:, j * 128:j * 128 + nr], xn[0:nr, kc * 128:(kc + 1) * 128], identb[0:nr, 0:nr],
                           [xn, identb], [ph])
                    for j in range(4):
                        kc = k4 * 4 + j
                        if nr == 128:
                            TS("dve", hT[:, kc, c0:c0 + nr], pv[:, j * 128:j * 128 + nr], G1[:, kc, 0:1], SH1[:, kc, 0:1],
                               ALU.mult, ALU.add, [ph, G1, SH1b], [hT])
                        else:
                            TTo("dve", hT[:, kc, c0:c0 + nr], pv[:, j * 128:j * 128 + nr], G1[:, kc, 1:NS + 1], ALU.mult,
                                [ph, G1], [hT])
                            TTo("dve", hT[:, kc, c0:c0 + nr], hT[:, kc, c0:c0 + nr], SH1[:, kc, 1:NS + 1], ALU.add,
                                [hT, SH1b], [hT])
            TAP("hT", hT[:, :, 0:ST], [hT])
            TAP("xT", xT[:, :, 0:ST], [xT])
            if stop == "A":
                return

            def optB(wsl, wcols, nco):
                outs = []
                for tt in range(TT):
                    pm = MMR.next()
                    for kc in range(NKC):
                        MM(pm[:, 0:nco], hT[:, kc, tt * 128:(tt + 1) * 128], wsl[:, kc, wcols:wcols + nco], [hT, wsl], [pm],
                           start=(kc == 0), stop=(kc == NKC - 1))
                    outs.append(pm)
                return outs

            def optA(wsl, m, c0, c1):
                pm = MMR.next()
                for kc in range(NKC):
                    MM(pm[:, 0:c1 - c0], wsl[:, kc, m * 128:(m + 1) * 128], hT[:, kc, c0:c1], [hT, wsl], [pm],
                       start=(kc == 0), stop=(kc == NKC - 1))
                return pm

            for tt in range(TT):
                pq = QR.next()
                for kc in range(NKC):
                    MM(pq[:, 0:32], hT[:, kc, tt * 128:(tt + 1) * 128], wdt[:, kc, :], [hT, wdt], [pq],
                       start=(kc == 0), stop=(kc == NKC - 1))
                TTo("dve", dtt[:, tt, :], pq[:, 0:32], rowb[:, 0:32], ALU.add, [pq, rowb], [(dtt, tt)])
                ACT(dtt[:, tt, :], dtt[:, tt, :], AF.Exp, [(dtt, tt)], [(dtt, tt)])
                ACT(dtt[:, tt, :], dtt[:, tt, :], AF.Ln, [(dtt, tt)], [(dtt, tt)], bias=1.0)
                TTo("dve", dtA[:, tt, :], dtt[:, tt, :], rowb[:, 32:64], ALU.mult, [(dtt, tt), rowb], [(dtA, tt)])
                p1 = QR.next(); p2 = QR.next()
                MM(p1[:, 0:32], C("U"), dtA[:, tt, :], [cst, (dtA, tt)], [p1])
                MM(p2[:, 0:32], C("ones"), dtA[:, tt, :], [cst, (dtA, tt)], [p2])
                CP("dve", cum[:, tt, :], p1[:, 0:32], [p1], [(cum, tt)])
                ACT(expcum[:, tt, :], p1[:, 0:32], AF.Exp, [p1], [(expcum, tt)])
                ACT(Eend[:, tt, :], p2[:, 0:32], AF.Exp, [p2], [(Eend, tt)])
                TTo("dve", wend[:, tt, :], p2[:, 0:32], cum[:, tt, :], ALU.subtract, [p2, (cum, tt)], [(wend, tt)])
                ACT(wend[:, tt, :], wend[:, tt, :], AF.Exp, [(wend, tt)], [(wend, tt)])
                TTo("dve", wend[:, tt, :], wend[:, tt, :], dtt[:, tt, :], ALU.mult, [(wend, tt), (dtt, tt)], [(wend, tt)])
            if with_s:
                pq = QR.next()
                for kc in range(NKC):
                    MM(pq[0:32, 0:NS], wdt[:, kc, :], hT[:, kc, ST:NC_], [hT, wdt], [pq], start=(kc == 0), stop=(kc == NKC - 1))
                ACT(dts[:, 0, :], pq[0:32, 0:NS], AF.Exp, [pq, hcol], [dts], bias=hcol[:, 0:1])
                ACT(dts[:, 0, :], dts[:, 0, :], AF.Ln, [dts], [dts], bias=1.0)
                TS("dve", dts[:, 1, :], dts[:, 0, :], hcol[:, 1:2], None, ALU.mult, None, [dts, hcol], [dts])
                ACT(dts[:, 1, :], dts[:, 1, :], AF.Exp, [dts], [dts])

            def gen_hgrn(hb):
                sq, kk, lf, ex, qg, kg, ke, vv, sg, ec = (sq_b[0], k_b[0], lf_b[0], ex_b[0], qg_b[0], kg_b[0], ke_b[0],
                                                          v_b[0], sg_b[0], ec_b[0])
                qgT, kgT = qgT_b[0], kgT_b[0]
                cs = slice(hb * WB, (hb + 1) * WB)
                for gname in "qfig":
                    if state_only and gname in "qg":
                        continue
                    gi = "qfig".index(gname)
                    wsl = W.take(w_in_d, 0, [(gi * D + hb * WB, WB)])
                    pms = optB(wsl, 0, WB)
                    for tt, pm in enumerate(pms):
                        if gname == "q":
                            ACT(sq[:, tt, :], pm[:, 0:WB], AF.Silu, [pm], [(sq, tt)])
                        elif gname == "i":
                            CP("act", vv[:, tt, :], pm[:, 0:WB], [pm], [(vv, tt)])
                        elif gname == "g":
                            ACT(sg[:, tt, :], pm[:, 0:WB], AF.Silu, [pm], [(sg, tt)])
                        else:
                            ACT(lf[:, tt, :], pm[:, 0:WB], AF.Sigmoid, [pm], [(lf, tt)], scale=-1.0)
                    if gname == "f":
                        for tt in range(TT):
                            TTo("dve", kk[:, tt, :], lf[:, tt, :], oml_b[:, cs], ALU.mult, [(lf, tt), oml_b], [(kk, tt)])
                        for tt in range(TT):
                            ACT(lf[:, tt, :], kk[:, tt, :], AF.Ln, [(kk, tt)], [(lf, tt)], bias=1.0, scale=-1.0)
                    if with_s:
                        for m in range(HU):
                            ch = hb * HU + m
                            pm = optA(wsl, m, ST, NC_)
                            if gname == "q":
                                ACT(qsT_s[:, ch, :], pm[:, 0:NS], AF.Silu, [pm], [qsT_s])
                            elif gname == "i":
                                CP("act", vT_s[:, ch, :], pm[:, 0:NS], [pm], [vT_s])
                            elif gname == "g":
                                ACT(sgT_s[:, ch, :], pm[:, 0:NS], AF.Silu, [pm], [sgT_s])
                            else:
                                ACT(kkT_s[:, ch, :], pm[:, 0:NS], AF.Sigmoid, [pm], [kkT_s], scale=-1.0)
                                TS("dve", kkT_s[:, ch, :], kkT_s[:, ch, :], omlc[:, ch:ch + 1], None, ALU.mult, None,
                                   [kkT_s, omlc], [kkT_s])
                                TS("dve", fT_s[:, ch, :], kkT_s[:, ch, :], -1.0, 1.0, ALU.mult, ALU.add, [kkT_s], [fT_s])
                    yield
                for tt in range(TT):
                    pa = HR.next(); pb = HR.next(); pc = QR.next()
                    MM(pa[:, 0:WB], C("triP"), lf[:, tt, :], [cst, (lf, tt)], [pa])
                    MM(pb[:, 0:WB], C("upper"), lf[:, tt, :], [cst, (lf, tt)], [pb])
                    for h in range(HU):
                        MM(pc[:, 2 * h:2 * h + 2], lf[:, tt, h * 128:(h + 1) * 128], hsel, [cst, (lf, tt)], [pc])
                    ACT(ec[:, tt, :, :].rearrange("p h c -> p (h c)"), pc[:, 0:2 * HU], AF.Exp, [pc], [(ec, tt)])
                    ACT(ex[:, 0, :], pa[:, 0:WB], AF.Exp, [pa], [(ex, 0)])
                    ACT(ex[:, 1, :], pa[:, 0:WB], AF.Exp, [pa], [(ex, 1)], scale=-1.0)
                    ACT(ex[:, 2, :], pb[:, 0:WB], AF.Exp, [pb], [(ex, 2)])
                    TTo("dve", kg[:, tt, :], kk[:, tt, :], ex[:, 1, :], ALU.mult, [(kk, tt), (ex, 1)], [(kg, tt)])
                    TTo("dve", ex[:, 2, :], ex[:, 2, :], ex[:, 1, :], ALU.mult, [(ex, 1), (ex, 2)], [(ex, 2)])
                    TTo("dve", ke[:, tt, :], kk[:, tt, :], ex[:, 2, :], ALU.mult, [(kk, tt), (ex, 2)], [(ke, tt)])
                    if not state_only:
                        TTo("dve", qg[:, tt, :], sq[:, tt, :], ex[:, 0, :], ALU.mult, [(sq, tt), (ex, 0)], [(qg, tt)])
                    yield
                    if not state_only:
                        ph = HR.next()
                        pv = ph[:].bitcast(BF16)
                        for h in range(HU):
                            TR(pv[:, h * 128:(h + 1) * 128], qg[:, tt, h * 128:(h + 1) * 128], identb[:],
                               [(qg, tt), identb], [ph])
                            TR(pv[:, (HU + h) * 128:(HU + h + 1) * 128], kg[:, tt, h * 128:(h + 1) * 128], identb[:],
                               [(kg, tt), identb], [ph])
                        CP("act", qgT[:, :, tt * 128:(tt + 1) * 128],
                           pv[:, 0:HU * 128].rearrange("p (h c) -> p h c", h=HU), [ph], [(qgT, tt)])
                        CP("dve", kgT[:, :, tt * 128:(tt + 1) * 128],
                           pv[:, HU * 128:2 * HU * 128].rearrange("p (h c) -> p h c", h=HU), [ph], [(kgT, tt)])
                        yield
                for tt in range(TT):
                    tc_ = slice(tt * 128, (tt + 1) * 128)
                    po = PO if not state_only else None
                    for h in range(HU):
                        hd = hb * HU + h
                        hc = slice(h * 128, (h + 1) * 128)
                        if not state_only:
                            AT, Sm = AT_b[(tt * HU + h) % 2], Sm_b[(tt * HU + h) % 2]
                            TS("dve", Sm[:], Sst[:, hd, :], ec[:, tt, h, 0:1], None, ALU.mult, None, [(Sst, hd), (ec, tt)], [Sm])
                            pA = QR.next()
                            MM(pA[:], kgT[:, h, tc_], qgT[:, h, tc_], [(kgT, tt), (qgT, tt)], [pA])
                            TTo("dve", AT[:], pA[:], C("cmask"), ALU.mult, [pA, cst], [AT])
                            yield
                            MM(po[:, hc], AT[:], vv[:, tt, hc], [AT, (vv, tt)], [po], start=True, stop=False)
                            MM(po[:, hc], qgT[:, h, tc_], Sm[:], [(qgT, tt), Sm], [po], start=False, stop=True)
                        pd = QR.next()
                        MM(pd[:], ke[:, tt, hc], vv[:, tt, hc], [(ke, tt), (vv, tt)], [pd])
                        STT("dve", Sst[:, hd, :], Sst[:, hd, :], ec[:, tt, h, 1:2], pd[:], ALU.mult, ALU.add,
                            [(Sst, hd), (ec, tt), pd], [(Sst, hd)])
                        yield
                    if not state_only:
                        on, onb, rs = on_b[tt % 2], onb_b[tt % 2], rs_b[tt % 2]
                        for h in range(HU):
                            ACT(junk[:, 0:128], po[:, h * 128:(h + 1) * 128], AF.Square, [po], [rs], accum=rs[:, h:h + 1])
                        rsqrt(rs[:, 2:2 + HU], rs[:, 0:HU], [rs], [rs], 1.0 / 128)
                        TTo("dve", on[:].rearrange("p (h c) -> p h c", h=HU), po[:, 0:WB].rearrange("p (h c) -> p h c", h=HU),
                            rs[:, 2:2 + HU].unsqueeze(2).to_broadcast([128, HU, 128]), ALU.mult, [po, rs], [on])
                        TTo("dve", onb[:], on[:], sg[:, tt, :], ALU.mult, [on, (sg, tt)], [onb])
                        yield
                        ph = HR.next()
                        pv = ph[:].bitcast(BF16)
                        for h in range(HU):
                            TR(pv[:, h * 128:(h + 1) * 128], onb[:, h * 128:(h + 1) * 128], identb[:], [onb, identb], [ph])
                        TTo("dve", oT[:, hb * HU:(hb + 1) * HU, tc_], pv[:, 0:HU * 128].rearrange("p (h c) -> p h c", h=HU),
                            col("hn", hb * HU, HU).unsqueeze(2).to_broadcast([128, HU, 128]), ALU.mult, [ph, colv],
                            [(oT, range(hb * HU, (hb + 1) * HU))])
                        yield

            def gen_ssd(g):
                sz, raw, acc, xa = sz_b[0], raw_b[0], acc_b[0], xa_b[0]
                hs = slice(g * 4, g * 4 + 4)
                if not state_only:
                    wsl = W.take(w_in_d, 0, [(4 * D + g * 256, 256)])
                    pms = optB(wsl, 0, 256)
                    for tt, pm in enumerate(pms):
                        ACT(sz[:, tt, :], pm[:, 0:256], AF.Silu, [pm], [(sz, tt)])
                    if with_s:
                        for m in range(2):
                            pm = optA(wsl, m, ST, NC_)
                            ACT(szT_s[:, g * 2 + m, :], pm[:, 0:NS], AF.Silu, [pm], [szT_s])
                    yield
                for blk in range(2):
                    if blk == 0:
                        wsl = W.take(w_in_d, 0, [(5 * D + g * 256, 256)])
                    else:
                        wsl = W.take(w_in_d, 0, [(5 * D + D + g * 128, 128), (5 * D + D + 1024 + g * 128, 128)])
                    chids = [(g * 2 + m) if blk == 0 else (16 + g if m == 0 else 24 + g) for m in range(2)]
                    for m in range(2):
                        chid = chids[m]
                        pm = optA(wsl, m, 0, ncol)
                        CP("act", raw[:, m, 3:3 + ST], pm[:, 0:ST], [pm], [(raw, m)])
                        CP("dve", raw[:, m, 0:3], halo[:, chid, :], [(halo, chid)], [(raw, m)])
                        CP("dve", halo[:, chid, :], raw[:, m, ST:ST + 3], [(raw, m)], [(halo, chid)])
                        if with_s:
                            CP("act", raw_s[:, chid, :], pm[:, ST:NC_], [pm], [raw_s])
                    cw = lambda m, i, chids=chids: col("cw", chids[m] * 4 + i, 1)
                    for m in range(2):
                        TS("dve", acc[:, m, :], raw[:, m, 0:ST], cw(m, 0), col("cb", chids[m], 1), ALU.mult, ALU.add,
                           [(raw, m), colv], [(acc, m)])
                    for i in (1, 2, 3):
                        for m in range(2):
                            STT("dve", acc[:, m, :], raw[:, m, i:i + ST], cw(m, i), acc[:, m, :], ALU.mult, ALU.add,
                                [(raw, m), (acc, m), colv], [(acc, m)])
                    for m in range(2):
                        ACT(xa[:, blk * 2 + m, :], acc[:, m, :], AF.Silu, [(acc, m)], [(xa, blk * 2 + m)])
                    yield

                def hb3(ap):
                    return ap.unsqueeze(2).to_broadcast([128, 4, 64])
                for tt in range(TT):
                    tc_ = slice(tt * 128, (tt + 1) * 128)
                    xst, Bt, xw = xst_b[0], Bt_b[0], xw_b[0]
                    ph = HR.next()
                    pv = ph[:].bitcast(BF16)
                    for m in range(3):
                        TR(pv[:, m * 128:(m + 1) * 128], xa[:, m, tc_], identb[:], [(xa, m), identb], [ph])
                    CP("act", xst[:], pv[:, 0:256], [ph], [xst])
                    CP("dve", Bt[:], pv[:, 256:384], [ph], [Bt])
                    x3 = xst[:].rearrange("p (h c) -> p h c", h=4)
                    TTo("dve", xw[:].rearrange("p (h c) -> p h c", h=4), x3, hb3(wend[:, tt, hs]), ALU.mult,
                        [xst, (wend, tt)], [xw])
                    if not state_only:
                        xdt, xsd, CBT, seg, dec, scTT = xdt_b[0], xsd_b[0], CBT_b[0], seg_b[0], dec_b[0], scT_b[0]
                        yi, yg, ob = yi_b[0], yg_b[0], ob_b[0]
                        TTo("dve", xdt[:].rearrange("p (h c) -> p h c", h=4), x3, hb3(dtt[:, tt, hs]), ALU.mult,
                            [xst, (dtt, tt)], [xdt])
                        TTo("dve", xsd[:].rearrange("p (h c) -> p h c", h=4), x3, hb3(rowb[:, 64 + g * 4:64 + g * 4 + 4]),
                            ALU.mult, [xst, rowb], [xsd])
                        yield
                        pcb = QR.next()
                        MM(pcb[:], xa[:, 2, tc_], xa[:, 3, tc_], [(xa, 2), (xa, 3)], [pcb])
                        CP("act", CBT[:], pcb[:], [pcb], [CBT])
                        pcr = MMR.next()
                        for hh in range(4):
                            hd = g * 4 + hh
                            MM(pcr[:, hh * 128:(hh + 1) * 128], dtA[:, tt, hd:hd + 1].to_broadcast([128, 128]), C("U"),
                               [(dtA, tt), cst], [pcr])
                        for hh in range(4):
                            hd = g * 4 + hh
                            STT("dve", seg[:, hh * 128:(hh + 1) * 128], pcr[:, hh * 128:(hh + 1) * 128], cum[:, tt, hd:hd + 1],
                                C("negmask"), ALU.subtract, ALU.add, [pcr, (cum, tt), cst], [seg])
                        ACT(dec[:], seg[:], AF.Exp, [seg], [dec])
                        TTo("dve", scTT[:].rearrange("p (h c) -> p h c", h=4), dec[:].rearrange("p (h c) -> p h c", h=4),
                            CBT[:].unsqueeze(1).to_broadcast([128, 4, 128]), ALU.mult, [dec, CBT], [scTT])
                        yield
                        py = HR.next(); pyi = HR.next()
                        MM(py[:, 0:256], identb[:], xsd[:], [identb, xsd], [py], start=True, stop=False)
                        for hh in range(4):
                            MM(py[:, hh * 64:(hh + 1) * 64], scTT[:, hh * 128:(hh + 1) * 128], xdt[:, hh * 64:(hh + 1) * 64],
                               [scTT, xdt], [py], start=False, stop=(hh == 3))
                        MM(pyi[:, 0:256], xa[:, 3, tc_], hstb[:, g * 256:(g + 1) * 256], [(xa, 3), (hstb, g)], [pyi])
                        TTo("dve", yi[:].rearrange("p (h c) -> p h c", h=4), pyi[:, 0:256].rearrange("p (h c) -> p h c", h=4),
                            hb3(expcum[:, tt, hs]), ALU.mult, [pyi, (expcum, tt)], [yi])
                        TTo("dve", yi[:], py[:, 0:256], yi[:], ALU.add, [py, yi], [yi])
                        TTo("dve", yg[:], yi[:], sz[:, tt, :], ALU.mult, [yi, (sz, tt)], [yg])
                        rs = rs2_b[0]
                        ACT(junk[:, 0:256], yg[:], AF.Square, [yg], [rs], accum=rs[:, 0:1])
                        rsqrt(rs[:, 1:2], rs[:, 0:1], [rs], [rs], 1.0 / 256)
                        TS("dve", ob[:], yg[:], rs[:, 1:2], None, ALU.mult, None, [yg, rs], [ob])
                    pdl = HR.next()
                    MM(pdl[:, 0:256], Bt[:], xw[:], [Bt, xw], [pdl])
                    h3 = hst[:, g * 256:(g + 1) * 256].rearrange("p (h c) -> p h c", h=4)
                    TTo("dve", h3, h3, hb3(Eend[:, tt, hs]), ALU.mult, [(hst, g), (Eend, tt)], [(hst, g)])
                    TTo("dve", hst[:, g * 256:(g + 1) * 256], hst[:, g * 256:(g + 1) * 256], pdl[:, 0:256], ALU.add,
                        [(hst, g), pdl], [(hst, g)])
                    CP("act", hstb[:, g * 256:(g + 1) * 256], hst[:, g * 256:(g + 1) * 256], [(hst, g)], [(hstb, g)])
                    yield
                    if not state_only:
                        ph2 = HR.next()
                        pv2 = ph2[:].bitcast(BF16)
                        for m in range(2):
                            TR(pv2[:, m * 128:(m + 1) * 128], ob[:, m * 128:(m + 1) * 128], identb[:], [ob, identb], [ph2])
                        TTo("dve", oT[:, 16 + g * 2:16 + g * 2 + 2, tc_], pv2[:, 0:256].rearrange("p (h c) -> p h c", h=2),
                            col("sn", g * 2, 2).unsqueeze(2).to_broadcast([128, 2, 128]), ALU.mult, [ph2, colv],
                            [(oT, range(16 + g * 2, 16 + g * 2 + 2))])
                        yield

            stepc = 0
            for u_ in range(8):
                gens = [gen_hgrn(u_), gen_ssd(u_)]
                while gens:
                    for gen in list(gens):
                        try:
                            next(gen)
                        except StopIteration:
                            gens.remove(gen)
                        stepc += 1
                        if preconv and stepc % 6 == 0:
                            W.preconvert()

            TAP("oT", oT[:, :, 0:ST], [oT])
            TAP("hst", hst[:], [hst])
            if state_only or stop == "ssd":
                return

            if with_s and do_samples:
                sample_phase()

            def fm_proj(wd, nkb, K_rhs, dstT):
                for cbk in range(D // WB):
                    pms = [MMR.next() for _ in range(WB // 128)]
                    for kb in range(nkb):
                        wsl = W.take(wd, kb * D, [(cbk * WB, WB)])
                        for m in range(WB // 128):
                            for kc in range(NKC):
                                MM(pms[m][:, 0:ncol], wsl[:, kc, m * 128:(m + 1) * 128], K_rhs[:, kb * NKC + kc, 0:ncol],
                                   [wsl, K_rhs], [pms[m]], start=(kb == 0 and kc == 0), stop=(kb == nkb - 1 and kc == NKC - 1))
                    for m in range(WB // 128):
                        dc = cbk * (WB // 128) + m
                        CP("act", dstT[:, dc, 0:ncol], pms[m][:, 0:ncol], [pms[m]], [(dstT, dc)])
                        ACT(junkf[:, 0:ncol], pms[m][:, 0:ncol], AF.Square, [pms[m]], [junkf])
                        MM(STAT[:, 0:ncol], C("ones"), junkf[:, 0:ncol], [cst, junkf], [STAT], start=(dc == 0), stop=(dc == NKC - 1))

            def stat_rstd():
                rsqrt(rstd_b[:, 0:ncol], STAT[:, 0:ncol], [STAT], [rstd_b], 1.0 / D)

            def resid_add(GT, srcT):
                for dc in range(NKC):
                    STT("dve", junkf[:, 0:ST], srcT[:, dc, 0:ST], GT[:, dc, 0:1], rstd_b[:, 0:ST], ALU.mult, ALU.mult,
                        [(srcT, dc), GT, rstd_b], [junkf])
                    TTo("dve", xT[:, dc, 0:ST], xT[:, dc, 0:ST], junkf[:, 0:ST], ALU.add, [(xT, dc), junkf], [(xT, dc)])
                    if with_s:
                        TTo("dve", junkf[:, ST:NC_], srcT[:, dc, ST:NC_], GT[:, dc, 1:NS + 1], ALU.mult, [(srcT, dc), GT], [junkf])
                        TTo("dve", junkf[:, ST:NC_], junkf[:, ST:NC_], rstd_b[:, ST:NC_], ALU.mult, [junkf, rstd_b], [junkf])
                        TTo("dve", xT[:, dc, ST:NC_], xT[:, dc, ST:NC_], junkf[:, ST:NC_], ALU.add, [(xT, dc), junkf], [(xT, dc)])

            fm_proj(w_out_d, 2, oT, mixT)
            stat_rstd()
            resid_add(GT1, mixT)
            TAP("x1T", xT[:, :, 0:ST], [xT])
            if stop == "out":
                return

            for dc in range(NKC):
                ACT(junkf[:, 0:ncol], xT[:, dc, 0:ncol], AF.Square, [(xT, dc)], [junkf])
                MM(STAT[:, 0:ncol], C("ones"), junkf[:, 0:ncol], [cst, junkf], [STAT], start=(dc == 0), stop=(dc == NKC - 1))
            stat_rstd()
            for dc in range(NKC):
                STT("dve", junkf[:, 0:ST], xT[:, dc, 0:ST], G2[:, dc, 0:1], rstd_b[:, 0:ST], ALU.mult, ALU.mult,
                    [(xT, dc), G2, rstd_b], [junkf])
                TS("dve", hT[:, dc, 0:ST], junkf[:, 0:ST], SH2[:, dc, 0:1], None, ALU.add, None, [junkf, SH2b], [hT])
                if with_s:
                    TTo("dve", junkf[:, ST:NC_], xT[:, dc, ST:NC_], G2[:, dc, 1:NS + 1], ALU.mult, [(xT, dc), G2], [junkf])
                    TTo("dve", junkf[:, ST:NC_], junkf[:, ST:NC_], rstd_b[:, ST:NC_], ALU.mult, [junkf, rstd_b], [junkf])
                    TTo("dve", hT[:, dc, ST:NC_], junkf[:, ST:NC_], SH2[:, dc, 1:NS + 1], ALU.add, [junkf, SH2b], [hT])

            S.barrier()
            for cbk in range(4 * D // WB):
                wsl = W.take(w_up_d, 0, [(cbk * WB, WB)])
                for m in range(WB // 128):
                    fc = cbk * (WB // 128) + m
                    pm = MMR.next()
                    for kc in range(NKC):
                        MM(pm[:, 0:ncol], wsl[:, kc, m * 128:(m + 1) * 128], hT[:, kc, 0:ncol], [wsl, hT], [pm],
                           start=(kc == 0), stop=(kc == NKC - 1))
                    ACT(junkf[:, 0:ncol], pm[:, 0:ncol], AF.Relu, [pm], [junkf])
                    TTo("dve", aT[:, fc, 0:ncol], junkf[:, 0:ncol], junkf[:, 0:ncol], ALU.mult, [junkf], [(aT, fc)])
            fm_proj(w_down_d, 4, aT, mixT)
            stat_rstd()
            resid_add(GT2, mixT)

            S.barrier()
            for tt in range(TT):
                yt = xtile[xcnt[0] % 2]
                xcnt[0] += 1
                for k4 in range(4):
                    pm = MMR.next()
                    for j in range(4):
                        dc = k4 * 4 + j
                        TR(pm[:, j * 128:(j + 1) * 128], xT[:, dc, tt * 128:(tt + 1) * 128], C("ident"), [(xT, dc), cst], [pm])
                    CP("act", yt[:, k4 * 512:(k4 + 1) * 512], pm[:], [pm], [yt])
                DMA("sp", yt.name, out_rows[0][tt * 128:(tt + 1) * 128, :], yt[:], [yt], ())
            if with_s:
                yt = xtile[xcnt[0] % 2]
                xcnt[0] += 1
                for k4 in range(4):
                    pm = MMR.next()
                    for j in range(4):
                        dc = k4 * 4 + j
                        TR(pm[0:NS, j * 128:(j + 1) * 128], xT[:, dc, ST:NC_], C("ident"), [(xT, dc), cst], [pm])
                    CP("act", yt[0:NS, k4 * 512:(k4 + 1) * 512], pm[0:NS, :], [pm], [yt])
                DMA("sp", yt.name, ys_d, yt[0:NS, :], [yt], ())

        rstd_b = sb("rstd_b", [128, NC_])
        dts = sb("dts", [32, 2, NS])
        out_rows = [None]

        def raview(off_b, ncols_f32, shape3=None):
            v = RA[:, off_b // 2:off_b // 2 + ncols_f32 * 2].bitcast(F32)
            if shape3 is not None:
                v = v.rearrange("p (a b) -> p a b", a=shape3)
            return v
        Sb_s = [Buf("Sb%d" % i, raview(i * 8192, 2048, 16), 16) for i in range(2)]
        hb_s = [Buf("hb%d" % i, raview(16384 + i * 8192, 2048, 16), 16) for i in range(2)]
        yT_s = sb("yT_s", [128, 16, NS], nparts=16)
        dE = sb("dE", [128, 16, 2 * NS])
        dtx = sb("dtx", [128, 16, NS])
        oS = sb("oS", [128, 16 * NS])
        tmpS = sb("tmpS", [128, 16 * NS])
        scrb = Buf("scrb", scr_d)
        vTb_s = sb("vTb_s", [128, 16, NS], BF16)
        bcb_s = sb("bcb_s", [128, 16, NS], BF16)

        def sample_phase():
            S.barrier()
            cstT = hb_s[0]
            cv = cstT[:].rearrange("p a b -> p (a b)")[:, 0:32 * 48].rearrange("p (c x) -> p c x", c=32)
            for hf in range(2):
                xt = xtile[hf]
                DMA("sp", xt.name, xt[0:3 * NS, :], scv_d[:, hf * D:(hf + 1) * D], (), [xt])
                for k4 in range(4):
                    pm = MMR.next()
                    for j in range(4):
                        TR(pm[:, j * 128:j * 128 + 3 * NS], xt[0:3 * NS, (k4 * 4 + j) * 128:(k4 * 4 + j + 1) * 128],
                           C("ident")[0:3 * NS, 0:3 * NS], [xt, cst], [pm])
                    CP("act", cv[:, hf * 16 + k4 * 4:hf * 16 + k4 * 4 + 4, :],
                       pm[:].rearrange("p (j c) -> p j c", j=4)[:, :, 0:3 * NS], [pm], [cstT])
            DMA("sp", "cvs01", cvs_d[:, 0:2, :], scv_d.rearrange("(b r) c -> b r c", r=3)[:, 1:3, :], (), ())
            for chid in range(32):
                c3 = cv[:, chid, :].rearrange("p (b r) -> p b r", r=3)
                cw = lambda i: col("cw", chid * 4 + i, 1)
                TS("dve", xbc_s[:, chid, :], c3[:, :, 0], cw(0), col("cb", chid, 1), ALU.mult, ALU.add, [cstT, colv], [xbc_s])
                for i in (1, 2):
                    STT("dve", xbc_s[:, chid, :], c3[:, :, i], cw(i), xbc_s[:, chid, :], ALU.mult, ALU.add,
                        [cstT, colv, xbc_s], [xbc_s])
                STT("dve", xbc_s[:, chid, :], raw_s[:, chid, :], cw(3), xbc_s[:, chid, :], ALU.mult, ALU.add,
                    [raw_s, colv, xbc_s], [xbc_s])
            ACT(xbc_s[:], xbc_s[:], AF.Silu, [xbc_s], [xbc_s])
            for hf in range(2):
                xt = xtile[hf]
                for k4 in range(4):
                    pm = MMR.next()
                    for j in range(4):
                        chid = hf * 16 + k4 * 4 + j
                        TR(pm[0:NS, j * 128:(j + 1) * 128], raw_s[:, chid, :], C("ident"), [raw_s, cst], [pm])
                    CP("act", xt[0:NS, k4 * 512:(k4 + 1) * 512], pm[0:NS, :], [pm], [xt])
                DMA("sp", xt.name, cvs_d[:, 2, hf * D:(hf + 1) * D], xt[0:NS, :], [xt], ())
            DMA("sp", "scrw", scr_d.rearrange("j two x -> (j two) x"), dts[:].rearrange("p q b -> p (q b)"), [dts], [scrb])
            for two in range(2):
                DMA("sp", "dE%d" % two, dE[two * 64:(two + 1) * 64, :, :], scr_d[:, two, :].partition_broadcast(64),
                    [scrb], [dE])
            TTo("dve", dtx[:], dE[:, :, 0:NS], xbc_s[:, 0:16, :], ALU.mult, [dE, xbc_s], [dtx])

            CP("dve", vTb_s[:], vT_s[:], [vT_s], [vTb_s])
            CP("dve", bcb_s[:], xbc_s[:, 16:32, :], [xbc_s], [bcb_s])

            def load(b):
                DMA("sp", Sb_s[b % 2].name, Sb_s[b % 2][:], shg_d[b].rearrange("h k v -> k h v"), (), [Sb_s[b % 2]])
                DMA("sp", hb_s[b % 2].name, hb_s[b % 2][:], ssm_d[b].rearrange("(j two) p n -> (two p) j n", two=2), (),
                    [hb_s[b % 2]])
            load(0)
            for b in range(NS):
                if b + 1 < NS:
                    load(b + 1)
                Sb, hb = Sb_s[b % 2], hb_s[b % 2]
                for h4 in range(4):
                    pm = MMR.next()
                    for j in range(4):
                        h = h4 * 4 + j
                        MM(pm[:, j * 128:(j + 1) * 128], vTb_s[:, h, b:b + 1].to_broadcast([128, 128]), identb[:],
                           [vTb_s, identb], [pm])
                    for j in range(4):
                        h = h4 * 4 + j
                        ACT(Sb[:, h, :], Sb[:, h, :], AF.Identity, [(Sb, h), fT_s], [(Sb, h)], scale=fT_s[:, h, b:b + 1])
                        STT("dve", Sb[:, h, :], pm[:, j * 128:(j + 1) * 128], kkT_s[:, h, b:b + 1], Sb[:, h, :], ALU.mult, ALU.add,
                            [pm, kkT_s, (Sb, h)], [(Sb, h)])
                        MM(STAT[:, h * NS + b:h * NS + b + 1], Sb[:, h, :], qsT_s[:, h, b:b + 1], [(Sb, h), qsT_s], [STAT])
                DMA("sp", Sb.name, hgs_d[b].rearrange("h k v -> k h v"), Sb[:], [Sb], ())
                for g2 in range(4):
                    pm = MMR.next()
                    for gg in range(2):
                        g = g2 * 2 + gg
                        MM(pm[:, (gg * 2) * 128:(gg * 2 + 1) * 128], bcb_s[:, g, b:b + 1].to_broadcast([128, 128]),
                           identb[:], [bcb_s, identb], [pm])
                        MM(pm[:, (gg * 2 + 1) * 128:(gg * 2 + 2) * 128], bcb_s[:, 8 + g, b:b + 1].to_broadcast([128, 128]),
                           identb[:], [bcb_s, identb], [pm])
                    for gg in range(2):
                        g = g2 * 2 + gg
                        for jj in range(2):
                            j = g * 2 + jj
                            ACT(hb[:, j, :], hb[:, j, :], AF.Identity, [(hb, j), dE], [(hb, j)], scale=dE[:, j, NS + b:NS + b + 1])
                            STT("dve", hb[:, j, :], pm[:, (gg * 2) * 128:(gg * 2 + 1) * 128], dtx[:, j, b:b + 1], hb[:, j, :],
                                ALU.mult, ALU.add, [pm, dtx, (hb, j)], [(hb, j)])
                            STT("dve", dumf[:, 0:128], hb[:, j, :], 1.0, pm[:, (gg * 2 + 1) * 128:(gg * 2 + 2) * 128],
                                ALU.mult, ALU.mult, [(hb, j), pm], [(yT_s, j)], accum=yT_s[:, j, b:b + 1])
                DMA("sp", hb.name, sss_d[b].rearrange("(j two) p n -> (two p) j n", two=2), hb[:], [hb], ())
            CP("act", oS[:], STAT[:, 0:16 * NS], [STAT], [oS])
            TTo("dve", tmpS[:], oS[:], oS[:], ALU.mult, [oS], [tmpS])
            MM(STAT[:, 0:16 * NS], C("ones"), tmpS[:], [cst, tmpS], [STAT])
            rsqrt(tmpS[:], STAT[:, 0:16 * NS], [STAT], [tmpS], 1.0 / 128)
            TTo("dve", oS[:], oS[:], tmpS[:], ALU.mult, [oS, tmpS], [oS])
            TTo("dve", oS[:], oS[:], sgT_s[:].rearrange("p h b -> p (h b)"), ALU.mult, [oS, sgT_s], [oS])
            TTo("dve", oT[:, 0:16, ST:NC_], oS[:].rearrange("p (h b) -> p h b", h=16),
                col("hn").unsqueeze(2).to_broadcast([128, 16, NS]), ALU.mult, [oS, colv], [(oT, range(0, 16))])
            TTo("dve", oS[:].rearrange("p (h b) -> p h b", h=16), xbc_s[:, 0:16, :],
                col("dskE").unsqueeze(2).to_broadcast([128, 16, NS]), ALU.mult, [xbc_s, colv], [oS])
            TTo("dve", oS[:], oS[:], yT_s[:].rearrange("p h b -> p (h b)"), ALU.add, [oS, yT_s], [oS])
            TTo("dve", oS[:], oS[:], szT_s[:].rearrange("p h b -> p (h b)"), ALU.mult, [oS, szT_s], [oS])
            TTo("dve", tmpS[:], oS[:], oS[:], ALU.mult, [oS], [tmpS])
            MM(STAT[:, 0:16 * NS], C("ones"), tmpS[:], [cst, tmpS], [STAT])
            t4 = tmpS[:].rearrange("p (g t b) -> p g t b", g=8, t=2)
            s4 = STAT[:, 0:16 * NS].rearrange("p (g t b) -> p g t b", g=8, t=2)
            CP("act", tmpS[:], STAT[:, 0:16 * NS], [STAT], [tmpS])
            TTo("dve", t4[:, :, 0, :], t4[:, :, 0, :], t4[:, :, 1, :], ALU.add, [tmpS], [tmpS])
            CP("dve", t4[:, :, 1, :], t4[:, :, 0, :], [tmpS], [tmpS])
            rsqrt(tmpS[:], tmpS[:], [tmpS], [tmpS], 1.0 / 256)
            TTo("dve", oS[:], oS[:], tmpS[:], ALU.mult, [oS, tmpS], [oS])
            TTo("dve", oT[:, 16:32, ST:NC_], oS[:].rearrange("p (h b) -> p h b", h=16),
                col("sn").unsqueeze(2).to_broadcast([128, 16, NS]), ALU.mult, [oS, colv], [(oT, range(16, 32))])
            S.barrier()


        def prefix_pass():
            NT8 = SEQ_HALF // 128
            S.barrier()
            hTp = Buf("hTp", XM[:].bitcast(BF16)[:, 0:NKC * SEQ_HALF].rearrange("p (a b) -> p a b", a=NKC), 1)
            off = [0]

            def pbuf(name, shape, dt=F32, np_=1):
                n = 1
                for d_ in shape[1:]:
                    n *= d_
                nb = (n * (4 if dt == F32 else 2) + 31) // 32 * 32
                o = off[0]
                off[0] += nb
                assert off[0] <= RA_BYTES, ("RA overflow (prefix)", name, off[0], RA_BYTES)
                v = RA[:, o // 2:(o + nb) // 2]
                if dt == F32:
                    v = v.bitcast(F32)
                v = v[:, 0:n]
                if len(shape) == 3:
                    v = v.rearrange("p (a b) -> p a b", a=shape[1])
                elif len(shape) == 4:
                    v = v.rearrange("p (a b c) -> p a b c", a=shape[1], b=shape[2])
                return Buf("pp_" + name, v, np_)
            p_dtt = pbuf("dtt", [128, NT8, 32], F32, NT8); p_dtA = pbuf("dtA", [128, NT8, 32], F32, NT8)
            p_cum = pbuf("cum", [128, NT8, 32], F32, NT8); p_Eend = pbuf("Eend", [128, NT8, 32], F32, NT8)
            p_wend = pbuf("wend", [128, NT8, 32], F32, NT8)
            p_lf = [pbuf("lf%d" % i, [128, WB]) for i in range(2)]
            p_kk = [pbuf("kk%d" % i, [128, WB]) for i in range(2)]
            p_ex = [pbuf("ex%d" % i, [128, 2, WB], F32, 2) for i in range(2)]
            p_ke = pbuf("ke", [128, NT8, WB], BF16, NT8)
            p_ec = pbuf("ec", [128, NT8, HU, 2], F32, NT8)
            p_vv = [pbuf("vv%d" % i, [128, WB], BF16) for i in range(2)]
            p_raw = pbuf("raw", [128, 3 + SEQ_HALF])
            p_acc = pbuf("acc", [128, SEQ_HALF])
            p_xa = pbuf("xa", [128, 3, SEQ_HALF], BF16, 3)
            p_xst = [pbuf("xst%d" % i, [128, 256], BF16) for i in range(2)]
            p_Bt = [pbuf("Bt%d" % i, [128, 128], BF16) for i in range(2)]
            p_xw = [pbuf("xw%d" % i, [128, 256], BF16) for i in range(2)]

            for t8 in range(NT8):
                xi = xcnt[0] % 2
                xcnt[0] += 1
                xt, xn = xtile[xi], xnb[xi]
                DMA("sp", xt.name, xt[:], xpre_d[t8 * 128:(t8 + 1) * 128, :], (), [xt])
                for q4 in range(4):
                    ACT(junk[:, :], xt[:, q4 * 512:(q4 + 1) * 512], AF.Square, [xt], [st4], accum=st4[:, q4:q4 + 1])
                S.op("dve", lambda e: e.tensor_reduce(st4[:, 4:5], st4[:, 0:4], AX.X, ALU.add), [st4], [st4])
                rsqrt(st4[:, 5:6], st4[:, 4:5], [st4], [st4], 1.0 / D)
                TS("dve", xn[:], xt[:], st4[:, 5:6], None, ALU.mult, None, [xt, st4], [xn])
                for k4 in range(4):
                    ph = HR.next()
                    pv = ph[:].bitcast(BF16)
                    for j in range(4):
                        kc = k4 * 4 + j
                        TR(pv[:, j * 128:(j + 1) * 128], xn[:, kc * 128:(kc + 1) * 128], identb[:], [xn, identb], [ph])
                    for j in range(4):
                        kc = k4 * 4 + j
                        if j % 2 == 0:
                            TS("dve", hTp[:, kc, t8 * 128:(t8 + 1) * 128], pv[:, j * 128:(j + 1) * 128], G1[:, kc, 0:1],
                               SH1[:, kc, 0:1], ALU.mult, ALU.add, [ph, G1, SH1b], [hTp])
                        else:
                            ACT(hTp[:, kc, t8 * 128:(t8 + 1) * 128], pv[:, j * 128:(j + 1) * 128], AF.Identity, [ph, G1, SH1b], [hTp],
                                bias=SH1[:, kc, 0:1], scale=G1[:, kc, 0:1])
            for t8 in range(NT8):
                pq = QR.next()
                for kc in range(NKC):
                    MM(pq[:, 0:32], hTp[:, kc, t8 * 128:(t8 + 1) * 128], wdt[:, kc, :], [hTp, wdt], [pq],
                       start=(kc == 0), stop=(kc == NKC - 1))
                TTo("dve", p_dtt[:, t8, :], pq[:, 0:32], rowb[:, 0:32], ALU.add, [pq, rowb], [(p_dtt, t8)])
                ACT(p_dtt[:, t8, :], p_dtt[:, t8, :], AF.Exp, [(p_dtt, t8)], [(p_dtt, t8)])
                ACT(p_dtt[:, t8, :], p_dtt[:, t8, :], AF.Ln, [(p_dtt, t8)], [(p_dtt, t8)], bias=1.0)
                TTo("dve", p_dtA[:, t8, :], p_dtt[:, t8, :], rowb[:, 32:64], ALU.mult, [(p_dtt, t8), rowb], [(p_dtA, t8)])
                p1 = QR.next(); p2 = QR.next()
                MM(p1[:, 0:32], C("U"), p_dtA[:, t8, :], [cst, (p_dtA, t8)], [p1])
                MM(p2[:, 0:32], C("ones"), p_dtA[:, t8, :], [cst, (p_dtA, t8)], [p2])
                CP("dve", p_cum[:, t8, :], p1[:, 0:32], [p1], [(p_cum, t8)])
                ACT(p_Eend[:, t8, :], p2[:, 0:32], AF.Exp, [p2], [(p_Eend, t8)])
                TTo("dve", p_wend[:, t8, :], p2[:, 0:32], p_cum[:, t8, :], ALU.subtract, [p2, (p_cum, t8)], [(p_wend, t8)])
                ACT(p_wend[:, t8, :], p_wend[:, t8, :], AF.Exp, [(p_wend, t8)], [(p_wend, t8)])
                TTo("dve", p_wend[:, t8, :], p_wend[:, t8, :], p_dtt[:, t8, :], ALU.mult, [(p_wend, t8), (p_dtt, t8)],
                    [(p_wend, t8)])

            def tokB(wsl, t8, nco):
                pm = MMR.next()
                for kc in range(NKC):
                    MM(pm[:, 0:nco], hTp[:, kc, t8 * 128:(t8 + 1) * 128], wsl[:, kc, 0:nco], [hTp, wsl], [pm],
                       start=(kc == 0), stop=(kc == NKC - 1))
                return pm

            def gh(hb):
                cs = slice(hb * WB, (hb + 1) * WB)
                wsl = W.take(w_in_d, 0, [(1 * D + hb * WB, WB)], hold=True)

                def stA(t8):
                    lf, kk = p_lf[t8 % 2], p_kk[t8 % 2]
                    pm = tokB(wsl, t8, WB)
                    ACT(lf[:], pm[:, 0:WB], AF.Sigmoid, [pm], [lf], scale=-1.0)
                    TTo("dve", kk[:], lf[:], oml_b[:, cs], ALU.mult, [lf, oml_b], [kk])
                    ACT(lf[:], kk[:], AF.Ln, [kk], [lf], bias=1.0, scale=-1.0)
                    if t8 == NT8 - 1:
                        W.release(wsl)

                def stB(t8):
                    lf, kk, ex = p_lf[t8 % 2], p_kk[t8 % 2], p_ex[t8 % 2]
                    pa = HR.next(); pb = HR.next(); pc = QR.next()
                    MM(pa[:, 0:WB], C("triP"), lf[:], [cst, lf], [pa])
                    MM(pb[:, 0:WB], C("upper"), lf[:], [cst, lf], [pb])
                    for h in range(HU):
                        MM(pc[:, 2 * h:2 * h + 2], lf[:, h * 128:(h + 1) * 128], hsel, [cst, lf], [pc])
                    ACT(p_ec[:, t8, :, :].rearrange("p h c -> p (h c)"), pc[:, 0:2 * HU], AF.Exp, [pc], [(p_ec, t8)])
                    ACT(ex[:, 0, :], pa[:, 0:WB], AF.Exp, [pa], [(ex, 0)], scale=-1.0)
                    ACT(ex[:, 1, :], pb[:, 0:WB], AF.Exp, [pb], [(ex, 1)])
                    TTo("dve", ex[:, 1, :], ex[:, 1, :], ex[:, 0, :], ALU.mult, [(ex, 0), (ex, 1)], [(ex, 1)])
                    TTo("dve", p_ke[:, t8, :], kk[:], ex[:, 1, :], ALU.mult, [kk, (ex, 1)], [(p_ke, t8)])
                stA(0)
                yield
                for t8 in range(NT8):
                    if t8 + 1 < NT8:
                        stA(t8 + 1)
                        yield
                    stB(t8)
                    yield
                wsl2 = W.take(w_in_d, 0, [(2 * D + hb * WB, WB)], hold=True)

                def stAi(t8):
                    vv = p_vv[t8 % 2]
                    pm = tokB(wsl2, t8, WB)
                    CP("act", vv[:], pm[:, 0:WB], [pm], [vv])
                    if t8 == NT8 - 1:
                        W.release(wsl2)

                def stBi(t8):
                    vv = p_vv[t8 % 2]
                    for h in range(HU):
                        hd = hb * HU + h
                        hc = slice(h * 128, (h + 1) * 128)
                        pd = QR.next()
                        MM(pd[:], p_ke[:, t8, hc], vv[:, hc], [(p_ke, t8), vv], [pd])
                        STT("dve", Sst[:, hd, :], Sst[:, hd, :], p_ec[:, t8, h, 1:2], pd[:], ALU.mult, ALU.add,
                            [(Sst, hd), (p_ec, t8), pd], [(Sst, hd)])
                stAi(0)
                yield
                for t8 in range(NT8):
                    if t8 + 1 < NT8:
                        stAi(t8 + 1)
                        yield
                    stBi(t8)
                    yield

            def gs(g):
                hs = slice(g * 4, g * 4 + 4)

                def hb3(ap):
                    return ap.unsqueeze(2).to_broadcast([128, 4, 64])
                for blk in range(2):
                    if blk == 0:
                        wsl = W.take(w_in_d, 0, [(5 * D + g * 256, 256)], hold=True)
                    else:
                        wsl = W.take(w_in_d, 0, [(5 * D + D + g * 128, 128), (5 * D + D + 1024 + g * 128, 128)], hold=True)
                    for m in range(2):
                        chid = (g * 2 + m) if blk == 0 else (16 + g if m == 0 else 24 + g)
                        isC = (blk == 1 and m == 1)
                        for hf in ((1,) if isC else (0, 1)):
                            pm = MMR.next()
                            for kc in range(NKC):
                                MM(pm[:, 0:512], wsl[:, kc, m * 128:(m + 1) * 128], hTp[:, kc, hf * 512:(hf + 1) * 512],
                                   [hTp, wsl], [pm], start=(kc == 0), stop=(kc == NKC - 1))
                            CP("act", p_raw[:, 3 + hf * 512:3 + (hf + 1) * 512], pm[:, 0:512], [pm], [p_raw])
                        if m == 1:
                            W.release(wsl)
                        if isC:
                            CP("dve", halo[:, chid, :], p_raw[:, SEQ_HALF:SEQ_HALF + 3], [p_raw], [(halo, chid)])
                            yield
                            continue
                        CP("dve", p_raw[:, 0:3], halo[:, chid, :], [(halo, chid)], [p_raw])
                        CP("dve", halo[:, chid, :], p_raw[:, SEQ_HALF:SEQ_HALF + 3], [p_raw], [(halo, chid)])
                        cw = lambda i, chid=chid: col("cw", chid * 4 + i, 1)
                        TS("dve", p_acc[:], p_raw[:, 0:SEQ_HALF], cw(0), col("cb", chid, 1), ALU.mult, ALU.add, [p_raw, colv], [p_acc])
                        for i in (1, 2, 3):
                            STT("dve", p_acc[:], p_raw[:, i:i + SEQ_HALF], cw(i), p_acc[:], ALU.mult, ALU.add,
                                [p_raw, p_acc, colv], [p_acc])
                        ci = m if blk == 0 else 2
                        ACT(p_xa[:, ci, :], p_acc[:], AF.Silu, [p_acc], [(p_xa, ci)])
                        yield
                def stT(t8):
                    tc_ = slice(t8 * 128, (t8 + 1) * 128)
                    xst, Bt, xw = p_xst[t8 % 2], p_Bt[t8 % 2], p_xw[t8 % 2]
                    ph = HR.next()
                    pv = ph[:].bitcast(BF16)
                    for m in range(3):
                        TR(pv[:, m * 128:(m + 1) * 128], p_xa[:, m, tc_], identb[:], [(p_xa, m), identb], [ph])
                    CP("act", xst[:], pv[:, 0:256], [ph], [xst])
                    CP("dve", Bt[:], pv[:, 256:384], [ph], [Bt])
                    TTo("dve", xw[:].rearrange("p (h c) -> p h c", h=4), xst[:].rearrange("p (h c) -> p h c", h=4),
                        hb3(p_wend[:, t8, hs]), ALU.mult, [xst, (p_wend, t8)], [xw])

                def stP(t8):
                    Bt, xw = p_Bt[t8 % 2], p_xw[t8 % 2]
                    pdl = HR.next()
                    MM(pdl[:, 0:256], Bt[:], xw[:], [Bt, xw], [pdl])
                    h3 = hst[:, g * 256:(g + 1) * 256].rearrange("p (h c) -> p h c", h=4)
                    TTo("dve", h3, h3, hb3(p_Eend[:, t8, hs]), ALU.mult, [(hst, g), (p_Eend, t8)], [(hst, g)])
                    TTo("dve", hst[:, g * 256:(g + 1) * 256], hst[:, g * 256:(g + 1) * 256], pdl[:, 0:256], ALU.add,
                        [(hst, g), pdl], [(hst, g)])
                stT(0)
                yield
                for t8 in range(NT8):
                    if t8 + 1 < NT8:
                        stT(t8 + 1)
                        yield
                    stP(t8)
                    yield

            stepc = 0
            ga = gada()
            for u_ in range(8):
                gens = [gh(u_), gs(u_)]
                while gens:
                    for gen in list(gens):
                        try:
                            next(gen)
                        except StopIteration:
                            gens.remove(gen)
                        stepc += 1
                        if stepc % 6 == 0:
                            W.preconvert()
                        if stepc % 12 == 0:
                            next(ga, None)
            for _ in ga:
                pass
            S.barrier()

        DMA("sp", "rowb", rowb[:], rows3_d.partition_broadcast(128), (), [rowb])
        ACT(rowb[:, 32:64], rowb[:, 32:64], AF.Exp, [rowb], [rowb])
        TS("dve", rowb[:, 32:64], rowb[:, 32:64], -1.0, None, ALU.mult, None, [rowb], [rowb])
        ACT(hcol[:, 1:2], hcol[:, 1:2], AF.Exp, [hcol], [hcol])
        TS("dve", hcol[:, 1:2], hcol[:, 1:2], -1.0, None, ALU.mult, None, [hcol], [hcol])

        if stop == "p0":
            nst = npre = 0
        if npre:
            prefix_pass()
        else:
            for _ in gada():
                pass
        ada_finish()
        if npre:
            for hd in range(16):
                TS("dve", Sst[:, hd, :], Sst[:, hd, :], flag[:, 0:1], None, ALU.mult, None, [(Sst, hd), flag], [(Sst, hd)])
            for g in range(8):
                gsl = slice(g * 256, (g + 1) * 256)
                TS("dve", hst[:, gsl], hst[:, gsl], flag[:, 0:1], None, ALU.mult, None, [(hst, g), flag], [(hst, g)])
                CP("act", hstb[:, gsl], hst[:, gsl], [(hst, g)], [(hstb, g)])
            TS("dve", halo[:], halo[:], flag[:, 0:1], None, ALU.mult, None, [halo, flag], [halo])
        for s_ in range(nst):
            out_rows[0] = yp_d[s_ * ST:(s_ + 1) * ST, :]
            supertile(xp_d[s_ * ST:(s_ + 1) * ST, :], False, (s_ == 0) and do_samples, s_ == nst - 1, preconv=(s_ == 0))

        DMA("sp", "hgp", hgp_d.rearrange("h k v -> k h v"), Sst[:], [Sst], ())
        sso = xtile[0]
        for j in range(16):
            pq = QR.next()
            TR(pq[:], hst[:, j * 128:(j + 1) * 128], C("ident"), [(hst, j // 2), cst], [pq])
            CP("act", sso[:, j * 128:(j + 1) * 128], pq[:], [pq], [sso])
        DMA("sp", "xtile0", ssp_d.rearrange("(j two) p n -> (two p) j n", two=2),
            sso[:].rearrange("p (j n) -> p j n", j=16), [sso], ())
        cvo = xtile[1]
        for hf in range(2):
            for k4 in range(4):
                pm = MMR.next()
                for j in range(4):
                    chid = hf * 16 + k4 * 4 + j
                    TR(pm[0:3, j * 128:(j + 1) * 128], halo[:, chid, :], C("ident"), [(halo, chid), cst], [pm])
                CP("act", cvo[0:3, k4 * 512:(k4 + 1) * 512], pm[0:3, :], [pm], [cvo])
            DMA("sp", "xtile1", cvp_d[:, hf * D:(hf + 1) * D], cvo[0:3, :], [cvo], ())

        if W.record is not None:
            return W.record
        assert stop is not None or W.taken == len(W.reqs), (W.taken, len(W.reqs))
        S.emit(st)
    return nc


def build2(taps=None, **kw):
    reqs = build(bass.Bass("TRN2", target_bir_lowering=False), taps=taps, wreqs=None, **kw)
    return build(bass.Bass("TRN2", target_bir_lowering=False), taps=taps, wreqs=reqs, **kw)


def make_in_maps(inp, n_cores=8):
    f32 = lambda a: np.ascontiguousarray(np.asarray(a), dtype=np.float32)
    cst = host_consts()
    colv = np.zeros((128, NCOL), np.float32)

    def put(name, arr):
        o, w = COLS[name]
        assert arr.shape == (128, w), (name, arr.shape)
        colv[:, o:o + w] = arr
    put("g1", fm(inp["norm_pre_mix"][0], 16)); put("g2", fm(inp["norm_pre_mlp"][0], 16))
    put("gp1", fm(inp["norm_post_mix"][0], 16)); put("gp2", fm(inp["norm_post_mlp"][0], 16))
    put("bada", fm(inp["b_ada"][0], 96)); put("hn", fm(inp["hgrn_norm"][0], 16)); put("sn", fm(inp["ssd_norm"][0], 16))
    put("cb", fm(inp["conv_b"][0], 32))
    cw = np.asarray(inp["conv_w"][0], np.float32)
    put("cw", np.ascontiguousarray(cw.reshape(4, 32, 128).transpose(2, 1, 0).reshape(128, 128)))
    put("lb0", fm(inp["hgrn_lb_logits"][0], 16)); put("lb1", fm(inp["hgrn_lb_logits"][1], 16))
    put("dskE", fm(np.repeat(np.asarray(inp["d_skip"][0], np.float32), 64), 16))
    rowv = f32(np.concatenate([inp["hgrn_lb_logits"][0], inp["hgrn_lb_logits"][1]])[None, :])
    rows3 = f32(np.concatenate([inp["dt_bias"][0], inp["a_log"][0], inp["d_skip"][0]])[None, :])
    hcol = f32(np.stack([inp["dt_bias"][0], inp["a_log"][0]], axis=1))
    shared = dict(cst=cst, colv=colv, rowv=rowv, rows3=rows3, hcol=hcol,
                  w_ada=f32(inp["w_ada"][0]), w_in=f32(inp["w_in"][0]), w_out=f32(inp["w_out"][0]),
                  w_up=f32(inp["w_up"][0]), w_down=f32(inp["w_down"][0]))
    maps = []
    for c in range(n_cores):
        b, half = c // 2, c % 2
        sl = slice(c * NS, (c + 1) * NS)
        m = dict(shared)
        m["xp"] = f32(inp["x_prompt"][b, half * SEQ_HALF:(half + 1) * SEQ_HALF])
        m["xpre"] = f32(inp["x_prompt"][b, 0:SEQ_HALF])
        m["xs"] = f32(inp["x_sample"][sl, 0])
        m["c17"] = f32(np.concatenate([inp["c_prompt"][b:b + 1], inp["c_sample"][sl]], axis=0))
        m["flag"] = np.full((128, 1), float(half), np.float32)
        m["shg"] = f32(inp["state_hgrn"][0, sl])
        m["ssm"] = f32(inp["state_ssm"][0, sl])
        m["scv"] = f32(np.asarray(inp["state_conv"][0, sl]).reshape(NS * 3, 4096))
        maps.append(m)
    return maps


_NC_CACHE = {}


def kernel(**inputs):
    inp = {k: np.asarray(v) for k, v in inputs.items()}
    if "nc" not in _NC_CACHE:
        _NC_CACHE["nc"] = build2()
    nc = _NC_CACHE["nc"]
    maps = make_in_maps(inp)
    res = run_bass_kernel_spmd(nc, maps, core_ids=list(range(8)))
    r = res.results
    yp = np.stack([np.concatenate([r[2 * b]["yp"], r[2 * b + 1]["yp"]], axis=0) for b in range(4)])
    ys = np.concatenate([r[c]["ys"] for c in range(8)], axis=0)[:, None, :]
    hgp = np.stack([r[2 * b + 1]["hgp"] for b in range(4)])[None]
    ssp = np.stack([r[2 * b + 1]["ssp"] for b in range(4)])[None]
    cvp = np.stack([r[2 * b + 1]["cvp"] for b in range(4)])[None]
    hgs = np.concatenate([r[c]["hgs"] for c in range(8)], axis=0)[None]
    sss = np.concatenate([r[c]["sss"] for c in range(8)], axis=0)[None]
    cvs = np.concatenate([r[c]["cvs"] for c in range(8)], axis=0)[None]
    return tuple(np.ascontiguousarray(a, dtype=np.float32) for a in (yp, ys, hgp, ssp, cvp, hgs, sss, cvs))
```
